# Optimizing a Trainium2 kernel written in Bass

```python
import jax, jax.numpy as jnp
from jax import lax
import numpy as np

D_MODEL = 1024
BATCH = 2
SEQ = 8192
DEPTH = 1

CHUNK = 64
Q_BLOCK = 128
N_META = 16
SB_HEADS = 8
SB_HEAD_DIM = 64
GLA_HEADS = 4
GLA_HEAD_DK = 128
GLA_HEAD_DV = 256
GATE_RANK = 16
GATE_TAU = 16.0
D_FF = 2816
CONV_W = 3
N_BRANCHES = 2
EPS = 1e-6

SB_WIDTH = SB_HEADS * SB_HEAD_DIM
GLA_K_WIDTH = GLA_HEADS * GLA_HEAD_DK
GLA_V_WIDTH = GLA_HEADS * GLA_HEAD_DV
IN_WIDTHS = (SB_WIDTH, SB_WIDTH, SB_WIDTH, GLA_K_WIDTH, GLA_K_WIDTH, GLA_V_WIDTH, GLA_V_WIDTH, GATE_RANK, N_BRANCHES * D_MODEL)
IN_TOTAL = 3 * SB_WIDTH + 2 * GLA_K_WIDTH + 2 * GLA_V_WIDTH + GATE_RANK + N_BRANCHES * D_MODEL

kernel_name = "hybrid_stickbreak_gla_convffn_block"


def _rmsnorm(x, g):
    xf = x.astype(jnp.float32)
    y = xf * lax.rsqrt(jnp.mean(xf * xf, axis=-1, keepdims=True) + EPS)
    return (y * g.astype(jnp.float32)).astype(x.dtype)


def _split_cols(a, widths):
    outs, start = [], 0
    for w in widths:
        outs.append(a[..., start:start + w])
        start += w
    return outs


def _heads(a, n_heads):
    b, l, _ = a.shape
    return a.reshape(b, l, n_heads, -1).transpose(0, 2, 1, 3)


def _stick_breaking(q, k, v):
    b, h, lp, dh = q.shape
    nb = lp // Q_BLOCK
    scale = dh ** -0.5
    kpos = jnp.arange(lp)
    qb = q.reshape(b, h, nb, Q_BLOCK, dh).transpose(2, 0, 1, 3, 4)

    def block(args):
        qi, i = args
        z = jnp.einsum('bhqd,bhkd->bhqk', qi, k).astype(jnp.float32) * scale
        qpos = i * Q_BLOCK + jnp.arange(Q_BLOCK)
        strict = kpos[None, :] < qpos[:, None]
        u = jnp.where(strict, jax.nn.log_sigmoid(-z), 0.0)
        tail = lax.cumsum(u, axis=3, reverse=True) - u
        w = jnp.where(strict, jnp.exp(jax.nn.log_sigmoid(z) + tail), 0.0)
        return jnp.einsum('bhqk,bhkd->bhqd', w.astype(v.dtype), v)

    out = lax.map(block, (qb, jnp.arange(nb)))
    return out.transpose(1, 2, 0, 3, 4).reshape(b, h, lp, dh)


def _gla(q, k, v, g):
    b, h, lp, dk = q.shape
    dv = v.shape[-1]
    nc = lp // CHUNK

    def to_chunks(a):
        return a.reshape(b, h, nc, CHUNK, a.shape[-1]).transpose(2, 0, 1, 3, 4)

    qc, kc, vc, gc = to_chunks(q), to_chunks(k), to_chunks(v), to_chunks(g)
    causal = jnp.tril(jnp.ones((CHUNK, CHUNK), dtype=bool))

    def step(state, inp):
        qi, ki, vi, gi = inp
        cum = jnp.cumsum(gi, axis=2)
        diff = cum[:, :, :, None, :] - cum[:, :, None, :, :]
        decay = jnp.exp(jnp.where(causal[None, None, :, :, None], diff, -jnp.inf))
        att = jnp.einsum('bhtd,bhsd,bhtsd->bhts', qi, ki, decay)
        o = jnp.einsum('bhts,bhsv->bhtv', att, vi) + jnp.einsum('bhtd,bhdv->bhtv', qi * jnp.exp(cum), state)
        last = cum[:, :, -1:, :]
        state = jnp.exp(last[:, :, 0, :])[..., None] * state + jnp.einsum('bhsd,bhsv->bhdv', ki * jnp.exp(last - cum), vi)
        return state, o

    state0 = jnp.zeros((b, h, dk, dv), jnp.float32)
    _, oc = lax.scan(step, state0, (qc, kc, vc, gc))
    return oc.transpose(1, 2, 0, 3, 4).reshape(b, h, lp, dv)


def _causal_dwconv(a, w, bias):
    a_p = jnp.pad(a, ((0, 0), (CONV_W - 1, 0), (0, 0)))
    y = lax.conv_general_dilated(a_p, w[:, None, :].astype(a.dtype), window_strides=(1,), padding='VALID',
                                 dimension_numbers=('NWC', 'WIO', 'NWC'), feature_group_count=a.shape[-1])
    return y + bias.astype(a.dtype)


def setup_inputs(seed: int = 0) -> dict:
    key = jax.random.key(seed)
    ks = jax.random.split(key, 20)
    f32 = jnp.float32

    def nrm(k, shape, scale):
        return jax.random.normal(k, shape, f32) * scale

    def gain(k, n):
        return 1.0 + 0.02 * jax.random.normal(k, (DEPTH, n), f32)

    return {
        "x": nrm(ks[0], (BATCH, SEQ, D_MODEL), 1.0),
        "meta_tokens": nrm(ks[1], (N_META, D_MODEL), 1.0),
        "norm_mix_pre": gain(ks[2], D_MODEL),
        "w_in": nrm(ks[3], (DEPTH, D_MODEL, IN_TOTAL), D_MODEL ** -0.5),
        "w_gk_up": nrm(ks[4], (DEPTH, GATE_RANK, GLA_K_WIDTH), GATE_RANK ** -0.5),
        "b_gk": nrm(ks[5], (DEPTH, GLA_K_WIDTH), 0.1),
        "gla_head_norm": gain(ks[6], GLA_HEAD_DV),
        "w_sb_out": nrm(ks[7], (DEPTH, SB_WIDTH, D_MODEL), SB_WIDTH ** -0.5),
        "w_gla_out": nrm(ks[8], (DEPTH, GLA_V_WIDTH, D_MODEL), GLA_V_WIDTH ** -0.5),
        "w_o": nrm(ks[9], (DEPTH, D_MODEL, D_MODEL), D_MODEL ** -0.5),
        "norm_mix_post": gain(ks[10], D_MODEL),
        "norm_ffn_pre": gain(ks[11], D_MODEL),
        "w_ffn_up": nrm(ks[12], (DEPTH, D_MODEL, D_FF), D_MODEL ** -0.5),
        "w_ffn_gate": nrm(ks[13], (DEPTH, D_MODEL, D_FF), D_MODEL ** -0.5),
        "conv_w": nrm(ks[14], (DEPTH, CONV_W, D_FF), CONV_W ** -0.5),
        "conv_b": nrm(ks[15], (DEPTH, D_FF), 0.02),
        "w_ffn_down": nrm(ks[16], (DEPTH, D_FF, D_MODEL), D_FF ** -0.5),
        "norm_ffn_post": gain(ks[17], D_MODEL),
    }


def reference(x, meta_tokens, norm_mix_pre, w_in, w_gk_up, b_gk, gla_head_norm, w_sb_out, w_gla_out, w_o,
              norm_mix_post, norm_ffn_pre, w_ffn_up, w_ffn_gate, conv_w, conv_b, w_ffn_down, norm_ffn_post):
    b, s, _ = x.shape
    L = s + N_META
    Lp = -(-L // Q_BLOCK) * Q_BLOCK
    meta = jnp.broadcast_to(meta_tokens[None].astype(x.dtype), (b, N_META, D_MODEL))
    h = jnp.concatenate([meta, x], axis=1)

    for l in range(DEPTH):
        hn = _rmsnorm(h, norm_mix_pre[l])
        hn_p = jnp.pad(hn, ((0, 0), (0, Lp - L), (0, 0)))
        proj = hn_p @ w_in[l]
        sb_q, sb_k, sb_v, g_q, g_k, g_v, g_r, g_lr, merge = _split_cols(proj, IN_WIDTHS)

        sb = _stick_breaking(_heads(sb_q, SB_HEADS), _heads(sb_k, SB_HEADS), _heads(sb_v, SB_HEADS))
        sb = sb.transpose(0, 2, 1, 3).reshape(b, Lp, SB_WIDTH)[:, :L]

        glog = jax.nn.log_sigmoid((g_lr @ w_gk_up[l] + b_gk[l]).astype(jnp.float32)) / GATE_TAU
        gq = _heads(g_q, GLA_HEADS).astype(jnp.float32) * (GLA_HEAD_DK ** -0.5)
        gk = _heads(g_k, GLA_HEADS).astype(jnp.float32)
        gv = _heads(g_v, GLA_HEADS).astype(jnp.float32)
        o = _gla(gq, gk, gv, _heads(glog, GLA_HEADS))
        o = _rmsnorm(o, gla_head_norm[l])
        o = o.transpose(0, 2, 1, 3).reshape(b, Lp, GLA_V_WIDTH)[:, :L].astype(h.dtype)
        o = o * jax.nn.silu(g_r[:, :L])

        gates = jax.nn.sigmoid(merge[:, :L])
        gate_sb, gate_gla = gates[..., :D_MODEL], gates[..., D_MODEL:]
        mix = (gate_sb * (sb @ w_sb_out[l]) + gate_gla * (o @ w_gla_out[l])) @ w_o[l]
        h = h + _rmsnorm(mix, norm_mix_post[l])

        hn = _rmsnorm(h, norm_ffn_pre[l])
        up = hn @ w_ffn_up[l]
        act = jax.nn.gelu(_causal_dwconv(up, conv_w[l], conv_b[l]), approximate=True)
        ffn = (act * (hn @ w_ffn_gate[l])) @ w_ffn_down[l]
        h = h + _rmsnorm(ffn, norm_ffn_post[l])

    return h[:, N_META:]
```

```python
import contextlib
import numpy as np
import concourse.bass as bass
import concourse.mybir as mybir
from concourse.bass_utils import run_bass_kernel_spmd

F32 = mybir.dt.float32
BF16 = mybir.dt.bfloat16
AF = mybir.ActivationFunctionType
ALU = mybir.AluOpType

ENGS = ("pe", "act", "dve", "pool", "sp")
NS = 8
EP = 4096
SAME_ENG_DIST = 1 << 30


class Sched:
    def __init__(self, nc):
        self.nc = nc
        self.ops = []
        self.last_w = {}
        self.readers = {}
        self.emitted = 0
        self.eng_rank = {e: 0 for e in ENGS}
        self.dma_cnt = {e: 0 for e in ENGS}
        self.dma_ops = {e: [] for e in ENGS}
        self.seen = {e: {} for e in ENGS}
        self.sems = {e: [] for e in ENGS}
        self.dsems = {}
        self.stack = contextlib.ExitStack()
        self.phase_dmas = []
        self.last_op = {e: None for e in ENGS}

    def add(self, eng, fn, reads=(), writes=(), dma=False):
        oid = len(self.ops)
        deps = set()
        for b in reads:
            w = self.last_w.get(b)
            if w is not None:
                deps.add(w)
        for b in writes:
            w = self.last_w.get(b)
            if w is not None:
                deps.add(w)
            rd = self.readers.get(b)
            if rd:
                deps.update(rd["e"].values())
                deps.update(rd["d"])
        for b in reads:
            rd = self.readers.setdefault(b, {"e": {}, "d": []})
            if dma:
                rd["d"].append(oid)
            else:
                rd["e"][eng] = oid
        for b in writes:
            self.last_w[b] = oid
            self.readers[b] = {"e": {}, "d": []}
        op = dict(eng=eng, fn=fn, dma=dma, deps=deps, id=oid)
        if dma:
            n = self.dma_cnt[eng]
            self.dma_cnt[eng] += 1
            op["tok"] = ("d", eng, n % NS, 16 * (n // NS + 1))
            if n >= NS:
                deps.add(self.dma_ops[eng][n - NS])
            self.dma_ops[eng].append(oid)
            op["erank"] = self.eng_rank[eng] + 1
            self.phase_dmas.append(oid)
        elif fn is not None:
            self.eng_rank[eng] += 1
            op["tok"] = ("e", eng, self.eng_rank[eng])
            op["erank"] = self.eng_rank[eng]
        else:
            op["tok"] = None
            op["erank"] = self.eng_rank[eng] + 1
        deps.discard(oid)
        self.ops.append(op)
        if fn is not None:
            self.last_op[eng] = oid
        return oid

    def dma(self, fn, reads=(), writes=()):
        return self.add("sp", fn, reads, writes, dma=True)

    def barrier(self):
        deps = set(v for v in self.last_op.values() if v is not None)
        deps.update(self.phase_dmas)
        self.phase_dmas = []
        for e in ENGS:
            oid = self.add(e, None)
            self.ops[oid]["deps"] = set(deps)

    def _sem(self, eng, idx):
        lst = self.sems[eng]
        while len(lst) <= idx:
            lst.append(self.stack.enter_context(self.nc.semaphore(f"s_{eng}_{len(lst)}")))
        return lst[idx]

    def _dsem(self, q, slot):
        k = (q, slot)
        if k not in self.dsems:
            self.dsems[k] = self.stack.enter_context(self.nc.semaphore(f"d_{q}_{slot}"))
        return self.dsems[k]

    def emit(self):
        ops = self.ops[self.emitted:]
        self.emitted = len(self.ops)
        streams = {e: [o for o in ops if o["eng"] == e] for e in ENGS}
        nc = self.nc
        for e in ENGS:
            if self.eng_rank[e] > 0:
                self._sem(e, (self.eng_rank[e] - 1) // EP)
            if self.dma_cnt[e] > 0:
                for sl in range(NS):
                    self._dsem(e, sl)
        with nc.Block() as block:
            def body(e, h):
                seen = self.seen[e]
                for op in streams[e]:
                    waits = {}
                    for d in op["deps"]:
                        tok = self.ops[d]["tok"]
                        if tok is None:
                            continue
                        if tok[0] == "e":
                            F, r = tok[1], tok[2]
                            if F == e:
                                if e == "pe" or op["erank"] - r > SAME_ENG_DIST:
                                    continue
                            if seen.get(F, 0) >= r:
                                continue
                            waits[F] = max(waits.get(F, 0), r)
                        else:
                            k = (tok[1], tok[2])
                            if seen.get(k, 0) >= tok[3]:
                                continue
                            waits[k] = max(waits.get(k, 0), tok[3])
                    for k, v in waits.items():
                        if isinstance(k, tuple):
                            h.wait_ge(self._dsem(*k), v)
                        else:
                            h.wait_ge(self._sem(k, (v - 1) // EP), (v - 1) % EP + 1)
                        seen[k] = v
                    if op["fn"] is None:
                        continue
                    ins = op["fn"](h)
                    tok = op["tok"]
                    if tok[0] == "e":
                        r = tok[2]
                        ins.then_inc(self._sem(e, (r - 1) // EP), 1)
                    else:
                        ins.then_inc(self._dsem(tok[1], tok[2]), 16)

            if streams["pe"]:
                @block.tensor
                def _(h):
                    body("pe", h)
            if streams["act"]:
                @block.scalar
                def _(h):
                    body("act", h)
            if streams["dve"]:
                @block.vector
                def _(h):
                    body("dve", h)
            if streams["pool"]:
                @block.gpsimd
                def _(h):
                    body("pool", h)
            if streams["sp"]:
                @block.sync
                def _(h):
                    body("sp", h)


D = 1024
DFF = 2816
NCF = 22
WR = 640
EPS = 1e-6
C_SBQ, C_SBK, C_SBV, C_GQ, C_GK, C_GV, C_GR, C_LR, C_MG = 0, 512, 1024, 1536, 2048, 2560, 3584, 4608, 4624
GK_ = 1.5957691216057308


def build(nslot):
    nc = bass.Bass("TRN2", target_bir_lowering=False)
    NBLK = 16 * nslot + 1
    NR = NBLK * 128
    NW = nslot
    NQ = NW * WR

    def din(name, shape, dt=F32):
        return nc.dram_tensor(name, shape, dt, kind="ExternalInput").ap()

    def dscr(name, shape, dt):
        return nc.dram_tensor(name, shape, dt).ap()

    xp = din("xp", [NR, D])
    xo = din("xo", [NQ, D])
    qpos_d = din("qpos", [128, NQ])
    kpos_d = din("kpos", [128, NBLK])
    oh_d = din("oh", [128, 4])
    cst_d = din("cst", [128, 5, 128])
    w_in = din("w_in", [D, 6672])
    w_lr = din("w_lr", [D, 32])
    wgk_d = din("wgk", [32, 512])
    wsb_d = din("w_sb_out", [512, D])
    wgla_d = din("w_gla_out", [D, D])
    wo_d = din("w_o", [D, D])
    wup_d = din("w_up", [D, DFF])
    wgt_d = din("w_gate", [D, DFF])
    wdn_d = din("w_down", [DFF, D])
    gb_d = din("gains", [128, 4, D])
    gh_d = din("ghead", [128, 256])
    cw_d = din("cw", [128, NCF, 4])
    out_d = nc.dram_tensor("out", [NW * 512, D], F32, kind="ExternalOutput").ap()

    kT_d = dscr("kT_d", [4, 128, NR], BF16)
    v_d = dscr("v_d", [8, 128, NBLK * 64], BF16)
    qT_d = dscr("qT_d", [4, 128, NQ], BF16)
    sbT_d = dscr("sbT_d", [8, 64, NQ], BF16)
    hnT_d = dscr("hnT_d", [NW, 128, 8 * WR], BF16)
    ogT_d = dscr("ogT_d", [NW, 128, 8 * WR], BF16)
    hn2T_d = dscr("hn2T_d", [NW, 128, 8 * WR], BF16)
    prodT_d = dscr("prodT_d", [NW, 128, NCF * WR], BF16)
    h1_d = dscr("h1_d", [NQ, D], F32)

    S = Sched(nc)
    uid = [0]

    def T(st, shape, dt, name=None):
        uid[0] += 1
        return st.enter_context(nc.sbuf_tensor("sb_" + (name or f"t{uid[0]}"), shape, dt))

    with S.stack:
        gst = S.stack
        PS = [gst.enter_context(nc.psum_tensor(f"ps{i}", [128, 1024], F32)) for i in range(4)]
        bk = [0]

        def bank():
            i = bk[0] % 8
            bk[0] += 1
            return PS[i // 2][:, (i % 2) * 512:(i % 2) * 512 + 512], f"ps{i}"

        def dbank():
            if bk[0] % 2:
                bk[0] += 1
            i = bk[0] % 8
            bk[0] += 2
            return PS[i // 2], [f"ps{i}", f"ps{i + 1}"]

        alt = [0]

        def evac(out_ap, in_ap, reads, writes, scale=None):
            alt[0] += 1
            if scale is not None:
                S.add("act", lambda h: h.activation(out=out_ap, in_=in_ap, func=AF.Copy, scale=scale), reads, writes)
            elif alt[0] % 2:
                S.add("act", lambda h: h.copy(out=out_ap, in_=in_ap), reads, writes)
            else:
                S.add("dve", lambda h: h.tensor_copy(out=out_ap, in_=in_ap), reads, writes)

        cst = T(gst, [128, 5, 128], F32, "cst")
        cstb = T(gst, [128, 2, 128], BF16, "cstb")
        m1rep = T(gst, [128, 4, 128], F32, "m1rep")
        ones2 = T(gst, [128, 2], F32, "ones2")
        gh = T(gst, [128, 256], F32, "gh")
        oh = T(gst, [128, 4], F32, "oh")
        kpos = T(gst, [128, NBLK], F32, "kpos")
        stage = [T(gst, [128, 512], F32, f"stage{i}") for i in range(6)]
        stg = [0]
        abst = gst.enter_context(contextlib.ExitStack())
        Sown = T(abst, [128, NW, D], F32, "Sown")
        S.dma(lambda h: h.dma_start(out=cst[:, :, :], in_=cst_d), writes=["cst"])
        S.dma(lambda h: h.dma_start(out=gh[:, :], in_=gh_d), writes=["gh"])
        S.dma(lambda h: h.dma_start(out=oh[:, :], in_=oh_d), writes=["oh"])
        S.dma(lambda h: h.dma_start(out=kpos[:, :], in_=kpos_d), writes=["kpos"])
        S.add("pool", lambda h: h.tensor_copy(out=cstb[:, :, :], in_=cst[:, 1:3, :]), ["cst"], ["cstb"])
        for i in range(4):
            S.add("pool", lambda h, i=i: h.tensor_copy(out=m1rep[:, i, :], in_=cst[:, 3, :]), ["cst"], ["m1rep"])
        S.add("pool", lambda h: h.memset(ones2[:, :], 1.0), [], ["ones2"])
        S.add("pool", lambda h: h.memset(Sown[:, :, :], 0.0), [], ["Sown"])
        ident = cst[:, 0, :]
        negtri = cstb[:, 0, :]
        negones = cstb[:, 1, :]
        M1 = cst[:, 3, :]
        M2 = cst[:, 4, :]

        def load_w(dst, dkey, col0, src, r0, nrows_chunks, c0, ncols, kdim=128):
            for kc in range(nrows_chunks):
                for cc in range(0, ncols, 512):
                    n = min(512, ncols - cc)
                    si = stg[0] % 6
                    stg[0] += 1
                    sk = f"stage{si}"
                    S.dma(lambda h, si=si, kc=kc, cc=cc, n=n: h.dma_start(
                        out=stage[si][0:kdim, 0:n],
                        in_=src[r0 + kc * kdim:r0 + (kc + 1) * kdim, c0 + cc:c0 + cc + n]), writes=[sk])
                    ceng = ("act", "dve", "pool", "act", "dve")[stg[0] % 5]
                    if ceng == "act":
                        S.add("act", lambda h, si=si, kc=kc, cc=cc, n=n: h.copy(
                            out=dst[0:kdim, kc, col0 + cc:col0 + cc + n], in_=stage[si][0:kdim, 0:n]), [sk], [dkey])
                    else:
                        S.add(ceng, lambda h, si=si, kc=kc, cc=cc, n=n: h.tensor_copy(
                            out=dst[0:kdim, kc, col0 + cc:col0 + cc + n], in_=stage[si][0:kdim, 0:n]), [sk], [dkey])

        def rstd_from_ss(ss_ap, rs_ap, n, key_in, key_out):
            S.add("act", lambda h: h.activation(out=rs_ap, in_=ss_ap, func=AF.Ln, scale=1.0 / n, bias=EPS),
                  [key_in], [key_out])
            S.add("act", lambda h: h.activation(out=rs_ap, in_=rs_ap, func=AF.Exp, scale=-0.5),
                  [key_out], [key_out])

        def mixer_phase(own):
            with contextlib.ExitStack() as st:
                if own:
                    WCOLS = 3616
                    O_SBQ, O_GQ, O_GK, O_GV, O_GR, O_LR = 0, 512, 1024, 1536, 2560, 3584
                else:
                    WCOLS = 2592
                    O_SBK, O_SBV, O_GK, O_GV, O_LR = 0, 512, 1024, 1536, 2560
                wA = T(st, [128, 8, WCOLS], BF16)
                wk = "wA"
                if own:
                    load_w(wA, wk, O_SBQ, w_in, 0, 8, C_SBQ, 512)
                    load_w(wA, wk, O_GQ, w_in, 0, 8, C_GQ, 1024)
                    load_w(wA, wk, O_GV, w_in, 0, 8, C_GV, 2048)
                    load_w(wA, wk, O_LR, w_lr, 0, 8, 0, 32)
                else:
                    load_w(wA, wk, O_SBK, w_in, 0, 8, C_SBK, 1024)
                    load_w(wA, wk, O_GK, w_in, 0, 8, C_GK, 1536)
                    load_w(wA, wk, O_LR, w_lr, 0, 8, 0, 32)
                wgk = T(st, [32, 1, 512], BF16)
                load_w(wgk, "wgk", 0, wgk_d, 0, 1, 0, 512, kdim=32)
                xt = [T(st, [128, D], F32) for _ in range(2)]
                g0 = T(st, [128, D], F32)
                S.dma(lambda h: h.dma_start(out=g0[:, :], in_=gb_d[:, 0, :]), writes=["gb"])
                junk = T(st, [128, D], F32)
                hn = [T(st, [128, D], F32) for _ in range(2)]
                hnT = [T(st, [128, 8, 128], BF16) for _ in range(2)]
                stat = [T(st, [128, 16], F32) for _ in range(2)]
                lrT = T(st, [32, 128], BF16)
                S.add("pool", lambda h: h.memset(lrT[:, :], 1.0), [], ["lrT"])
                e_t = T(st, [128, 512], F32)
                sp_t = T(st, [128, 512], F32)
                erev = T(st, [128, 512], F32)
                khat = T(st, [128, 512], BF16)
                gv = [T(st, [128, D], BF16) for _ in range(2)]
                dec = T(st, [128, 8], F32)
                Sst = T(st, [128, D], F32)
                if own:
                    hnTw = T(st, [128, 8, WR], BF16)
                    ogTw = T(st, [128, 8, WR], BF16)
                    qTw = T(st, [128, 4, 128], BF16)
                    eq = T(st, [128, 512], F32)
                    ek = T(st, [128, 512], F32)
                    qtl = T(st, [128, 512], BF16)
                    ktl = T(st, [128, 512], BF16)
                    attm = T(st, [128, 512], BF16)
                    Sbf = T(st, [128, D], BF16)
                    eg = T(st, [128, D], F32)
                    grs = T(st, [128, D], F32)
                    og2 = [T(st, [128, D], F32) for _ in range(2)]
                else:
                    kTb = T(st, [128, 4, 128], BF16)
                    vtb = T(st, [128, 512], BF16)
                    S.add("pool", lambda h: h.memset(Sst[:, :], 0.0), [], ["Sst"])

                nblocks = NW * 5 if own else NBLK
                src = xo if own else xp
                deferred = []
                for bi in range(nblocks):
                    w_i, wb_i = divmod(bi, 5)
                    xi = bi % 2
                    hi = bi % 2
                    if own:
                        og = og2[bi % 2]
                        ogk = f"og{bi % 2}"
                    xk, hk, hTk, sk_, gvk = f"xt{xi}", f"hn{hi}", f"hnT{hi}", f"stat{hi}", f"gv{hi}"
                    xt_, hn_, hnT_, st_, gv_ = xt[xi], hn[hi], hnT[hi], stat[hi], gv[hi]
                    S.dma(lambda h, xt_=xt_, bi=bi: h.dma_start(out=xt_[:, :], in_=src[bi * 128:(bi + 1) * 128, :]),
                          writes=[xk])
                    S.add("act", lambda h, xt_=xt_, st_=st_: h.activation(out=junk[:, :], in_=xt_[:, :], func=AF.Square,
                                                                          accum_out=st_[:, 0:1]), [xk], ["junk", sk_ + "a"])
                    rstd_from_ss(st_[:, 0:1], st_[:, 1:2], D, sk_ + "a", sk_ + "b")
                    S.add("dve", lambda h, xt_=xt_, st_=st_, hn_=hn_: h.scalar_tensor_tensor(
                        out=hn_[:, :], in0=xt_[:, :], scalar=st_[:, 1:2], in1=g0[:, :], op0=ALU.mult, op1=ALU.mult),
                        [xk, sk_ + "b", "gb"], [hk])
                    for half in range(2):
                        pb, pk = bank()
                        for q in range(4):
                            kc = half * 4 + q
                            S.add("pe", lambda h, pb=pb, q=q, kc=kc, hn_=hn_: h.transpose(
                                out=pb[:, q * 128:(q + 1) * 128], in_=hn_[:, kc * 128:(kc + 1) * 128], identity=ident),
                                [hk, "cst"], [pk])
                        evac(hnT_[:, half * 4:half * 4 + 4, :], pb.rearrange("p (a b) -> p a b", a=4), [pk], [hTk])
                    if own:
                        S.add("pool", lambda h, hnT_=hnT_, wb_i=wb_i: h.tensor_copy(
                            out=hnTw[:, :, wb_i * 128:(wb_i + 1) * 128], in_=hnT_[:, :, :]), [hTk], ["hnTw"])
                    while deferred:
                        deferred.pop(0)()
                    if own and wb_i == 0:
                        S.add("dve", lambda h, w_i=w_i: h.tensor_copy(out=Sst[:, :], in_=Sown[:, w_i, :]),
                              ["Sown"], ["Sst"])
                    if (not own) and bi % 4 == 0 and bi < 16 * nslot:
                        Tt = bi // 4
                        m_, j_ = divmod(Tt, 4)
                        S.add("dve", lambda h, m_=m_, j_=j_: h.scalar_tensor_tensor(
                            out=Sown[:, m_, :], in0=Sst[:, :], scalar=oh[:, j_:j_ + 1], in1=Sown[:, m_, :],
                            op0=ALU.mult, op1=ALU.add), ["Sst", "oh", "Sown"], ["Sown"])

                    def proj_tok(pb, pk, col0, n, hnT_=hnT_, hTk=hTk):
                        for kc in range(8):
                            S.add("pe", lambda h, kc=kc: h.matmul(pb[:, 0:n], lhsT=hnT_[:, kc, :],
                                                                   rhs=wA[:, kc, col0:col0 + n],
                                                                   start=(kc == 0), stop=(kc == 7)), [hTk, wk], [pk])

                    def proj_feat(out_ap, pk, col0, m, hnT_=hnT_, hTk=hTk):
                        for kc in range(8):
                            S.add("pe", lambda h, kc=kc: h.matmul(out_ap, lhsT=wA[:, kc, col0:col0 + m],
                                                                   rhs=hnT_[:, kc, :],
                                                                   start=(kc == 0), stop=(kc == 7)), [hTk, wk], [pk])

                    pb, pk = bank()
                    proj_feat(pb[0:32, 0:128], pk, O_LR, 32)
                    evac(lrT[0:16, :], pb[0:16, 0:128], [pk, "lrT"], ["lrT"])
                    hd_dst = qTw if own else kTb
                    hd_key = "qTw" if own else "kTb"
                    hd_col = O_SBQ if own else O_SBK
                    pb, pk = bank()
                    for pp in range(4):
                        proj_feat(pb[:, pp * 128:(pp + 1) * 128], pk, hd_col + pp * 128, 128)
                    evac(hd_dst[:, :, :], pb.rearrange("p (a b) -> p a b", a=4),
                         [pk], [hd_key], scale=(0.125 if own else None))
                    if own:
                        S.dma(lambda h, bi=bi: h.dma_start(
                            out=qT_d[:, :, bi * 128:(bi + 1) * 128].rearrange("h d n -> d h n"), in_=qTw[:, :, :]),
                            reads=["qTw"], writes=["qT_d"])
                    else:
                        S.dma(lambda h, bi=bi: h.dma_start(
                            out=kT_d[:, :, bi * 128:(bi + 1) * 128].rearrange("h d n -> d h n"), in_=kTb[:, :, :]),
                            reads=["kTb"], writes=["kT_d"])
                        pb, pk = bank()
                        proj_tok(pb, pk, O_SBV, 512)
                        evac(vtb[:, :], pb, [pk], ["vtb"])
                        S.dma(lambda h, bi=bi: h.dma_start(
                            out=v_d[:, :, bi * 64:(bi + 1) * 64].rearrange("h p d -> p h d"),
                            in_=vtb[:, :].rearrange("p (h d) -> p h d", h=8)), reads=["vtb"], writes=["v_d"])
                    if (not own) and bi == NBLK - 1:
                        continue
                    pg, pgk = bank()
                    S.add("pe", lambda h, pg=pg: h.matmul(pg, lhsT=lrT[:, :], rhs=wgk[:, 0, :], start=True, stop=True),
                          ["lrT", "wgk"], [pgk])
                    S.add("act", lambda h, pg=pg: h.activation(out=e_t[:, :], in_=pg, func=AF.Exp, scale=-1.0),
                          [pgk], ["e_t"])
                    S.add("act", lambda h: h.activation(out=sp_t[:, :], in_=e_t[:, :], func=AF.Ln, bias=1.0),
                          ["e_t"], ["sp_t"])
                    pkk, pkkk = bank()
                    proj_tok(pkk, pkkk, O_GK, 512)
                    pv, pvk = dbank()
                    for half in range(2):
                        for kc in range(8):
                            S.add("pe", lambda h, kc=kc, half=half, pv=pv, hnT_=hnT_: h.matmul(
                                pv[:, half * 512:(half + 1) * 512], lhsT=hnT_[:, kc, :],
                                rhs=wA[:, kc, O_GV + half * 512:O_GV + (half + 1) * 512],
                                start=(kc == 0), stop=(kc == 7)), [hTk, wk], [pvk[half]])
                    evac(gv_[:, :], pv[:, :], pvk, [gvk])

                    pr, prk = bank()
                    S.add("pe", lambda h, pr=pr: h.matmul(pr, lhsT=M2, rhs=sp_t[:, :], start=True, stop=True),
                          ["sp_t", "cst"], [prk])
                    S.add("act", lambda h, pr=pr: h.activation(out=erev[:, :], in_=pr, func=AF.Exp, scale=-1.0 / 16),
                          [prk], ["erev"])
                    pd, pdk = bank()
                    for hh in range(4):
                        S.add("pe", lambda h, pd=pd, hh=hh: h.matmul(pd[:, 2 * hh:2 * hh + 2],
                                                                      lhsT=sp_t[:, hh * 128:(hh + 1) * 128],
                                                                      rhs=ones2[:, :], start=True, stop=True),
                              ["sp_t", "ones2"], [pdk])
                    S.add("act", lambda h, pd=pd: h.activation(out=dec[:, :], in_=pd[:, 0:8], func=AF.Exp,
                                                               scale=-1.0 / 16), [pdk], ["dec"])
                    S.add("dve", lambda h, pkk=pkk: h.tensor_tensor(out=khat[:, :], in0=pkk, in1=erev[:, :], op=ALU.mult),
                          [pkkk, "erev"], ["khat"])
                    if own:
                        pc, pck = bank()
                        for hh in range(4):
                            S.add("pe", lambda h, pc=pc, hh=hh: h.matmul(pc[:, hh * 128:(hh + 1) * 128],
                                                                          lhsT=sp_t[:, hh * 128:(hh + 1) * 128],
                                                                          rhs=M1, start=True, stop=True),
                                  ["sp_t", "cst"], [pck])
                        S.add("act", lambda h, pc=pc: h.activation(out=eq[:, :], in_=pc, func=AF.Exp, scale=-1.0 / 16),
                              [pck], ["eq"])
                        S.add("act", lambda h, pc=pc: h.activation(out=ek[:, :], in_=pc, func=AF.Exp, scale=1.0 / 16),
                              [pck], ["ek"])
                        pq, pqk = bank()
                        for hh in range(4):
                            proj_feat(pq[:, hh * 128:(hh + 1) * 128], pqk, O_GQ + hh * 128, 128)
                        S.add("dve", lambda h, pq=pq: h.scalar_tensor_tensor(
                            out=qtl[:, :], in0=pq, scalar=128.0 ** -0.5, in1=eq[:, :], op0=ALU.mult, op1=ALU.mult),
                            [pqk, "eq"], ["qtl"])
                        pk2, pk2k = bank()
                        for hh in range(4):
                            proj_feat(pk2[:, hh * 128:(hh + 1) * 128], pk2k, O_GK + hh * 128, 128)
                        S.add("dve", lambda h, pk2=pk2: h.tensor_tensor(out=ktl[:, :], in0=pk2, in1=ek[:, :], op=ALU.mult),
                              [pk2k, "ek"], ["ktl"])
                        pa, pak = bank()
                        for hh in range(4):
                            S.add("pe", lambda h, pa=pa, hh=hh: h.matmul(pa[:, hh * 128:(hh + 1) * 128],
                                                                          lhsT=ktl[:, hh * 128:(hh + 1) * 128],
                                                                          rhs=qtl[:, hh * 128:(hh + 1) * 128],
                                                                          start=True, stop=True), ["ktl", "qtl"], [pak])
                        S.add("dve", lambda h, pa=pa: h.tensor_tensor(
                            out=attm[:, :], in0=pa, in1=m1rep[:, :, :].rearrange("p a b -> p (a b)"), op=ALU.mult),
                            [pak, "m1rep"], ["attm"])
                        S.add("pool", lambda h: h.tensor_copy(out=Sbf[:, :], in_=Sst[:, :]), ["Sst"], ["Sbf"])
                        po, pok = dbank()
                        for hh in range(4):
                            oc = po[:, hh * 256:(hh + 1) * 256]
                            S.add("pe", lambda h, oc=oc, hh=hh, gv_=gv_: h.matmul(
                                oc, lhsT=attm[:, hh * 128:(hh + 1) * 128], rhs=gv_[:, hh * 256:(hh + 1) * 256],
                                start=True, stop=False), ["attm", gvk], [pok[hh // 2]])
                            S.add("pe", lambda h, oc=oc, hh=hh: h.matmul(
                                oc, lhsT=qtl[:, hh * 128:(hh + 1) * 128], rhs=Sbf[:, hh * 256:(hh + 1) * 256],
                                start=False, stop=True), ["qtl", "Sbf"], [pok[hh // 2]])
                        for hh in range(4):
                            S.add("act", lambda h, hh=hh, po=po, st_=st_: h.activation(
                                out=junk[:, 0:256], in_=po[:, hh * 256:(hh + 1) * 256], func=AF.Square,
                                accum_out=st_[:, 4 + hh:5 + hh]), [pok[hh // 2]], ["junk", sk_ + "c"])
                        rstd_from_ss(st_[:, 4:8], st_[:, 8:12], 256, sk_ + "c", sk_ + "d")
                        for hh in range(4):
                            S.add("dve", lambda h, hh=hh, po=po, st_=st_, og=og: h.scalar_tensor_tensor(
                                out=og[:, hh * 256:(hh + 1) * 256], in0=po[:, hh * 256:(hh + 1) * 256],
                                scalar=st_[:, 8 + hh:9 + hh], in1=gh[:, :], op0=ALU.mult, op1=ALU.mult),
                                [pok[hh // 2], sk_ + "d", "gh", ogk], [ogk])
                        pgr, pgrk = dbank()
                        for half in range(2):
                            for kc in range(8):
                                S.add("pe", lambda h, kc=kc, half=half, pgr=pgr, hnT_=hnT_: h.matmul(
                                    pgr[:, half * 512:(half + 1) * 512], lhsT=hnT_[:, kc, :],
                                    rhs=wA[:, kc, O_GR + half * 512:O_GR + (half + 1) * 512],
                                    start=(kc == 0), stop=(kc == 7)), [hTk, wk], [pgrk[half]])
                        S.add("act", lambda h, pgr=pgr: h.copy(out=grs[:, :], in_=pgr[:, :]), pgrk, ["grs"])
                        S.add("act", lambda h: h.activation(out=eg[:, :], in_=grs[:, :], func=AF.Exp, scale=-1.0),
                              ["grs"], ["eg"])
                        S.add("dve", lambda h: h.tensor_scalar(out=eg[:, :], in0=eg[:, :], scalar1=1.0, scalar2=None,
                                                               op0=ALU.add), ["eg"], ["eg"])
                        S.add("dve", lambda h: h.reciprocal(out=eg[:, :], in_=eg[:, :]), ["eg"], ["eg"])
                        S.add("dve", lambda h: h.tensor_tensor(out=eg[:, :], in0=grs[:, :], in1=eg[:, :],
                                                               op=ALU.mult), ["grs", "eg"], ["eg"])
                        S.add("dve", lambda h, og=og: h.tensor_tensor(out=og[:, :], in0=og[:, :], in1=eg[:, :], op=ALU.mult),
                              [ogk, "eg"], [ogk])
                        def og_transposes(og=og, ogk=ogk, wb_i=wb_i, w_i=w_i):
                            for half in range(2):
                                pb, pk = bank()
                                for q in range(4):
                                    kc = half * 4 + q
                                    S.add("pe", lambda h, pb=pb, q=q, kc=kc: h.transpose(
                                        out=pb[:, q * 128:(q + 1) * 128], in_=og[:, kc * 128:(kc + 1) * 128], identity=ident),
                                        [ogk, "cst"], [pk])
                                evac(ogTw[:, half * 4:half * 4 + 4, wb_i * 128:(wb_i + 1) * 128],
                                     pb.rearrange("p (a b) -> p a b", a=4), [pk], ["ogTw"])
                            if wb_i == 4:
                                S.dma(lambda h: h.dma_start(out=ogT_d[w_i], in_=ogTw[:, :, :].rearrange("p a b -> p (a b)")),
                                      reads=["ogTw"], writes=["ogT_d"])
                        deferred.append(og_transposes)
                    def state_update(gv_=gv_, gvk=gvk):
                        ps_, psk = dbank()
                        for hh in range(4):
                            S.add("pe", lambda h, hh=hh, ps_=ps_, gv_=gv_: h.matmul(
                                ps_[:, hh * 256:(hh + 1) * 256], lhsT=khat[:, hh * 128:(hh + 1) * 128],
                                rhs=gv_[:, hh * 256:(hh + 1) * 256], start=True, stop=True), ["khat", gvk], [psk[hh // 2]])
                        for hh in range(4):
                            S.add("dve", lambda h, hh=hh, ps_=ps_: h.scalar_tensor_tensor(
                                out=Sst[:, hh * 256:(hh + 1) * 256], in0=Sst[:, hh * 256:(hh + 1) * 256],
                                scalar=dec[:, 2 * hh:2 * hh + 1], in1=ps_[:, hh * 256:(hh + 1) * 256],
                                op0=ALU.mult, op1=ALU.add), ["Sst", "dec", psk[hh // 2]], ["Sst"])
                    deferred.append(state_update)
                    if own and wb_i == 4:
                        S.dma(lambda h, w_i=w_i: h.dma_start(out=hnT_d[w_i], in_=hnTw[:, :, :].rearrange("p a b -> p (a b)")),
                              reads=["hnTw"], writes=["hnT_d"])
                while deferred:
                    deferred.pop(0)()
                S.barrier()
                S.emit()

        mixer_phase(False)
        mixer_phase(True)
        abst.close()

        with contextlib.ExitStack() as st:
            NM = 512
            NHL = 2 * NW
            qposb = T(st, [128, NQ], F32)
            S.dma(lambda h: h.dma_start(out=qposb[:, :], in_=qpos_d), writes=["qposb"])
            qposh = T(st, [128, NHL], F32)
            for w_i in range(NW):
                S.add("pool", lambda h, w_i=w_i: h.tensor_copy(out=qposh[:, 2 * w_i:2 * w_i + 2],
                                                               in_=qposb[:, w_i * WR + 126:w_i * WR + 128]),
                      ["qposb"], ["qposh"])
            kTh = [T(st, [128, NR], BF16) for _ in range(2)]
            qTh = [T(st, [128, NQ], BF16) for _ in range(2)]
            qhl = [T(st, [128, NHL], BF16) for _ in range(2)]
            vh = [T(st, [128, NBLK * 64], BF16) for _ in range(4)]
            sbo = [T(st, [64, NQ], BF16) for _ in range(4)]
            for a in range(4):
                S.add("pool", lambda h, a=a: h.memset(sbo[a][:, :], 0.0), [], [f"sbo{a}"])
            NSL = 4
            e_s = [T(st, [128, NM], F32) for _ in range(NSL)]
            sp_s = [T(st, [128, NM], BF16) for _ in range(NSL)]
            spm_s = [T(st, [128, NM], BF16) for _ in range(NSL)]
            w_s = [T(st, [128, NM], BF16) for _ in range(NSL)]
            wm_s = [T(st, [128, NM], BF16) for _ in range(NSL)]
            Rf_s = [T(st, [128, NM], F32) for _ in range(NSL)]
            Rb_s = [T(st, [128, NM], BF16) for _ in range(NSL)]

            class Job:
                def __init__(self, a, w_i, halo, ab):
                    self.a, self.w_i, self.halo, self.ab = a, w_i, halo, ab
                    self.par = ab // 2
                    if halo:
                        self.N = NHL
                        self.top = 16 * (NW - 1) + 12
                        self.mfrom = 0
                    else:
                        self.N = NM
                        self.top = 16 * w_i + 16
                        self.mfrom = 16 * w_i + 1
                    self.kb = self.top
                    self.len = self.top + 1

                def bind(self, si):
                    self.si = si
                    N = self.N
                    self.Z = PS[si // 2][:, (si % 2) * 512:(si % 2) * 512 + N]
                    self.Zk = f"ps{si}"
                    self.O = PS[2 + si // 2][0:64, (si % 2) * 512:(si % 2) * 512 + N]
                    self.Ok = f"ps{4 + si}"

                def q_ap(self):
                    if self.halo:
                        return qhl[self.par][self.a * 64:self.a * 64 + 64, 0:self.N], f"qhl{self.par}"
                    c0 = self.w_i * WR + 128
                    return qTh[self.par][self.a * 64:self.a * 64 + 64, c0:c0 + self.N], f"qTh{self.par}"

                def pos_ap(self):
                    if self.halo:
                        return qposh[:, 0:self.N], "qposh"
                    c0 = self.w_i * WR + 128
                    return qposb[:, c0:c0 + self.N], "qposb"

                def stage(self, sg):
                    a, si, kb, N = self.ab, self.si, self.kb, self.N
                    first, last, masked = kb == self.top, kb == 0, kb >= self.mfrom
                    Z, Zk, O, Ok = self.Z, self.Zk, self.O, self.Ok
                    e_, sp_, spm_, w_, wm_, Rf_, Rb_ = (t[si][:, 0:N] for t in (e_s, sp_s, spm_s, w_s, wm_s, Rf_s, Rb_s))
                    if sg == 0:
                        q, qk = self.q_ap()
                        par, hb = self.par, self.a * 64
                        S.add("pe", lambda h: h.matmul(Z, lhsT=kTh[par][hb:hb + 64, kb * 128:(kb + 1) * 128], rhs=q,
                                                       start=True, stop=True), [f"kTh{par}", qk], [Zk])
                    elif sg == 1:
                        S.add("act", lambda h: h.activation(out=e_, in_=Z, func=AF.Exp), [Zk], [f"e{si}"])
                    elif sg == 2:
                        if masked:
                            S.add("act", lambda h: h.activation(out=sp_, in_=e_, func=AF.Ln, bias=1.0), [f"e{si}"], [f"sp{si}"])
                        else:
                            S.add("act", lambda h: h.activation(out=spm_, in_=e_, func=AF.Ln, bias=1.0), [f"e{si}"], [f"spm{si}"])
                    elif sg == 3:
                        if masked:
                            p, pk_ = self.pos_ap()
                            S.add("dve", lambda h: h.scalar_tensor_tensor(out=spm_, in0=p, scalar=kpos[:, kb:kb + 1], in1=sp_,
                                                                          op0=ALU.is_gt, op1=ALU.mult),
                                  [pk_, "kpos", f"sp{si}"], [f"spm{si}"])
                    elif sg == 4:
                        S.add("pe", lambda h: h.matmul(Z, lhsT=negtri, rhs=spm_, start=False, stop=True,
                                                       skip_group_check=True), ["cstb", f"spm{si}"], [Zk])
                        if not first:
                            S.add("pe", lambda h: h.matmul(Z, lhsT=negones, rhs=Rb_, start=False, stop=True,
                                                           skip_group_check=True), ["cstb", f"Rb{si}"], [Zk])
                    elif sg == 5:
                        if masked:
                            S.add("act", lambda h: h.activation(out=w_, in_=Z, func=AF.Exp), [Zk], [f"w{si}"])
                        else:
                            S.add("act", lambda h: h.activation(out=wm_, in_=Z, func=AF.Exp), [Zk], [f"wm{si}"])
                    elif sg == 6:
                        if masked:
                            p, pk_ = self.pos_ap()
                            S.add("dve", lambda h: h.scalar_tensor_tensor(out=wm_, in0=p, scalar=kpos[:, kb:kb + 1], in1=w_,
                                                                          op0=ALU.is_gt, op1=ALU.mult),
                                  [pk_, "kpos", f"w{si}"], [f"wm{si}"])
                    elif sg == 7:
                        S.add("pe", lambda h: h.matmul(O, lhsT=vh[a][:, kb * 64:(kb + 1) * 64], rhs=wm_,
                                                       start=first, stop=last), [f"vh{a}", f"wm{si}"], [Ok])
                    elif sg == 8:
                        if not last:
                            if first:
                                S.add("dve", lambda h: h.tensor_copy(out=Rf_, in_=spm_), [f"spm{si}"], [f"Rf{si}"])
                            else:
                                S.add("dve", lambda h: h.tensor_tensor(out=Rf_, in0=Rf_, in1=spm_, op=ALU.add),
                                      [f"Rf{si}", f"spm{si}"], [f"Rf{si}"])
                    elif sg == 9:
                        if not last:
                            S.add("dve", lambda h: h.tensor_copy(out=Rb_, in_=Rf_), [f"Rf{si}"], [f"Rb{si}"])

                def finish(self):
                    a, O, Ok = self.ab, self.O, self.Ok
                    if self.halo:
                        for w_i in range(NW):
                            S.add("act", lambda h, w_i=w_i: h.copy(out=sbo[a][:, w_i * WR + 126:w_i * WR + 128],
                                                                   in_=O[:, 2 * w_i:2 * w_i + 2]), [Ok], [f"sbo{a}"])
                    else:
                        c0 = self.w_i * WR + 128
                        S.add("act", lambda h: h.copy(out=sbo[a][:, c0:c0 + self.N], in_=O), [Ok], [f"sbo{a}"])

            def load_hp(hp):
                par = hp % 2
                S.dma(lambda h, par=par, hp=hp: h.dma_start(out=kTh[par][:, :], in_=kT_d[hp]), reads=["kT_d"], writes=[f"kTh{par}"])
                S.dma(lambda h, par=par, hp=hp: h.dma_start(out=qTh[par][:, :], in_=qT_d[hp]), reads=["qT_d"], writes=[f"qTh{par}"])
                for w_i in range(NW):
                    S.add("pool", lambda h, par=par, w_i=w_i: h.tensor_copy(
                        out=qhl[par][:, 2 * w_i:2 * w_i + 2], in_=qTh[par][:, w_i * WR + 126:w_i * WR + 128]),
                        [f"qTh{par}"], [f"qhl{par}"])
                for a in range(2):
                    hd = 2 * hp + a
                    ab = 2 * par + a
                    S.dma(lambda h, ab=ab, hd=hd: h.dma_start(out=vh[ab][:, :], in_=v_d[hd]), reads=["v_d"], writes=[f"vh{ab}"])

            load_hp(0)
            load_hp(1)
            jobs = []
            left = {}
            for hp in range(4):
                jl = ([Job(a, w_i, False, 2 * (hp % 2) + a) for w_i in range(NW) for a in range(2)]
                      + [Job(a, 0, True, 2 * (hp % 2) + a) for a in range(2)])
                mains = sorted([j_ for j_ in jl if not j_.halo], key=lambda j_: -j_.len)
                halos = [j_ for j_ in jl if j_.halo]
                jl = mains[0:3] + halos[0:1] + mains[3:6] + halos[1:2] + mains[6:]
                for j_ in jl:
                    j_.hp = hp
                jobs += jl
                left[hp] = len(jl)
            slots = [None] * NSL
            sptr = [0] * NSL
            offs = [0, 2, 5, 7]
            tick = 0
            while jobs or any(sl is not None for sl in slots):
                for si in range(NSL):
                    if slots[si] is None and jobs and tick >= offs[si]:
                        slots[si] = jobs.pop(0)
                        slots[si].bind(si)
                        sptr[si] = 0
                    sl = slots[si]
                    if sl is None:
                        continue
                    sl.stage(sptr[si])
                    sptr[si] += 1
                    if sptr[si] == 10:
                        sptr[si] = 0
                        if sl.kb == 0:
                            sl.finish()
                            slots[si] = None
                            hp = sl.hp
                            left[hp] -= 1
                            if left[hp] == 0:
                                for a in range(2):
                                    hd = 2 * hp + a
                                    ab = 2 * (hp % 2) + a
                                    S.dma(lambda h, ab=ab, hd=hd: h.dma_start(out=sbT_d[hd], in_=sbo[ab][:, :]),
                                          reads=[f"sbo{ab}"], writes=["sbT_d"])
                                if hp + 2 < 4:
                                    load_hp(hp + 2)
                        else:
                            sl.kb -= 1
                tick += 1
            S.barrier()
            S.emit()
        bk[0] = 0

        with contextlib.ExitStack() as st:
            NH = 257
            H0 = 126
            wm = T(st, [128, 8, 2048], BF16)
            load_w(wm, "wm", 0, w_in, 0, 8, C_MG, 2048)
            wsb = T(st, [64, 8, D], BF16)
            load_w(wsb, "wsb", 0, wsb_d, 0, 8, 0, D, kdim=64)
            wgl = T(st, [128, 8, D], BF16)
            load_w(wgl, "wgl", 0, wgla_d, 0, 8, 0, D)
            wo = T(st, [128, 8, D], BF16)
            load_w(wo, "wo", 0, wo_d, 0, 8, 0, D)
            hnTw = T(st, [128, 8, WR], BF16)
            ogTw = T(st, [128, 8, WR], BF16)
            sbTw = T(st, [64, 8, WR], BF16)
            gT = T(st, [128, 16, WR], BF16)
            mpT = T(st, [128, 8, WR], BF16)
            S.add("pool", lambda h: h.memset(mpT[:, :, :], 0.0), [], ["mpT"])
            hn2Tw = hnTw
            g12 = T(st, [128, 2, D], F32)
            S.dma(lambda h: h.dma_start(out=g12[:, :, :], in_=gb_d[:, 1:3, :]), writes=["gb"])
            t1 = [T(st, [128, NH], F32) for _ in range(2)]
            t2 = [T(st, [128, NH], F32) for _ in range(2)]
            xw = [T(st, [128, D], F32)] * 2
            tt = T(st, [128, D], F32)
            h1 = [T(st, [128, D], F32)] * 2
            hn2 = T(st, [128, D], F32)
            stat = [T(st, [128, 8], F32) for _ in range(2)]
            for w_i in range(NW):
                S.dma(lambda h, w_i=w_i: h.dma_start(out=hnTw[:, :, :].rearrange("p a b -> p (a b)"), in_=hnT_d[w_i]),
                      reads=["hnT_d"], writes=["hnTw"])
                S.dma(lambda h, w_i=w_i: h.dma_start(out=ogTw[:, :, :].rearrange("p a b -> p (a b)"), in_=ogT_d[w_i]),
                      reads=["ogT_d"], writes=["ogTw"])
                S.dma(lambda h, w_i=w_i: h.dma_start(
                    out=sbTw[:, :, :], in_=sbT_d[:, :, w_i * WR:(w_i + 1) * WR].rearrange("h d n -> d h n")),
                    reads=["sbT_d"], writes=["sbTw"])
                for i in range(16):
                    for hf in range(2):
                        pb, pk = bank()
                        for kc in range(8):
                            S.add("pe", lambda h, pb=pb, kc=kc, i=i, hf=hf: h.matmul(
                                pb[:, 0:NH], lhsT=wm[:, kc, i * 128:(i + 1) * 128], rhs=hnTw[:, kc, H0 + hf * NH:H0 + (hf + 1) * NH],
                                start=(kc == 0), stop=(kc == 7)), ["wm", "hnTw"], [pk])
                        S.add("act", lambda h, pb=pb, i=i, hf=hf: h.activation(
                            out=gT[:, i, H0 + hf * NH:H0 + (hf + 1) * NH], in_=pb[:, 0:NH], func=AF.Sigmoid), [pk], ["gT"])
                for i in range(8):
                    for hf in range(2):
                        pb, pk = bank()
                        for hd in range(8):
                            S.add("pe", lambda h, pb=pb, hd=hd, i=i, hf=hf: h.matmul(
                                pb[:, 0:NH], lhsT=wsb[:, hd, i * 128:(i + 1) * 128], rhs=sbTw[:, hd, H0 + hf * NH:H0 + (hf + 1) * NH],
                                start=(hd == 0), stop=(hd == 7)), ["wsb", "sbTw"], [pk])
                        S.add("dve", lambda h, pb=pb, i=i, hf=hf: h.tensor_tensor(
                            out=t1[hf][:, :], in0=pb[:, 0:NH], in1=gT[:, i, H0 + hf * NH:H0 + (hf + 1) * NH], op=ALU.mult),
                            [pk, "gT"], [f"t1{hf}"])
                        pb2, pk2 = bank()
                        for kc in range(8):
                            S.add("pe", lambda h, pb2=pb2, kc=kc, i=i, hf=hf: h.matmul(
                                pb2[:, 0:NH], lhsT=wgl[:, kc, i * 128:(i + 1) * 128], rhs=ogTw[:, kc, H0 + hf * NH:H0 + (hf + 1) * NH],
                                start=(kc == 0), stop=(kc == 7)), ["wgl", "ogTw"], [pk2])
                        S.add("dve", lambda h, pb2=pb2, i=i, hf=hf: h.tensor_tensor(
                            out=t2[hf][:, :], in0=pb2[:, 0:NH], in1=gT[:, 8 + i, H0 + hf * NH:H0 + (hf + 1) * NH], op=ALU.mult),
                            [pk2, "gT"], [f"t2{hf}"])
                        S.add("pool", lambda h, i=i, hf=hf: h.tensor_tensor(
                            out=mpT[:, i, H0 + hf * NH:H0 + (hf + 1) * NH], in0=t1[hf][:, :], in1=t2[hf][:, :], op=ALU.add),
                            [f"t1{hf}", f"t2{hf}"], ["mpT"])
                d1def = []
                for b in range(5):
                    r0 = w_i * WR + b * 128
                    xi = 0
                    S.dma(lambda h, xi=xi, r0=r0: h.dma_start(out=xw[xi][:, :], in_=xo[r0:r0 + 128, :]), writes=[f"xw{xi}"])
                    pm, pmk = dbank()
                    for half in range(2):
                        for kc in range(8):
                            S.add("pe", lambda h, pm=pm, kc=kc, half=half, b=b: h.matmul(
                                pm[:, half * 512:(half + 1) * 512], lhsT=mpT[:, kc, b * 128:(b + 1) * 128],
                                rhs=wo[:, kc, half * 512:(half + 1) * 512], start=(kc == 0), stop=(kc == 7)),
                                ["mpT", "wo"], [pmk[half]])
                    while d1def:
                        d1def.pop(0)()
                    sk_ = f"dstat{xi}"
                    st_ = stat[xi]
                    S.add("act", lambda h, pm=pm, st_=st_: h.activation(out=tt[:, :], in_=pm[:, :], func=AF.Square,
                                                                        accum_out=st_[:, 0:1]), pmk, ["tt", sk_ + "a"])
                    rstd_from_ss(st_[:, 0:1], st_[:, 1:2], D, sk_ + "a", sk_ + "b")
                    S.add("dve", lambda h, pm=pm, st_=st_: h.scalar_tensor_tensor(
                        out=tt[:, :], in0=pm[:, :], scalar=st_[:, 1:2], in1=g12[:, 0, :], op0=ALU.mult, op1=ALU.mult),
                        pmk + [sk_ + "b", "gb"], ["tt"])
                    S.add("dve", lambda h, xi=xi: h.tensor_tensor(out=h1[xi][:, :], in0=tt[:, :], in1=xw[xi][:, :], op=ALU.add),
                          ["tt", f"xw{xi}"], [f"h1{xi}"])
                    S.dma(lambda h, xi=xi, r0=r0: h.dma_start(out=h1_d[r0:r0 + 128, :], in_=h1[xi][:, :]),
                          reads=[f"h1{xi}"], writes=["h1_d"])
                    S.add("act", lambda h, xi=xi, st_=st_: h.activation(out=tt[:, :], in_=h1[xi][:, :], func=AF.Square,
                                                                        accum_out=st_[:, 2:3]), [f"h1{xi}"], ["tt", sk_ + "c"])
                    rstd_from_ss(st_[:, 2:3], st_[:, 3:4], D, sk_ + "c", sk_ + "d")
                    S.add("dve", lambda h, xi=xi, st_=st_: h.scalar_tensor_tensor(
                        out=hn2[:, :], in0=h1[xi][:, :], scalar=st_[:, 3:4], in1=g12[:, 1, :], op0=ALU.mult, op1=ALU.mult),
                        [f"h1{xi}", sk_ + "d", "gb"], ["hn2"])
                    def hn2_transposes(b=b):
                        for half in range(2):
                            pb, pk = bank()
                            for q in range(4):
                                kc = half * 4 + q
                                S.add("pe", lambda h, pb=pb, q=q, kc=kc: h.transpose(
                                    out=pb[:, q * 128:(q + 1) * 128], in_=hn2[:, kc * 128:(kc + 1) * 128], identity=ident),
                                    ["hn2", "cst"], [pk])
                            evac(hn2Tw[:, half * 4:half * 4 + 4, b * 128:(b + 1) * 128],
                                 pb.rearrange("p (a b) -> p a b", a=4), [pk], ["hnTw"])
                    d1def.append(hn2_transposes)
                while d1def:
                    d1def.pop(0)()
                S.dma(lambda h, w_i=w_i: h.dma_start(out=hn2T_d[w_i], in_=hn2Tw[:, :, :].rearrange("p a b -> p (a b)")),
                      reads=["hnTw"], writes=["hn2T_d"])
            S.barrier()
            S.emit()

        with contextlib.ExitStack() as st:
            NH = 257
            H0 = 126
            wup = T(st, [128, 8, DFF], BF16)
            load_w(wup, "wup", 0, wup_d, 0, 8, 0, DFF)
            wgt = T(st, [128, 8, DFF], BF16)
            load_w(wgt, "wgt", 0, wgt_d, 0, 8, 0, DFF)
            cw = T(st, [128, NCF, 4], F32)
            S.dma(lambda h: h.dma_start(out=cw[:, :, :], in_=cw_d), writes=["cw"])
            hn2Tw = [T(st, [128, 8, WR], BF16)] * 2
            prodT = [T(st, [128, NCF, WR], BF16)] * 2
            S.add("pool", lambda h: h.memset(prodT[0][:, :, :], 0.0), [], ["prodT0"])
            upS = [T(st, [128, WR + 2], F32) for _ in range(2)]
            acc = [T(st, [128, WR], F32) for _ in range(2)]
            a2 = [T(st, [128, WR], F32) for _ in range(2)]
            sg = [T(st, [128, WR], F32) for _ in range(2)]
            gtS = [T(st, [128, 512], BF16) for _ in range(2)]
            for i in range(2):
                S.add("pool", lambda h, i=i: h.memset(upS[i][:, :], 0.0), [], [f"upS{i}"])
            for w_i in range(NW):
                wi = 0
                S.dma(lambda h, w_i=w_i, wi=wi: h.dma_start(out=hn2Tw[wi][:, :, :].rearrange("p a b -> p (a b)"),
                                                           in_=hn2T_d[w_i]), reads=["hn2T_d"], writes=[f"hn2Tw{wi}"])
                for c in range(NCF):
                    ci = c % 2
                    pgs = []
                    for hf in range(2):
                        pu, puk = bank()
                        for kc in range(8):
                            S.add("pe", lambda h, pu=pu, kc=kc, c=c, hf=hf, wi=wi: h.matmul(
                                pu[:, 0:NH], lhsT=wup[:, kc, c * 128:(c + 1) * 128], rhs=hn2Tw[wi][:, kc, H0 + hf * NH:H0 + (hf + 1) * NH],
                                start=(kc == 0), stop=(kc == 7)), ["wup", f"hn2Tw{wi}"], [puk])
                        S.add("act", lambda h, pu=pu, hf=hf, ci=ci: h.copy(out=upS[ci][:, 2 + H0 + hf * NH:2 + H0 + (hf + 1) * NH],
                                                                           in_=pu[:, 0:NH]), [puk], [f"upS{ci}"])
                        pg, pgk = bank()
                        for kc in range(8):
                            S.add("pe", lambda h, pg=pg, kc=kc, c=c, hf=hf, wi=wi: h.matmul(
                                pg[:, 0:256], lhsT=wgt[:, kc, c * 128:(c + 1) * 128], rhs=hn2Tw[wi][:, kc, 128 + hf * 256:128 + (hf + 1) * 256],
                                start=(kc == 0), stop=(kc == 7)), ["wgt", f"hn2Tw{wi}"], [pgk])
                        evac(gtS[ci][:, hf * 256:(hf + 1) * 256], pg[:, 0:256], [pgk], [f"gtS{ci}"])
                    ak, a2k, sgk = f"acc{ci}", f"a2{ci}", f"sg{ci}"
                    S.add("dve", lambda h, c=c, ci=ci: h.tensor_scalar(
                        out=acc[ci][:, 128:WR], in0=upS[ci][:, 130:WR + 2], scalar1=cw[:, c, 2:3], scalar2=cw[:, c, 3:4],
                        op0=ALU.mult, op1=ALU.add), [f"upS{ci}", "cw"], [ak])
                    S.add("dve", lambda h, c=c, ci=ci: h.scalar_tensor_tensor(
                        out=acc[ci][:, 128:WR], in0=upS[ci][:, 129:WR + 1], scalar=cw[:, c, 1:2], in1=acc[ci][:, 128:WR],
                        op0=ALU.mult, op1=ALU.add), [f"upS{ci}", "cw", ak], [ak])
                    S.add("dve", lambda h, c=c, ci=ci: h.scalar_tensor_tensor(
                        out=acc[ci][:, 128:WR], in0=upS[ci][:, 128:WR], scalar=cw[:, c, 0:1], in1=acc[ci][:, 128:WR],
                        op0=ALU.mult, op1=ALU.add), [f"upS{ci}", "cw", ak], [ak])
                    S.add("act", lambda h, ci=ci: h.activation(out=a2[ci][:, 128:WR], in_=acc[ci][:, 128:WR], func=AF.Square,
                                                               scale=0.044715 ** 0.5), [ak], [a2k])
                    S.add("dve", lambda h, ci=ci: h.scalar_tensor_tensor(
                        out=a2[ci][:, 128:WR], in0=a2[ci][:, 128:WR], scalar=1.0, in1=acc[ci][:, 128:WR],
                        op0=ALU.add, op1=ALU.mult), [a2k, ak], [a2k])
                    S.add("act", lambda h, ci=ci: h.activation(out=sg[ci][:, 128:WR], in_=a2[ci][:, 128:WR], func=AF.Sigmoid,
                                                               scale=GK_), [a2k], [sgk])
                    S.add("dve", lambda h, ci=ci: h.tensor_tensor(out=sg[ci][:, 128:WR], in0=sg[ci][:, 128:WR], in1=acc[ci][:, 128:WR],
                                                                  op=ALU.mult), [sgk, ak], [sgk])
                    S.add("dve", lambda h, c=c, ci=ci, wi=wi: h.tensor_tensor(
                        out=prodT[wi][:, c, 128:WR], in0=gtS[ci][:, :], in1=sg[ci][:, 128:WR],
                        op=ALU.mult), [f"gtS{ci}", sgk], [f"prodT{wi}"])
                S.dma(lambda h, w_i=w_i, wi=wi: h.dma_start(out=prodT_d[w_i], in_=prodT[wi][:, :, :].rearrange("p a b -> p (a b)")),
                      reads=[f"prodT{wi}"], writes=["prodT_d"])
            S.barrier()
            S.emit()

        with contextlib.ExitStack() as st:
            wdn = T(st, [128, NCF, D], BF16)
            load_w(wdn, "wdn", 0, wdn_d, 0, NCF, 0, D)
            prodT = [T(st, [128, NCF, WR], BF16) for _ in range(2)]
            h1 = [T(st, [128, D], F32) for _ in range(2)]
            ot = [T(st, [128, D], F32) for _ in range(2)]
            tt = T(st, [128, D], F32)
            junk = T(st, [128, D], F32)
            g3 = T(st, [128, D], F32)
            S.dma(lambda h: h.dma_start(out=g3[:, :], in_=gb_d[:, 3, :]), writes=["gb"])
            stat = [T(st, [128, 4], F32) for _ in range(2)]
            for w_i in range(NW):
                wi = w_i % 2
                S.dma(lambda h, w_i=w_i, wi=wi: h.dma_start(out=prodT[wi][:, :, :].rearrange("p a b -> p (a b)"),
                                                           in_=prodT_d[w_i]), reads=["prodT_d"], writes=[f"prodT{wi}"])
                for b in range(1, 5):
                    r0 = w_i * WR + b * 128
                    xi = b % 2
                    S.dma(lambda h, xi=xi, r0=r0: h.dma_start(out=h1[xi][:, :], in_=h1_d[r0:r0 + 128, :]),
                          reads=["h1_d"], writes=[f"h1{xi}"])
                    pm, pmk = dbank()
                    for half in range(2):
                        for c in range(NCF):
                            S.add("pe", lambda h, pm=pm, c=c, half=half, b=b, wi=wi: h.matmul(
                                pm[:, half * 512:(half + 1) * 512], lhsT=prodT[wi][:, c, b * 128:(b + 1) * 128],
                                rhs=wdn[:, c, half * 512:(half + 1) * 512], start=(c == 0), stop=(c == NCF - 1)),
                                [f"prodT{wi}", "wdn"], [pmk[half]])
                    sk_ = f"estat{xi}"
                    st_ = stat[xi]
                    S.add("act", lambda h, pm=pm, st_=st_: h.activation(out=junk[:, :], in_=pm[:, :], func=AF.Square,
                                                                        accum_out=st_[:, 0:1]), pmk, ["junk", sk_ + "a"])
                    rstd_from_ss(st_[:, 0:1], st_[:, 1:2], D, sk_ + "a", sk_ + "b")
                    S.add("dve", lambda h, pm=pm, st_=st_: h.scalar_tensor_tensor(
                        out=tt[:, :], in0=pm[:, :], scalar=st_[:, 1:2], in1=g3[:, :], op0=ALU.mult, op1=ALU.mult),
                        pmk + [sk_ + "b", "gb"], ["tt"])
                    S.add("dve", lambda h, xi=xi: h.tensor_tensor(out=ot[xi][:, :], in0=tt[:, :], in1=h1[xi][:, :], op=ALU.add),
                          ["tt", f"h1{xi}"], [f"ot{xi}"])
                    o0 = w_i * 512 + (b - 1) * 128
                    S.dma(lambda h, xi=xi, o0=o0: h.dma_start(out=out_d[o0:o0 + 128, :], in_=ot[xi][:, :]),
                          reads=[f"ot{xi}"], writes=[f"out{o0}"])
            S.barrier()
            S.emit()
    return nc


_CACHE = {}


def _consts():
    j = np.arange(128)
    c = np.zeros((128, 5, 128), np.float32)
    c[:, 0, :] = np.eye(128, dtype=np.float32)
    c[:, 1, :] = -(j[:, None] >= j[None, :]).astype(np.float32)
    c[:, 2, :] = -1.0
    c[:, 3, :] = (j[:, None] <= j[None, :]).astype(np.float32)
    c[:, 4, :] = (j[:, None] > j[None, :]).astype(np.float32)
    return c


def kernel(x, meta_tokens, norm_mix_pre, w_in, w_gk_up, b_gk, gla_head_norm, w_sb_out, w_gla_out, w_o,
           norm_mix_post, norm_ffn_pre, w_ffn_up, w_ffn_gate, conv_w, conv_b, w_ffn_down, norm_ffn_post):
    f = np.float32
    x = np.asarray(x, f)
    B, Sq, _ = x.shape
    nslot = Sq // 2048
    NBLK = 16 * nslot + 1
    if nslot not in _CACHE:
        _CACHE[nslot] = build(nslot)
    nc = _CACHE[nslot]
    w_in0 = np.ascontiguousarray(np.asarray(w_in, f)[0])
    w_lr = np.zeros((D, 32), f)
    w_lr[:, :16] = w_in0[:, C_LR:C_LR + 16]
    wgk = np.zeros((32, 512), f)
    wgk[:16] = np.asarray(w_gk_up, f)[0]
    wgk[16] = np.asarray(b_gk, f)[0]
    gains = np.stack([np.asarray(g, f)[0] for g in (norm_mix_pre, norm_mix_post, norm_ffn_pre, norm_ffn_post)], 0)
    gains = np.ascontiguousarray(np.broadcast_to(gains[None], (128, 4, D)))
    ghead = np.ascontiguousarray(np.broadcast_to(np.asarray(gla_head_norm, f)[0][None], (128, 256)))
    cwb = np.concatenate([np.asarray(conv_w, f)[0], np.asarray(conv_b, f)], 0)
    cw = np.ascontiguousarray(cwb.reshape(4, NCF, 128).transpose(2, 1, 0))
    kpos = (np.arange(NBLK)[None, :] * 128 + np.arange(128)[:, None]).astype(f)
    shared = dict(cst=_consts(), w_in=w_in0, w_lr=w_lr, wgk=wgk,
                  w_sb_out=np.ascontiguousarray(np.asarray(w_sb_out, f)[0]),
                  w_gla_out=np.ascontiguousarray(np.asarray(w_gla_out, f)[0]),
                  w_o=np.ascontiguousarray(np.asarray(w_o, f)[0]),
                  w_up=np.ascontiguousarray(np.asarray(w_ffn_up, f)[0]),
                  w_gate=np.ascontiguousarray(np.asarray(w_ffn_gate, f)[0]),
                  w_down=np.ascontiguousarray(np.asarray(w_ffn_down, f)[0]),
                  gains=gains, ghead=ghead, cw=cw, kpos=kpos)
    in_maps = []
    for c in range(8):
        b, j = divmod(c, 4)
        xpad = np.zeros((NBLK * 128, D), f)
        xpad[112:128] = np.asarray(meta_tokens, f)
        xpad[128:] = x[b]
        rows = np.concatenate([np.arange(512 * (4 * m + j), 512 * (4 * m + j) + WR) for m in range(nslot)])
        xo = np.ascontiguousarray(xpad[rows])
        qpos = np.ascontiguousarray(np.broadcast_to(rows.astype(f)[None], (128, rows.size)))
        oh = np.zeros((128, 4), f)
        oh[:, j] = 1.0
        in_maps.append(dict(shared, xp=xpad, xo=xo, qpos=qpos, oh=oh))
    res = run_bass_kernel_spmd(nc, in_maps, core_ids=list(range(8)))
    out = np.zeros((B, Sq, D), f)
    for c in range(8):
        b, j = divmod(c, 4)
        o = res.results[c]["out"]
        for m in range(nslot):
            t = 4 * m + j
            out[b, 512 * t:512 * t + 512] = o[m * 512:(m + 1) * 512]
    return out
```

```python
import contextlib
import numpy as np
import concourse.bass as bass
import concourse.mybir as mybir
from concourse.bass_utils import run_bass_kernel_spmd

F32 = mybir.dt.float32
BF16 = mybir.dt.bfloat16
AF = mybir.ActivationFunctionType
ALU = mybir.AluOpType

ENGS = ("pe", "act", "dve", "pool", "sp")
NS = 8
EP = 4096
SAME_ENG_DIST = 1 << 30


class Sched:
    def __init__(self, nc):
        self.nc = nc
        self.ops = []
        self.last_w = {}
        self.readers = {}
        self.emitted = 0
        self.eng_rank = {e: 0 for e in ENGS}
        self.dma_cnt = {e: 0 for e in ENGS}
        self.dma_ops = {e: [] for e in ENGS}
        self.seen = {e: {} for e in ENGS}
        self.sems = {e: [] for e in ENGS}
        self.dsems = {}
        self.stack = contextlib.ExitStack()
        self.phase_dmas = []
        self.last_op = {e: None for e in ENGS}

    def add(self, eng, fn, reads=(), writes=(), dma=False):
        oid = len(self.ops)
        deps = set()
        for b in reads:
            w = self.last_w.get(b)
            if w is not None:
                deps.add(w)
        for b in writes:
            w = self.last_w.get(b)
            if w is not None:
                deps.add(w)
            rd = self.readers.get(b)
            if rd:
                deps.update(rd["e"].values())
                deps.update(rd["d"])
        for b in reads:
            rd = self.readers.setdefault(b, {"e": {}, "d": []})
            if dma:
                rd["d"].append(oid)
            else:
                rd["e"][eng] = oid
        for b in writes:
            self.last_w[b] = oid
            self.readers[b] = {"e": {}, "d": []}
        op = dict(eng=eng, fn=fn, dma=dma, deps=deps, id=oid)
        if dma:
            n = self.dma_cnt[eng]
            self.dma_cnt[eng] += 1
            op["tok"] = ("d", eng, n % NS, 16 * (n // NS + 1))
            if n >= NS:
                deps.add(self.dma_ops[eng][n - NS])
            self.dma_ops[eng].append(oid)
            op["erank"] = self.eng_rank[eng] + 1
            self.phase_dmas.append(oid)
        elif fn is not None:
            self.eng_rank[eng] += 1
            op["tok"] = ("e", eng, self.eng_rank[eng])
            op["erank"] = self.eng_rank[eng]
        else:
            op["tok"] = None
            op["erank"] = self.eng_rank[eng] + 1
        deps.discard(oid)
        self.ops.append(op)
        if fn is not None:
            self.last_op[eng] = oid
        return oid

    def dma(self, fn, reads=(), writes=()):
        return self.add("sp", fn, reads, writes, dma=True)

    def barrier(self):
        deps = set(v for v in self.last_op.values() if v is not None)
        deps.update(self.phase_dmas)
        self.phase_dmas = []
        for e in ENGS:
            oid = self.add(e, None)
            self.ops[oid]["deps"] = set(deps)

    def _sem(self, eng, idx):
        lst = self.sems[eng]
        while len(lst) <= idx:
            lst.append(self.stack.enter_context(self.nc.semaphore(f"s_{eng}_{len(lst)}")))
        return lst[idx]

    def _dsem(self, q, slot):
        k = (q, slot)
        if k not in self.dsems:
            self.dsems[k] = self.stack.enter_context(self.nc.semaphore(f"d_{q}_{slot}"))
        return self.dsems[k]

    def emit(self):
        ops = self.ops[self.emitted:]
        self.emitted = len(self.ops)
        streams = {e: [o for o in ops if o["eng"] == e] for e in ENGS}
        nc = self.nc
        for e in ENGS:
            if self.eng_rank[e] > 0:
                self._sem(e, (self.eng_rank[e] - 1) // EP)
            if self.dma_cnt[e] > 0:
                for sl in range(NS):
                    self._dsem(e, sl)
        with nc.Block() as block:
            def body(e, h):
                seen = self.seen[e]
                for op in streams[e]:
                    waits = {}
                    for d in op["deps"]:
                        tok = self.ops[d]["tok"]
                        if tok is None:
                            continue
                        if tok[0] == "e":
                            F, r = tok[1], tok[2]
                            if F == e:
                                if e == "pe" or op["erank"] - r > SAME_ENG_DIST:
                                    continue
                            if seen.get(F, 0) >= r:
                                continue
                            waits[F] = max(waits.get(F, 0), r)
                        else:
                            k = (tok[1], tok[2])
                            if seen.get(k, 0) >= tok[3]:
                                continue
                            waits[k] = max(waits.get(k, 0), tok[3])
                    for k, v in waits.items():
                        if isinstance(k, tuple):
                            h.wait_ge(self._dsem(*k), v)
                        else:
                            h.wait_ge(self._sem(k, (v - 1) // EP), (v - 1) % EP + 1)
                        seen[k] = v
                    if op["fn"] is None:
                        continue
                    ins = op["fn"](h)
                    tok = op["tok"]
                    if tok[0] == "e":
                        r = tok[2]
                        ins.then_inc(self._sem(e, (r - 1) // EP), 1)
                    else:
                        ins.then_inc(self._dsem(tok[1], tok[2]), 16)

            if streams["pe"]:
                @block.tensor
                def _(h):
                    body("pe", h)
            if streams["act"]:
                @block.scalar
                def _(h):
                    body("act", h)
            if streams["dve"]:
                @block.vector
                def _(h):
                    body("dve", h)
            if streams["pool"]:
                @block.gpsimd
                def _(h):
                    body("pool", h)
            if streams["sp"]:
                @block.sync
                def _(h):
                    body("sp", h)


D = 1024
DFF = 2816
NCF = 22
WR = 640
EPS = 1e-6
C_SBQ, C_SBK, C_SBV, C_GQ, C_GK, C_GV, C_GR, C_LR, C_MG = 0, 512, 1024, 1536, 2048, 2560, 3584, 4608, 4624
GK_ = 1.5957691216057308


def build(nslot):
    nc = bass.Bass("TRN2", target_bir_lowering=False)
    NBLK = 16 * nslot + 1
    NR = NBLK * 128
    NW = nslot
    NQ = NW * WR

    def din(name, shape, dt=F32):
        return nc.dram_tensor(name, shape, dt, kind="ExternalInput").ap()

    def dscr(name, shape, dt):
        return nc.dram_tensor(name, shape, dt).ap()

    xp = din("xp", [NR, D])
    xo = din("xo", [NQ, D])
    qpos_d = din("qpos", [128, NQ])
    kpos_d = din("kpos", [128, NBLK])
    oh_d = din("oh", [128, 4])
    cst_d = din("cst", [128, 5, 128])
    w_in = din("w_in", [D, 6672])
    w_lr = din("w_lr", [D, 32])
    wgk_d = din("wgk", [32, 512])
    wsb_d = din("w_sb_out", [512, D])
    wgla_d = din("w_gla_out", [D, D])
    wo_d = din("w_o", [D, D])
    wup_d = din("w_up", [D, DFF])
    wgt_d = din("w_gate", [D, DFF])
    wdn_d = din("w_down", [DFF, D])
    gb_d = din("gains", [128, 4, D])
    gh_d = din("ghead", [128, 256])
    cw_d = din("cw", [128, NCF, 4])
    out_d = nc.dram_tensor("out", [NW * 512, D], F32, kind="ExternalOutput").ap()

    kT_d = dscr("kT_d", [4, 128, NR], BF16)
    v_d = dscr("v_d", [8, 128, NBLK * 64], BF16)
    qT_d = dscr("qT_d", [4, 128, NQ], BF16)
    sbT_d = dscr("sbT_d", [8, 64, NQ], BF16)
    hnT_d = dscr("hnT_d", [NW, 128, 8 * WR], BF16)
    ogT_d = dscr("ogT_d", [NW, 128, 8 * WR], BF16)
    hn2T_d = dscr("hn2T_d", [NW, 128, 8 * WR], BF16)
    prodT_d = dscr("prodT_d", [NW, 128, NCF * WR], BF16)
    h1_d = dscr("h1_d", [NQ, D], F32)

    S = Sched(nc)
    uid = [0]

    def T(st, shape, dt, name=None):
        uid[0] += 1
        return st.enter_context(nc.sbuf_tensor("sb_" + (name or f"t{uid[0]}"), shape, dt))

    with S.stack:
        gst = S.stack
        PS = [gst.enter_context(nc.psum_tensor(f"ps{i}", [128, 1024], F32)) for i in range(4)]
        bk = [0]

        def bank():
            i = bk[0] % 8
            bk[0] += 1
            return PS[i // 2][:, (i % 2) * 512:(i % 2) * 512 + 512], f"ps{i}"

        def dbank():
            if bk[0] % 2:
                bk[0] += 1
            i = bk[0] % 8
            bk[0] += 2
            return PS[i // 2], [f"ps{i}", f"ps{i + 1}"]

        alt = [0]

        def evac(out_ap, in_ap, reads, writes, scale=None):
            alt[0] += 1
            if scale is not None:
                S.add("act", lambda h: h.activation(out=out_ap, in_=in_ap, func=AF.Copy, scale=scale), reads, writes)
            elif alt[0] % 2:
                S.add("act", lambda h: h.copy(out=out_ap, in_=in_ap), reads, writes)
            else:
                S.add("dve", lambda h: h.tensor_copy(out=out_ap, in_=in_ap), reads, writes)

        cst = T(gst, [128, 5, 128], F32, "cst")
        cstb = T(gst, [128, 2, 128], BF16, "cstb")
        m1rep = T(gst, [128, 4, 128], F32, "m1rep")
        ones2 = T(gst, [128, 2], F32, "ones2")
        gh = T(gst, [128, 256], F32, "gh")
        oh = T(gst, [128, 4], F32, "oh")
        kpos = T(gst, [128, NBLK], F32, "kpos")
        stage = [T(gst, [128, 512], F32, f"stage{i}") for i in range(6)]
        stg = [0]
        abst = gst.enter_context(contextlib.ExitStack())
        Sown = T(abst, [128, NW, D], F32, "Sown")
        S.dma(lambda h: h.dma_start(out=cst[:, :, :], in_=cst_d), writes=["cst"])
        S.dma(lambda h: h.dma_start(out=gh[:, :], in_=gh_d), writes=["gh"])
        S.dma(lambda h: h.dma_start(out=oh[:, :], in_=oh_d), writes=["oh"])
        S.dma(lambda h: h.dma_start(out=kpos[:, :], in_=kpos_d), writes=["kpos"])
        S.add("pool", lambda h: h.tensor_copy(out=cstb[:, :, :], in_=cst[:, 1:3, :]), ["cst"], ["cstb"])
        for i in range(4):
            S.add("pool", lambda h, i=i: h.tensor_copy(out=m1rep[:, i, :], in_=cst[:, 3, :]), ["cst"], ["m1rep"])
        S.add("pool", lambda h: h.memset(ones2[:, :], 1.0), [], ["ones2"])
        S.add("pool", lambda h: h.memset(Sown[:, :, :], 0.0), [], ["Sown"])
        ident = cst[:, 0, :]
        negtri = cstb[:, 0, :]
        negones = cstb[:, 1, :]
        M1 = cst[:, 3, :]
        M2 = cst[:, 4, :]

        def load_w(dst, dkey, col0, src, r0, nrows_chunks, c0, ncols, kdim=128):
            for kc in range(nrows_chunks):
                for cc in range(0, ncols, 512):
                    n = min(512, ncols - cc)
                    si = stg[0] % 6
                    stg[0] += 1
                    sk = f"stage{si}"
                    S.dma(lambda h, si=si, kc=kc, cc=cc, n=n: h.dma_start(
                        out=stage[si][0:kdim, 0:n],
                        in_=src[r0 + kc * kdim:r0 + (kc + 1) * kdim, c0 + cc:c0 + cc + n]), writes=[sk])
                    ceng = ("act", "dve", "pool", "act", "dve")[stg[0] % 5]
                    if ceng == "act":
                        S.add("act", lambda h, si=si, kc=kc, cc=cc, n=n: h.copy(
                            out=dst[0:kdim, kc, col0 + cc:col0 + cc + n], in_=stage[si][0:kdim, 0:n]), [sk], [dkey])
                    else:
                        S.add(ceng, lambda h, si=si, kc=kc, cc=cc, n=n: h.tensor_copy(
                            out=dst[0:kdim, kc, col0 + cc:col0 + cc + n], in_=stage[si][0:kdim, 0:n]), [sk], [dkey])

        def rstd_from_ss(ss_ap, rs_ap, n, key_in, key_out):
            S.add("act", lambda h: h.activation(out=rs_ap, in_=ss_ap, func=AF.Ln, scale=1.0 / n, bias=EPS),
                  [key_in], [key_out])
            S.add("act", lambda h: h.activation(out=rs_ap, in_=rs_ap, func=AF.Exp, scale=-0.5),
                  [key_out], [key_out])

        def mixer_phase(own):
            with contextlib.ExitStack() as st:
                if own:
                    WCOLS = 3616
                    O_SBQ, O_GQ, O_GK, O_GV, O_GR, O_LR = 0, 512, 1024, 1536, 2560, 3584
                else:
                    WCOLS = 2592
                    O_SBK, O_SBV, O_GK, O_GV, O_LR = 0, 512, 1024, 1536, 2560
                wA = T(st, [128, 8, WCOLS], BF16)
                wk = "wA"
                if own:
                    load_w(wA, wk, O_SBQ, w_in, 0, 8, C_SBQ, 512)
                    load_w(wA, wk, O_GQ, w_in, 0, 8, C_GQ, 1024)
                    load_w(wA, wk, O_GV, w_in, 0, 8, C_GV, 2048)
                    load_w(wA, wk, O_LR, w_lr, 0, 8, 0, 32)
                else:
                    load_w(wA, wk, O_SBK, w_in, 0, 8, C_SBK, 1024)
                    load_w(wA, wk, O_GK, w_in, 0, 8, C_GK, 1536)
                    load_w(wA, wk, O_LR, w_lr, 0, 8, 0, 32)
                wgk = T(st, [32, 1, 512], BF16)
                load_w(wgk, "wgk", 0, wgk_d, 0, 1, 0, 512, kdim=32)
                xt = [T(st, [128, D], F32) for _ in range(2)]
                g0 = T(st, [128, D], F32)
                S.dma(lambda h: h.dma_start(out=g0[:, :], in_=gb_d[:, 0, :]), writes=["gb"])
                junk = T(st, [128, D], F32)
                hn = [T(st, [128, D], F32) for _ in range(2)]
                hnT = [T(st, [128, 8, 128], BF16) for _ in range(2)]
                stat = [T(st, [128, 16], F32) for _ in range(2)]
                lrT = T(st, [32, 128], BF16)
                S.add("pool", lambda h: h.memset(lrT[:, :], 1.0), [], ["lrT"])
                e_t = T(st, [128, 512], F32)
                sp_t = T(st, [128, 512], F32)
                erev = T(st, [128, 512], F32)
                khat = T(st, [128, 512], BF16)
                gv = [T(st, [128, D], BF16) for _ in range(2)]
                dec = T(st, [128, 8], F32)
                Sst = T(st, [128, D], F32)
                if own:
                    hnTw = T(st, [128, 8, WR], BF16)
                    ogTw = T(st, [128, 8, WR], BF16)
                    qTw = T(st, [128, 4, 128], BF16)
                    eq = T(st, [128, 512], F32)
                    ek = T(st, [128, 512], F32)
                    qtl = T(st, [128, 512], BF16)
                    ktl = T(st, [128, 512], BF16)
                    attm = T(st, [128, 512], BF16)
                    Sbf = T(st, [128, D], BF16)
                    eg = T(st, [128, D], F32)
                    grs = T(st, [128, D], F32)
                    og2 = [T(st, [128, D], F32) for _ in range(2)]
                else:
                    kTb = T(st, [128, 4, 128], BF16)
                    vtb = T(st, [128, 512], BF16)
                    S.add("pool", lambda h: h.memset(Sst[:, :], 0.0), [], ["Sst"])

                nblocks = NW * 5 if own else NBLK
                src = xo if own else xp
                deferred = []
                for bi in range(nblocks):
                    w_i, wb_i = divmod(bi, 5)
                    xi = bi % 2
                    hi = bi % 2
                    if own:
                        og = og2[bi % 2]
                        ogk = f"og{bi % 2}"
                    xk, hk, hTk, sk_, gvk = f"xt{xi}", f"hn{hi}", f"hnT{hi}", f"stat{hi}", f"gv{hi}"
                    xt_, hn_, hnT_, st_, gv_ = xt[xi], hn[hi], hnT[hi], stat[hi], gv[hi]
                    S.dma(lambda h, xt_=xt_, bi=bi: h.dma_start(out=xt_[:, :], in_=src[bi * 128:(bi + 1) * 128, :]),
                          writes=[xk])
                    S.add("act", lambda h, xt_=xt_, st_=st_: h.activation(out=junk[:, :], in_=xt_[:, :], func=AF.Square,
                                                                          accum_out=st_[:, 0:1]), [xk], ["junk", sk_ + "a"])
                    rstd_from_ss(st_[:, 0:1], st_[:, 1:2], D, sk_ + "a", sk_ + "b")
                    S.add("dve", lambda h, xt_=xt_, st_=st_, hn_=hn_: h.scalar_tensor_tensor(
                        out=hn_[:, :], in0=xt_[:, :], scalar=st_[:, 1:2], in1=g0[:, :], op0=ALU.mult, op1=ALU.mult),
                        [xk, sk_ + "b", "gb"], [hk])
                    for half in range(2):
                        pb, pk = bank()
                        for q in range(4):
                            kc = half * 4 + q
                            S.add("pe", lambda h, pb=pb, q=q, kc=kc, hn_=hn_: h.transpose(
                                out=pb[:, q * 128:(q + 1) * 128], in_=hn_[:, kc * 128:(kc + 1) * 128], identity=ident),
                                [hk, "cst"], [pk])
                        evac(hnT_[:, half * 4:half * 4 + 4, :], pb.rearrange("p (a b) -> p a b", a=4), [pk], [hTk])
                    if own:
                        S.add("pool", lambda h, hnT_=hnT_, wb_i=wb_i: h.tensor_copy(
                            out=hnTw[:, :, wb_i * 128:(wb_i + 1) * 128], in_=hnT_[:, :, :]), [hTk], ["hnTw"])
                    while deferred:
                        deferred.pop(0)()
                    if own and wb_i == 0:
                        S.add("dve", lambda h, w_i=w_i: h.tensor_copy(out=Sst[:, :], in_=Sown[:, w_i, :]),
                              ["Sown"], ["Sst"])
                    if (not own) and bi % 4 == 0 and bi < 16 * nslot:
                        Tt = bi // 4
                        m_, j_ = divmod(Tt, 4)
                        S.add("dve", lambda h, m_=m_, j_=j_: h.scalar_tensor_tensor(
                            out=Sown[:, m_, :], in0=Sst[:, :], scalar=oh[:, j_:j_ + 1], in1=Sown[:, m_, :],
                            op0=ALU.mult, op1=ALU.add), ["Sst", "oh", "Sown"], ["Sown"])

                    def proj_tok(pb, pk, col0, n, hnT_=hnT_, hTk=hTk):
                        for kc in range(8):
                            S.add("pe", lambda h, kc=kc: h.matmul(pb[:, 0:n], lhsT=hnT_[:, kc, :],
                                                                   rhs=wA[:, kc, col0:col0 + n],
                                                                   start=(kc == 0), stop=(kc == 7)), [hTk, wk], [pk])

                    def proj_feat(out_ap, pk, col0, m, hnT_=hnT_, hTk=hTk):
                        for kc in range(8):
                            S.add("pe", lambda h, kc=kc: h.matmul(out_ap, lhsT=wA[:, kc, col0:col0 + m],
                                                                   rhs=hnT_[:, kc, :],
                                                                   start=(kc == 0), stop=(kc == 7)), [hTk, wk], [pk])

                    pb, pk = bank()
                    proj_feat(pb[0:32, 0:128], pk, O_LR, 32)
                    evac(lrT[0:16, :], pb[0:16, 0:128], [pk, "lrT"], ["lrT"])
                    hd_dst = qTw if own else kTb
                    hd_key = "qTw" if own else "kTb"
                    hd_col = O_SBQ if own else O_SBK
                    pb, pk = bank()
                    for pp in range(4):
                        proj_feat(pb[:, pp * 128:(pp + 1) * 128], pk, hd_col + pp * 128, 128)
                    evac(hd_dst[:, :, :], pb.rearrange("p (a b) -> p a b", a=4),
                         [pk], [hd_key], scale=(0.125 if own else None))
                    if own:
                        S.dma(lambda h, bi=bi: h.dma_start(
                            out=qT_d[:, :, bi * 128:(bi + 1) * 128].rearrange("h d n -> d h n"), in_=qTw[:, :, :]),
                            reads=["qTw"], writes=["qT_d"])
                    else:
                        S.dma(lambda h, bi=bi: h.dma_start(
                            out=kT_d[:, :, bi * 128:(bi + 1) * 128].rearrange("h d n -> d h n"), in_=kTb[:, :, :]),
                            reads=["kTb"], writes=["kT_d"])
                        pb, pk = bank()
                        proj_tok(pb, pk, O_SBV, 512)
                        evac(vtb[:, :], pb, [pk], ["vtb"])
                        S.dma(lambda h, bi=bi: h.dma_start(
                            out=v_d[:, :, bi * 64:(bi + 1) * 64].rearrange("h p d -> p h d"),
                            in_=vtb[:, :].rearrange("p (h d) -> p h d", h=8)), reads=["vtb"], writes=["v_d"])
                    if (not own) and bi == NBLK - 1:
                        continue
                    pg, pgk = bank()
                    S.add("pe", lambda h, pg=pg: h.matmul(pg, lhsT=lrT[:, :], rhs=wgk[:, 0, :], start=True, stop=True),
                          ["lrT", "wgk"], [pgk])
                    S.add("act", lambda h, pg=pg: h.activation(out=e_t[:, :], in_=pg, func=AF.Exp, scale=-1.0),
                          [pgk], ["e_t"])
                    S.add("act", lambda h: h.activation(out=sp_t[:, :], in_=e_t[:, :], func=AF.Ln, bias=1.0),
                          ["e_t"], ["sp_t"])
                    pkk, pkkk = bank()
                    proj_tok(pkk, pkkk, O_GK, 512)
                    pv, pvk = dbank()
                    for half in range(2):
                        for kc in range(8):
                            S.add("pe", lambda h, kc=kc, half=half, pv=pv, hnT_=hnT_: h.matmul(
                                pv[:, half * 512:(half + 1) * 512], lhsT=hnT_[:, kc, :],
                                rhs=wA[:, kc, O_GV + half * 512:O_GV + (half + 1) * 512],
                                start=(kc == 0), stop=(kc == 7)), [hTk, wk], [pvk[half]])
                    evac(gv_[:, :], pv[:, :], pvk, [gvk])

                    pr, prk = bank()
                    S.add("pe", lambda h, pr=pr: h.matmul(pr, lhsT=M2, rhs=sp_t[:, :], start=True, stop=True),
                          ["sp_t", "cst"], [prk])
                    S.add("act", lambda h, pr=pr: h.activation(out=erev[:, :], in_=pr, func=AF.Exp, scale=-1.0 / 16),
                          [prk], ["erev"])
                    pd, pdk = bank()
                    for hh in range(4):
                        S.add("pe", lambda h, pd=pd, hh=hh: h.matmul(pd[:, 2 * hh:2 * hh + 2],
                                                                      lhsT=sp_t[:, hh * 128:(hh + 1) * 128],
                                                                      rhs=ones2[:, :], start=True, stop=True),
                              ["sp_t", "ones2"], [pdk])
                    S.add("act", lambda h, pd=pd: h.activation(out=dec[:, :], in_=pd[:, 0:8], func=AF.Exp,
                                                               scale=-1.0 / 16), [pdk], ["dec"])
                    S.add("dve", lambda h, pkk=pkk: h.tensor_tensor(out=khat[:, :], in0=pkk, in1=erev[:, :], op=ALU.mult),
                          [pkkk, "erev"], ["khat"])
                    if own:
                        pc, pck = bank()
                        for hh in range(4):
                            S.add("pe", lambda h, pc=pc, hh=hh: h.matmul(pc[:, hh * 128:(hh + 1) * 128],
                                                                          lhsT=sp_t[:, hh * 128:(hh + 1) * 128],
                                                                          rhs=M1, start=True, stop=True),
                                  ["sp_t", "cst"], [pck])
                        S.add("act", lambda h, pc=pc: h.activation(out=eq[:, :], in_=pc, func=AF.Exp, scale=-1.0 / 16),
                              [pck], ["eq"])
                        S.add("act", lambda h, pc=pc: h.activation(out=ek[:, :], in_=pc, func=AF.Exp, scale=1.0 / 16),
                              [pck], ["ek"])
                        pq, pqk = bank()
                        for hh in range(4):
                            proj_feat(pq[:, hh * 128:(hh + 1) * 128], pqk, O_GQ + hh * 128, 128)
                        S.add("dve", lambda h, pq=pq: h.scalar_tensor_tensor(
                            out=qtl[:, :], in0=pq, scalar=128.0 ** -0.5, in1=eq[:, :], op0=ALU.mult, op1=ALU.mult),
                            [pqk, "eq"], ["qtl"])
                        pk2, pk2k = bank()
                        for hh in range(4):
                            proj_feat(pk2[:, hh * 128:(hh + 1) * 128], pk2k, O_GK + hh * 128, 128)
                        S.add("dve", lambda h, pk2=pk2: h.tensor_tensor(out=ktl[:, :], in0=pk2, in1=ek[:, :], op=ALU.mult),
                              [pk2k, "ek"], ["ktl"])
                        pa, pak = bank()
                        for hh in range(4):
                            S.add("pe", lambda h, pa=pa, hh=hh: h.matmul(pa[:, hh * 128:(hh + 1) * 128],
                                                                          lhsT=ktl[:, hh * 128:(hh + 1) * 128],
                                                                          rhs=qtl[:, hh * 128:(hh + 1) * 128],
                                                                          start=True, stop=True), ["ktl", "qtl"], [pak])
                        S.add("dve", lambda h, pa=pa: h.tensor_tensor(
                            out=attm[:, :], in0=pa, in1=m1rep[:, :, :].rearrange("p a b -> p (a b)"), op=ALU.mult),
                            [pak, "m1rep"], ["attm"])
                        S.add("pool", lambda h: h.tensor_copy(out=Sbf[:, :], in_=Sst[:, :]), ["Sst"], ["Sbf"])
                        po, pok = dbank()
                        for hh in range(4):
                            oc = po[:, hh * 256:(hh + 1) * 256]
                            S.add("pe", lambda h, oc=oc, hh=hh, gv_=gv_: h.matmul(
                                oc, lhsT=attm[:, hh * 128:(hh + 1) * 128], rhs=gv_[:, hh * 256:(hh + 1) * 256],
                                start=True, stop=False), ["attm", gvk], [pok[hh // 2]])
                            S.add("pe", lambda h, oc=oc, hh=hh: h.matmul(
                                oc, lhsT=qtl[:, hh * 128:(hh + 1) * 128], rhs=Sbf[:, hh * 256:(hh + 1) * 256],
                                start=False, stop=True), ["qtl", "Sbf"], [pok[hh // 2]])
                        for hh in range(4):
                            S.add("act", lambda h, hh=hh, po=po, st_=st_: h.activation(
                                out=junk[:, 0:256], in_=po[:, hh * 256:(hh + 1) * 256], func=AF.Square,
                                accum_out=st_[:, 4 + hh:5 + hh]), [pok[hh // 2]], ["junk", sk_ + "c"])
                        rstd_from_ss(st_[:, 4:8], st_[:, 8:12], 256, sk_ + "c", sk_ + "d")
                        for hh in range(4):
                            S.add("dve", lambda h, hh=hh, po=po, st_=st_, og=og: h.scalar_tensor_tensor(
                                out=og[:, hh * 256:(hh + 1) * 256], in0=po[:, hh * 256:(hh + 1) * 256],
                                scalar=st_[:, 8 + hh:9 + hh], in1=gh[:, :], op0=ALU.mult, op1=ALU.mult),
                                [pok[hh // 2], sk_ + "d", "gh", ogk], [ogk])
                        pgr, pgrk = dbank()
                        for half in range(2):
                            for kc in range(8):
                                S.add("pe", lambda h, kc=kc, half=half, pgr=pgr, hnT_=hnT_: h.matmul(
                                    pgr[:, half * 512:(half + 1) * 512], lhsT=hnT_[:, kc, :],
                                    rhs=wA[:, kc, O_GR + half * 512:O_GR + (half + 1) * 512],
                                    start=(kc == 0), stop=(kc == 7)), [hTk, wk], [pgrk[half]])
                        S.add("act", lambda h, pgr=pgr: h.copy(out=grs[:, :], in_=pgr[:, :]), pgrk, ["grs"])
                        S.add("act", lambda h: h.activation(out=eg[:, :], in_=grs[:, :], func=AF.Exp, scale=-1.0),
                              ["grs"], ["eg"])
                        S.add("dve", lambda h: h.tensor_scalar(out=eg[:, :], in0=eg[:, :], scalar1=1.0, scalar2=None,
                                                               op0=ALU.add), ["eg"], ["eg"])
                        S.add("dve", lambda h: h.reciprocal(out=eg[:, :], in_=eg[:, :]), ["eg"], ["eg"])
                        S.add("dve", lambda h: h.tensor_tensor(out=eg[:, :], in0=grs[:, :], in1=eg[:, :],
                                                               op=ALU.mult), ["grs", "eg"], ["eg"])
                        S.add("dve", lambda h, og=og: h.tensor_tensor(out=og[:, :], in0=og[:, :], in1=eg[:, :], op=ALU.mult),
                              [ogk, "eg"], [ogk])
                        def og_transposes(og=og, ogk=ogk, wb_i=wb_i, w_i=w_i):
                            for half in range(2):
                                pb, pk = bank()
                                for q in range(4):
                                    kc = half * 4 + q
                                    S.add("pe", lambda h, pb=pb, q=q, kc=kc: h.transpose(
                                        out=pb[:, q * 128:(q + 1) * 128], in_=og[:, kc * 128:(kc + 1) * 128], identity=ident),
                                        [ogk, "cst"], [pk])
                                evac(ogTw[:, half * 4:half * 4 + 4, wb_i * 128:(wb_i + 1) * 128],
                                     pb.rearrange("p (a b) -> p a b", a=4), [pk], ["ogTw"])
                            if wb_i == 4:
                                S.dma(lambda h: h.dma_start(out=ogT_d[w_i], in_=ogTw[:, :, :].rearrange("p a b -> p (a b)")),
                                      reads=["ogTw"], writes=["ogT_d"])
                        deferred.append(og_transposes)
                    def state_update(gv_=gv_, gvk=gvk):
                        ps_, psk = dbank()
                        for hh in range(4):
                            S.add("pe", lambda h, hh=hh, ps_=ps_, gv_=gv_: h.matmul(
                                ps_[:, hh * 256:(hh + 1) * 256], lhsT=khat[:, hh * 128:(hh + 1) * 128],
                                rhs=gv_[:, hh * 256:(hh + 1) * 256], start=True, stop=True), ["khat", gvk], [psk[hh // 2]])
                        for hh in range(4):
                            S.add("dve", lambda h, hh=hh, ps_=ps_: h.scalar_tensor_tensor(
                                out=Sst[:, hh * 256:(hh + 1) * 256], in0=Sst[:, hh * 256:(hh + 1) * 256],
                                scalar=dec[:, 2 * hh:2 * hh + 1], in1=ps_[:, hh * 256:(hh + 1) * 256],
                                op0=ALU.mult, op1=ALU.add), ["Sst", "dec", psk[hh // 2]], ["Sst"])
                    deferred.append(state_update)
                    if own and wb_i == 4:
                        S.dma(lambda h, w_i=w_i: h.dma_start(out=hnT_d[w_i], in_=hnTw[:, :, :].rearrange("p a b -> p (a b)")),
                              reads=["hnTw"], writes=["hnT_d"])
                while deferred:
                    deferred.pop(0)()
                S.barrier()
                S.emit()

        mixer_phase(False)
        mixer_phase(True)
        abst.close()

        with contextlib.ExitStack() as st:
            NM = 512
            NHL = 2 * NW
            qposb = T(st, [128, NQ], F32)
            S.dma(lambda h: h.dma_start(out=qposb[:, :], in_=qpos_d), writes=["qposb"])
            qposh = T(st, [128, NHL], F32)
            for w_i in range(NW):
                S.add("pool", lambda h, w_i=w_i: h.tensor_copy(out=qposh[:, 2 * w_i:2 * w_i + 2],
                                                               in_=qposb[:, w_i * WR + 126:w_i * WR + 128]),
                      ["qposb"], ["qposh"])
            mk = T(st, [128, 16, NM], BF16)
            for r in range(1, 17):
                S.add("dve", lambda h, r=r: h.tensor_scalar(out=mk[:, r - 1, :], in0=qposb[:, 128:128 + NM],
                                                            scalar1=kpos[:, r:r + 1], scalar2=None, op0=ALU.is_gt),
                      ["qposb", "kpos"], ["mk"])
            kTh = [T(st, [128, NR], BF16) for _ in range(2)]
            qTh = [T(st, [128, NQ], BF16) for _ in range(2)]
            qhl = [T(st, [128, NHL], BF16) for _ in range(2)]
            vh = [T(st, [128, NBLK * 64], BF16) for _ in range(4)]
            sbo = [T(st, [64, NQ], BF16) for _ in range(4)]
            for a in range(4):
                S.add("pool", lambda h, a=a: h.memset(sbo[a][:, :], 0.0), [], [f"sbo{a}"])
            NSL = 4
            e_s = [T(st, [128, NM], F32) for _ in range(NSL)]
            sp_s = [T(st, [128, NM], BF16) for _ in range(NSL)]
            spm_s = [T(st, [128, NM], BF16) for _ in range(NSL)]
            w_s = [T(st, [128, NM], BF16) for _ in range(NSL)]
            wm_s = [T(st, [128, NM], BF16) for _ in range(NSL)]
            Rf_s = [T(st, [128, NM], F32) for _ in range(NSL)]
            Rb_s = [T(st, [128, NM], BF16) for _ in range(NSL)]

            class Job:
                def __init__(self, a, w_i, halo, ab):
                    self.a, self.w_i, self.halo, self.ab = a, w_i, halo, ab
                    self.par = ab // 2
                    if halo:
                        self.N = NHL
                        self.top = 16 * (NW - 1) + 12
                        self.mfrom = 0
                    else:
                        self.N = NM
                        self.top = 16 * w_i + 16
                        self.mfrom = 16 * w_i + 1
                    self.kb = self.top
                    self.len = self.top + 1

                def bind(self, si):
                    self.si = si
                    N = self.N
                    self.Z = PS[si // 2][:, (si % 2) * 512:(si % 2) * 512 + N]
                    self.Zk = f"ps{si}"
                    self.O = PS[2 + si // 2][0:64, (si % 2) * 512:(si % 2) * 512 + N]
                    self.Ok = f"ps{4 + si}"

                def q_ap(self):
                    if self.halo:
                        return qhl[self.par][self.a * 64:self.a * 64 + 64, 0:self.N], f"qhl{self.par}"
                    c0 = self.w_i * WR + 128
                    return qTh[self.par][self.a * 64:self.a * 64 + 64, c0:c0 + self.N], f"qTh{self.par}"

                def pos_ap(self):
                    if self.halo:
                        return qposh[:, 0:self.N], "qposh"
                    c0 = self.w_i * WR + 128
                    return qposb[:, c0:c0 + self.N], "qposb"

                def stage(self, sg):
                    a, si, kb, N = self.ab, self.si, self.kb, self.N
                    first, last, masked = kb == self.top, kb == 0, kb >= self.mfrom
                    Z, Zk, O, Ok = self.Z, self.Zk, self.O, self.Ok
                    e_, sp_, spm_, w_, wm_, Rf_, Rb_ = (t[si][:, 0:N] for t in (e_s, sp_s, spm_s, w_s, wm_s, Rf_s, Rb_s))
                    if sg == 0:
                        q, qk = self.q_ap()
                        par, hb = self.par, self.a * 64
                        S.add("pe", lambda h: h.matmul(Z, lhsT=kTh[par][hb:hb + 64, kb * 128:(kb + 1) * 128], rhs=q,
                                                       start=True, stop=True), [f"kTh{par}", qk], [Zk])
                    elif sg == 1:
                        S.add("act", lambda h: h.activation(out=e_, in_=Z, func=AF.Exp), [Zk], [f"e{si}"])
                    elif sg == 2:
                        if masked:
                            S.add("act", lambda h: h.activation(out=sp_, in_=e_, func=AF.Ln, bias=1.0), [f"e{si}"], [f"sp{si}"])
                        else:
                            S.add("act", lambda h: h.activation(out=spm_, in_=e_, func=AF.Ln, bias=1.0), [f"e{si}"], [f"spm{si}"])
                    elif sg == 3:
                        if masked and not self.halo:
                            r = kb - 16 * self.w_i
                            S.add("dve", lambda h: h.tensor_tensor(out=spm_, in0=sp_, in1=mk[:, r - 1, 0:N], op=ALU.mult),
                                  ["mk", f"sp{si}"], [f"spm{si}"])
                        elif masked:
                            p, pk_ = self.pos_ap()
                            S.add("dve", lambda h: h.scalar_tensor_tensor(out=spm_, in0=p, scalar=kpos[:, kb:kb + 1], in1=sp_,
                                                                          op0=ALU.is_gt, op1=ALU.mult),
                                  [pk_, "kpos", f"sp{si}"], [f"spm{si}"])
                    elif sg == 4:
                        S.add("pe", lambda h: h.matmul(Z, lhsT=negtri, rhs=spm_, start=False, stop=True,
                                                       skip_group_check=True), ["cstb", f"spm{si}"], [Zk])
                        if not first:
                            S.add("pe", lambda h: h.matmul(Z, lhsT=negones, rhs=Rb_, start=False, stop=True,
                                                           skip_group_check=True), ["cstb", f"Rb{si}"], [Zk])
                    elif sg == 5:
                        if masked:
                            S.add("act", lambda h: h.activation(out=w_, in_=Z, func=AF.Exp), [Zk], [f"w{si}"])
                        else:
                            S.add("act", lambda h: h.activation(out=wm_, in_=Z, func=AF.Exp), [Zk], [f"wm{si}"])
                    elif sg == 6:
                        if masked and not self.halo:
                            r = kb - 16 * self.w_i
                            S.add("dve", lambda h: h.tensor_tensor(out=wm_, in0=w_, in1=mk[:, r - 1, 0:N], op=ALU.mult),
                                  ["mk", f"w{si}"], [f"wm{si}"])
                        elif masked:
                            p, pk_ = self.pos_ap()
                            S.add("dve", lambda h: h.scalar_tensor_tensor(out=wm_, in0=p, scalar=kpos[:, kb:kb + 1], in1=w_,
                                                                          op0=ALU.is_gt, op1=ALU.mult),
                                  [pk_, "kpos", f"w{si}"], [f"wm{si}"])
                    elif sg == 7:
                        S.add("pe", lambda h: h.matmul(O, lhsT=vh[a][:, kb * 64:(kb + 1) * 64], rhs=wm_,
                                                       start=first, stop=last), [f"vh{a}", f"wm{si}"], [Ok])
                    elif sg == 8:
                        if not last:
                            if first:
                                S.add("dve", lambda h: h.tensor_copy(out=Rf_, in_=spm_), [f"spm{si}"], [f"Rf{si}"])
                            else:
                                S.add("dve", lambda h: h.tensor_tensor(out=Rf_, in0=Rf_, in1=spm_, op=ALU.add),
                                      [f"Rf{si}", f"spm{si}"], [f"Rf{si}"])
                    elif sg == 9:
                        if not last:
                            S.add("dve", lambda h: h.tensor_copy(out=Rb_, in_=Rf_), [f"Rf{si}"], [f"Rb{si}"])

                def finish(self):
                    a, O, Ok = self.ab, self.O, self.Ok
                    if self.halo:
                        for w_i in range(NW):
                            S.add("act", lambda h, w_i=w_i: h.copy(out=sbo[a][:, w_i * WR + 126:w_i * WR + 128],
                                                                   in_=O[:, 2 * w_i:2 * w_i + 2]), [Ok], [f"sbo{a}"])
                    else:
                        c0 = self.w_i * WR + 128
                        S.add("act", lambda h: h.copy(out=sbo[a][:, c0:c0 + self.N], in_=O), [Ok], [f"sbo{a}"])

            def load_hp(hp):
                par = hp % 2
                S.dma(lambda h, par=par, hp=hp: h.dma_start(out=kTh[par][:, :], in_=kT_d[hp]), reads=["kT_d"], writes=[f"kTh{par}"])
                S.dma(lambda h, par=par, hp=hp: h.dma_start(out=qTh[par][:, :], in_=qT_d[hp]), reads=["qT_d"], writes=[f"qTh{par}"])
                for w_i in range(NW):
                    S.add("pool", lambda h, par=par, w_i=w_i: h.tensor_copy(
                        out=qhl[par][:, 2 * w_i:2 * w_i + 2], in_=qTh[par][:, w_i * WR + 126:w_i * WR + 128]),
                        [f"qTh{par}"], [f"qhl{par}"])
                for a in range(2):
                    hd = 2 * hp + a
                    ab = 2 * par + a
                    S.dma(lambda h, ab=ab, hd=hd: h.dma_start(out=vh[ab][:, :], in_=v_d[hd]), reads=["v_d"], writes=[f"vh{ab}"])

            load_hp(0)
            load_hp(1)
            jobs = []
            left = {}
            for hp in range(4):
                jl = ([Job(a, w_i, False, 2 * (hp % 2) + a) for w_i in range(NW) for a in range(2)]
                      + [Job(a, 0, True, 2 * (hp % 2) + a) for a in range(2)])
                jl.sort(key=lambda j_: -j_.len)
                for j_ in jl:
                    j_.hp = hp
                jobs += jl
                left[hp] = len(jl)
            slots = [None] * NSL
            sptr = [0] * NSL
            offs = [0, 2, 5, 7]
            tick = 0
            while jobs or any(sl is not None for sl in slots):
                for si in range(NSL):
                    if slots[si] is None and jobs and tick >= offs[si]:
                        slots[si] = jobs.pop(0)
                        slots[si].bind(si)
                        sptr[si] = 0
                    sl = slots[si]
                    if sl is None:
                        continue
                    sl.stage(sptr[si])
                    sptr[si] += 1
                    if sptr[si] == 10:
                        sptr[si] = 0
                        if sl.kb == 0:
                            sl.finish()
                            slots[si] = None
                            hp = sl.hp
                            left[hp] -= 1
                            if left[hp] == 0:
                                for a in range(2):
                                    hd = 2 * hp + a
                                    ab = 2 * (hp % 2) + a
                                    S.dma(lambda h, ab=ab, hd=hd: h.dma_start(out=sbT_d[hd], in_=sbo[ab][:, :]),
                                          reads=[f"sbo{ab}"], writes=["sbT_d"])
                                if hp + 2 < 4:
                                    load_hp(hp + 2)
                        else:
                            sl.kb -= 1
                tick += 1
            S.barrier()
            S.emit()
        bk[0] = 0

        with contextlib.ExitStack() as st:
            NH = 257
            H0 = 126
            wm = T(st, [128, 8, 2048], BF16)
            load_w(wm, "wm", 0, w_in, 0, 8, C_MG, 2048)
            wsb = T(st, [64, 8, D], BF16)
            load_w(wsb, "wsb", 0, wsb_d, 0, 8, 0, D, kdim=64)
            wgl = T(st, [128, 8, D], BF16)
            load_w(wgl, "wgl", 0, wgla_d, 0, 8, 0, D)
            wo = T(st, [128, 8, D], BF16)
            load_w(wo, "wo", 0, wo_d, 0, 8, 0, D)
            hnTw = T(st, [128, 8, WR], BF16)
            ogTw = T(st, [128, 8, WR], BF16)
            sbTw = T(st, [64, 8, WR], BF16)
            gT = T(st, [128, 16, WR], BF16)
            mpT = T(st, [128, 8, WR], BF16)
            S.add("pool", lambda h: h.memset(mpT[:, :, :], 0.0), [], ["mpT"])
            hn2Tw = hnTw
            g12 = T(st, [128, 2, D], F32)
            S.dma(lambda h: h.dma_start(out=g12[:, :, :], in_=gb_d[:, 1:3, :]), writes=["gb"])
            t1 = [T(st, [128, NH], F32) for _ in range(2)]
            t2 = [T(st, [128, NH], F32) for _ in range(2)]
            xw = [T(st, [128, D], F32)] * 2
            tt = T(st, [128, D], F32)
            h1 = [T(st, [128, D], F32)] * 2
            hn2 = T(st, [128, D], F32)
            stat = [T(st, [128, 8], F32) for _ in range(2)]
            for w_i in range(NW):
                S.dma(lambda h, w_i=w_i: h.dma_start(out=hnTw[:, :, :].rearrange("p a b -> p (a b)"), in_=hnT_d[w_i]),
                      reads=["hnT_d"], writes=["hnTw"])
                S.dma(lambda h, w_i=w_i: h.dma_start(out=ogTw[:, :, :].rearrange("p a b -> p (a b)"), in_=ogT_d[w_i]),
                      reads=["ogT_d"], writes=["ogTw"])
                S.dma(lambda h, w_i=w_i: h.dma_start(
                    out=sbTw[:, :, :], in_=sbT_d[:, :, w_i * WR:(w_i + 1) * WR].rearrange("h d n -> d h n")),
                    reads=["sbT_d"], writes=["sbTw"])
                for i in range(16):
                    for hf in range(2):
                        pb, pk = bank()
                        for kc in range(8):
                            S.add("pe", lambda h, pb=pb, kc=kc, i=i, hf=hf: h.matmul(
                                pb[:, 0:NH], lhsT=wm[:, kc, i * 128:(i + 1) * 128], rhs=hnTw[:, kc, H0 + hf * NH:H0 + (hf + 1) * NH],
                                start=(kc == 0), stop=(kc == 7)), ["wm", "hnTw"], [pk])
                        S.add("act", lambda h, pb=pb, i=i, hf=hf: h.activation(
                            out=gT[:, i, H0 + hf * NH:H0 + (hf + 1) * NH], in_=pb[:, 0:NH], func=AF.Sigmoid), [pk], ["gT"])
                for i in range(8):
                    for hf in range(2):
                        pb, pk = bank()
                        for hd in range(8):
                            S.add("pe", lambda h, pb=pb, hd=hd, i=i, hf=hf: h.matmul(
                                pb[:, 0:NH], lhsT=wsb[:, hd, i * 128:(i + 1) * 128], rhs=sbTw[:, hd, H0 + hf * NH:H0 + (hf + 1) * NH],
                                start=(hd == 0), stop=(hd == 7)), ["wsb", "sbTw"], [pk])
                        S.add("dve", lambda h, pb=pb, i=i, hf=hf: h.tensor_tensor(
                            out=t1[hf][:, :], in0=pb[:, 0:NH], in1=gT[:, i, H0 + hf * NH:H0 + (hf + 1) * NH], op=ALU.mult),
                            [pk, "gT"], [f"t1{hf}"])
                        pb2, pk2 = bank()
                        for kc in range(8):
                            S.add("pe", lambda h, pb2=pb2, kc=kc, i=i, hf=hf: h.matmul(
                                pb2[:, 0:NH], lhsT=wgl[:, kc, i * 128:(i + 1) * 128], rhs=ogTw[:, kc, H0 + hf * NH:H0 + (hf + 1) * NH],
                                start=(kc == 0), stop=(kc == 7)), ["wgl", "ogTw"], [pk2])
                        S.add("dve", lambda h, pb2=pb2, i=i, hf=hf: h.tensor_tensor(
                            out=t2[hf][:, :], in0=pb2[:, 0:NH], in1=gT[:, 8 + i, H0 + hf * NH:H0 + (hf + 1) * NH], op=ALU.mult),
                            [pk2, "gT"], [f"t2{hf}"])
                        S.add("pool", lambda h, i=i, hf=hf: h.tensor_tensor(
                            out=mpT[:, i, H0 + hf * NH:H0 + (hf + 1) * NH], in0=t1[hf][:, :], in1=t2[hf][:, :], op=ALU.add),
                            [f"t1{hf}", f"t2{hf}"], ["mpT"])
                d1def = []
                for b in range(5):
                    r0 = w_i * WR + b * 128
                    xi = 0
                    S.dma(lambda h, xi=xi, r0=r0: h.dma_start(out=xw[xi][:, :], in_=xo[r0:r0 + 128, :]), writes=[f"xw{xi}"])
                    pm, pmk = dbank()
                    for half in range(2):
                        for kc in range(8):
                            S.add("pe", lambda h, pm=pm, kc=kc, half=half, b=b: h.matmul(
                                pm[:, half * 512:(half + 1) * 512], lhsT=mpT[:, kc, b * 128:(b + 1) * 128],
                                rhs=wo[:, kc, half * 512:(half + 1) * 512], start=(kc == 0), stop=(kc == 7)),
                                ["mpT", "wo"], [pmk[half]])
                    while d1def:
                        d1def.pop(0)()
                    sk_ = f"dstat{xi}"
                    st_ = stat[xi]
                    S.add("act", lambda h, pm=pm, st_=st_: h.activation(out=tt[:, :], in_=pm[:, :], func=AF.Square,
                                                                        accum_out=st_[:, 0:1]), pmk, ["tt", sk_ + "a"])
                    rstd_from_ss(st_[:, 0:1], st_[:, 1:2], D, sk_ + "a", sk_ + "b")
                    S.add("dve", lambda h, pm=pm, st_=st_: h.scalar_tensor_tensor(
                        out=tt[:, :], in0=pm[:, :], scalar=st_[:, 1:2], in1=g12[:, 0, :], op0=ALU.mult, op1=ALU.mult),
                        pmk + [sk_ + "b", "gb"], ["tt"])
                    S.add("dve", lambda h, xi=xi: h.tensor_tensor(out=h1[xi][:, :], in0=tt[:, :], in1=xw[xi][:, :], op=ALU.add),
                          ["tt", f"xw{xi}"], [f"h1{xi}"])
                    S.dma(lambda h, xi=xi, r0=r0: h.dma_start(out=h1_d[r0:r0 + 128, :], in_=h1[xi][:, :]),
                          reads=[f"h1{xi}"], writes=["h1_d"])
                    S.add("act", lambda h, xi=xi, st_=st_: h.activation(out=tt[:, :], in_=h1[xi][:, :], func=AF.Square,
                                                                        accum_out=st_[:, 2:3]), [f"h1{xi}"], ["tt", sk_ + "c"])
                    rstd_from_ss(st_[:, 2:3], st_[:, 3:4], D, sk_ + "c", sk_ + "d")
                    S.add("dve", lambda h, xi=xi, st_=st_: h.scalar_tensor_tensor(
                        out=hn2[:, :], in0=h1[xi][:, :], scalar=st_[:, 3:4], in1=g12[:, 1, :], op0=ALU.mult, op1=ALU.mult),
                        [f"h1{xi}", sk_ + "d", "gb"], ["hn2"])
                    def hn2_transposes(b=b):
                        for half in range(2):
                            pb, pk = bank()
                            for q in range(4):
                                kc = half * 4 + q
                                S.add("pe", lambda h, pb=pb, q=q, kc=kc: h.transpose(
                                    out=pb[:, q * 128:(q + 1) * 128], in_=hn2[:, kc * 128:(kc + 1) * 128], identity=ident),
                                    ["hn2", "cst"], [pk])
                            evac(hn2Tw[:, half * 4:half * 4 + 4, b * 128:(b + 1) * 128],
                                 pb.rearrange("p (a b) -> p a b", a=4), [pk], ["hnTw"])
                    d1def.append(hn2_transposes)
                while d1def:
                    d1def.pop(0)()
                S.dma(lambda h, w_i=w_i: h.dma_start(out=hn2T_d[w_i], in_=hn2Tw[:, :, :].rearrange("p a b -> p (a b)")),
                      reads=["hnTw"], writes=["hn2T_d"])
            S.barrier()
            S.emit()

        with contextlib.ExitStack() as st:
            NH = 257
            H0 = 126
            wup = T(st, [128, 8, DFF], BF16)
            load_w(wup, "wup", 0, wup_d, 0, 8, 0, DFF)
            wgt = T(st, [128, 8, DFF], BF16)
            load_w(wgt, "wgt", 0, wgt_d, 0, 8, 0, DFF)
            cw = T(st, [128, NCF, 4], F32)
            S.dma(lambda h: h.dma_start(out=cw[:, :, :], in_=cw_d), writes=["cw"])
            hn2Tw = [T(st, [128, 8, WR], BF16)] * 2
            prodT = [T(st, [128, NCF, WR], BF16)] * 2
            S.add("pool", lambda h: h.memset(prodT[0][:, :, :], 0.0), [], ["prodT0"])
            upS = [T(st, [128, WR + 2], F32) for _ in range(2)]
            acc = [T(st, [128, WR], F32) for _ in range(2)]
            a2 = [T(st, [128, WR], F32) for _ in range(2)]
            sg = [T(st, [128, WR], F32) for _ in range(2)]
            gtS = [T(st, [128, 512], BF16) for _ in range(2)]
            for i in range(2):
                S.add("pool", lambda h, i=i: h.memset(upS[i][:, :], 0.0), [], [f"upS{i}"])
            for w_i in range(NW):
                wi = 0
                S.dma(lambda h, w_i=w_i, wi=wi: h.dma_start(out=hn2Tw[wi][:, :, :].rearrange("p a b -> p (a b)"),
                                                           in_=hn2T_d[w_i]), reads=["hn2T_d"], writes=[f"hn2Tw{wi}"])
                for c in range(NCF):
                    ci = c % 2
                    pgs = []
                    for hf in range(2):
                        pu, puk = bank()
                        for kc in range(8):
                            S.add("pe", lambda h, pu=pu, kc=kc, c=c, hf=hf, wi=wi: h.matmul(
                                pu[:, 0:NH], lhsT=wup[:, kc, c * 128:(c + 1) * 128], rhs=hn2Tw[wi][:, kc, H0 + hf * NH:H0 + (hf + 1) * NH],
                                start=(kc == 0), stop=(kc == 7)), ["wup", f"hn2Tw{wi}"], [puk])
                        S.add("act", lambda h, pu=pu, hf=hf, ci=ci: h.copy(out=upS[ci][:, 2 + H0 + hf * NH:2 + H0 + (hf + 1) * NH],
                                                                           in_=pu[:, 0:NH]), [puk], [f"upS{ci}"])
                        pg, pgk = bank()
                        for kc in range(8):
                            S.add("pe", lambda h, pg=pg, kc=kc, c=c, hf=hf, wi=wi: h.matmul(
                                pg[:, 0:256], lhsT=wgt[:, kc, c * 128:(c + 1) * 128], rhs=hn2Tw[wi][:, kc, 128 + hf * 256:128 + (hf + 1) * 256],
                                start=(kc == 0), stop=(kc == 7)), ["wgt", f"hn2Tw{wi}"], [pgk])
                        evac(gtS[ci][:, hf * 256:(hf + 1) * 256], pg[:, 0:256], [pgk], [f"gtS{ci}"])
                    ak, a2k, sgk = f"acc{ci}", f"a2{ci}", f"sg{ci}"
                    S.add("dve", lambda h, c=c, ci=ci: h.tensor_scalar(
                        out=acc[ci][:, 128:WR], in0=upS[ci][:, 130:WR + 2], scalar1=cw[:, c, 2:3], scalar2=cw[:, c, 3:4],
                        op0=ALU.mult, op1=ALU.add), [f"upS{ci}", "cw"], [ak])
                    S.add("dve", lambda h, c=c, ci=ci: h.scalar_tensor_tensor(
                        out=acc[ci][:, 128:WR], in0=upS[ci][:, 129:WR + 1], scalar=cw[:, c, 1:2], in1=acc[ci][:, 128:WR],
                        op0=ALU.mult, op1=ALU.add), [f"upS{ci}", "cw", ak], [ak])
                    S.add("dve", lambda h, c=c, ci=ci: h.scalar_tensor_tensor(
                        out=acc[ci][:, 128:WR], in0=upS[ci][:, 128:WR], scalar=cw[:, c, 0:1], in1=acc[ci][:, 128:WR],
                        op0=ALU.mult, op1=ALU.add), [f"upS{ci}", "cw", ak], [ak])
                    S.add("act", lambda h, ci=ci: h.activation(out=a2[ci][:, 128:WR], in_=acc[ci][:, 128:WR], func=AF.Square,
                                                               scale=0.044715 ** 0.5), [ak], [a2k])
                    S.add("dve", lambda h, ci=ci: h.scalar_tensor_tensor(
                        out=a2[ci][:, 128:WR], in0=a2[ci][:, 128:WR], scalar=1.0, in1=acc[ci][:, 128:WR],
                        op0=ALU.add, op1=ALU.mult), [a2k, ak], [a2k])
                    S.add("act", lambda h, ci=ci: h.activation(out=sg[ci][:, 128:WR], in_=a2[ci][:, 128:WR], func=AF.Sigmoid,
                                                               scale=GK_), [a2k], [sgk])
                    S.add("dve", lambda h, ci=ci: h.tensor_tensor(out=sg[ci][:, 128:WR], in0=sg[ci][:, 128:WR], in1=acc[ci][:, 128:WR],
                                                                  op=ALU.mult), [sgk, ak], [sgk])
                    S.add("dve", lambda h, c=c, ci=ci, wi=wi: h.tensor_tensor(
                        out=prodT[wi][:, c, 128:WR], in0=gtS[ci][:, :], in1=sg[ci][:, 128:WR],
                        op=ALU.mult), [f"gtS{ci}", sgk], [f"prodT{wi}"])
                S.dma(lambda h, w_i=w_i, wi=wi: h.dma_start(out=prodT_d[w_i], in_=prodT[wi][:, :, :].rearrange("p a b -> p (a b)")),
                      reads=[f"prodT{wi}"], writes=["prodT_d"])
            S.barrier()
            S.emit()

        with contextlib.ExitStack() as st:
            wdn = T(st, [128, NCF, D], BF16)
            load_w(wdn, "wdn", 0, wdn_d, 0, NCF, 0, D)
            prodT = [T(st, [128, NCF, WR], BF16) for _ in range(2)]
            h1 = [T(st, [128, D], F32) for _ in range(2)]
            ot = [T(st, [128, D], F32) for _ in range(2)]
            tt = T(st, [128, D], F32)
            junk = T(st, [128, D], F32)
            g3 = T(st, [128, D], F32)
            S.dma(lambda h: h.dma_start(out=g3[:, :], in_=gb_d[:, 3, :]), writes=["gb"])
            stat = [T(st, [128, 4], F32) for _ in range(2)]
            for w_i in range(NW):
                wi = w_i % 2
                S.dma(lambda h, w_i=w_i, wi=wi: h.dma_start(out=prodT[wi][:, :, :].rearrange("p a b -> p (a b)"),
                                                           in_=prodT_d[w_i]), reads=["prodT_d"], writes=[f"prodT{wi}"])
                for b in range(1, 5):
                    r0 = w_i * WR + b * 128
                    xi = b % 2
                    S.dma(lambda h, xi=xi, r0=r0: h.dma_start(out=h1[xi][:, :], in_=h1_d[r0:r0 + 128, :]),
                          reads=["h1_d"], writes=[f"h1{xi}"])
                    pm, pmk = dbank()
                    for half in range(2):
                        for c in range(NCF):
                            S.add("pe", lambda h, pm=pm, c=c, half=half, b=b, wi=wi: h.matmul(
                                pm[:, half * 512:(half + 1) * 512], lhsT=prodT[wi][:, c, b * 128:(b + 1) * 128],
                                rhs=wdn[:, c, half * 512:(half + 1) * 512], start=(c == 0), stop=(c == NCF - 1)),
                                [f"prodT{wi}", "wdn"], [pmk[half]])
                    sk_ = f"estat{xi}"
                    st_ = stat[xi]
                    S.add("act", lambda h, pm=pm, st_=st_: h.activation(out=junk[:, :], in_=pm[:, :], func=AF.Square,
                                                                        accum_out=st_[:, 0:1]), pmk, ["junk", sk_ + "a"])
                    rstd_from_ss(st_[:, 0:1], st_[:, 1:2], D, sk_ + "a", sk_ + "b")
                    S.add("dve", lambda h, pm=pm, st_=st_: h.scalar_tensor_tensor(
                        out=tt[:, :], in0=pm[:, :], scalar=st_[:, 1:2], in1=g3[:, :], op0=ALU.mult, op1=ALU.mult),
                        pmk + [sk_ + "b", "gb"], ["tt"])
                    S.add("dve", lambda h, xi=xi: h.tensor_tensor(out=ot[xi][:, :], in0=tt[:, :], in1=h1[xi][:, :], op=ALU.add),
                          ["tt", f"h1{xi}"], [f"ot{xi}"])
                    o0 = w_i * 512 + (b - 1) * 128
                    S.dma(lambda h, xi=xi, o0=o0: h.dma_start(out=out_d[o0:o0 + 128, :], in_=ot[xi][:, :]),
                          reads=[f"ot{xi}"], writes=[f"out{o0}"])
            S.barrier()
            S.emit()
    return nc


_CACHE = {}


def _consts():
    j = np.arange(128)
    c = np.zeros((128, 5, 128), np.float32)
    c[:, 0, :] = np.eye(128, dtype=np.float32)
    c[:, 1, :] = -(j[:, None] >= j[None, :]).astype(np.float32)
    c[:, 2, :] = -1.0
    c[:, 3, :] = (j[:, None] <= j[None, :]).astype(np.float32)
    c[:, 4, :] = (j[:, None] > j[None, :]).astype(np.float32)
    return c


def kernel(x, meta_tokens, norm_mix_pre, w_in, w_gk_up, b_gk, gla_head_norm, w_sb_out, w_gla_out, w_o,
           norm_mix_post, norm_ffn_pre, w_ffn_up, w_ffn_gate, conv_w, conv_b, w_ffn_down, norm_ffn_post):
    f = np.float32
    x = np.asarray(x, f)
    B, Sq, _ = x.shape
    nslot = Sq // 2048
    NBLK = 16 * nslot + 1
    if nslot not in _CACHE:
        _CACHE[nslot] = build(nslot)
    nc = _CACHE[nslot]
    w_in0 = np.ascontiguousarray(np.asarray(w_in, f)[0])
    w_lr = np.zeros((D, 32), f)
    w_lr[:, :16] = w_in0[:, C_LR:C_LR + 16]
    wgk = np.zeros((32, 512), f)
    wgk[:16] = np.asarray(w_gk_up, f)[0]
    wgk[16] = np.asarray(b_gk, f)[0]
    gains = np.stack([np.asarray(g, f)[0] for g in (norm_mix_pre, norm_mix_post, norm_ffn_pre, norm_ffn_post)], 0)
    gains = np.ascontiguousarray(np.broadcast_to(gains[None], (128, 4, D)))
    ghead = np.ascontiguousarray(np.broadcast_to(np.asarray(gla_head_norm, f)[0][None], (128, 256)))
    cwb = np.concatenate([np.asarray(conv_w, f)[0], np.asarray(conv_b, f)], 0)
    cw = np.ascontiguousarray(cwb.reshape(4, NCF, 128).transpose(2, 1, 0))
    kpos = (np.arange(NBLK)[None, :] * 128 + np.arange(128)[:, None]).astype(f)
    shared = dict(cst=_consts(), w_in=w_in0, w_lr=w_lr, wgk=wgk,
                  w_sb_out=np.ascontiguousarray(np.asarray(w_sb_out, f)[0]),
                  w_gla_out=np.ascontiguousarray(np.asarray(w_gla_out, f)[0]),
                  w_o=np.ascontiguousarray(np.asarray(w_o, f)[0]),
                  w_up=np.ascontiguousarray(np.asarray(w_ffn_up, f)[0]),
                  w_gate=np.ascontiguousarray(np.asarray(w_ffn_gate, f)[0]),
                  w_down=np.ascontiguousarray(np.asarray(w_ffn_down, f)[0]),
                  gains=gains, ghead=ghead, cw=cw, kpos=kpos)
    in_maps = []
    for c in range(8):
        b, j = divmod(c, 4)
        xpad = np.zeros((NBLK * 128, D), f)
        xpad[112:128] = np.asarray(meta_tokens, f)
        xpad[128:] = x[b]
        rows = np.concatenate([np.arange(512 * (4 * m + j), 512 * (4 * m + j) + WR) for m in range(nslot)])
        xo = np.ascontiguousarray(xpad[rows])
        qpos = np.ascontiguousarray(np.broadcast_to(rows.astype(f)[None], (128, rows.size)))
        oh = np.zeros((128, 4), f)
        oh[:, j] = 1.0
        in_maps.append(dict(shared, xp=xpad, xo=xo, qpos=qpos, oh=oh))
    res = run_bass_kernel_spmd(nc, in_maps, core_ids=list(range(8)))
    out = np.zeros((B, Sq, D), f)
    for c in range(8):
        b, j = divmod(c, 4)
        o = res.results[c]["out"]
        for m in range(nslot):
            t = 4 * m + j
            out[b, 512 * t:512 * t + 512] = o[m * 512:(m + 1) * 512]
    return out
```

```python
import contextlib
import numpy as np
import concourse.bass as bass
import concourse.mybir as mybir
from concourse.bass_utils import run_bass_kernel_spmd

F32 = mybir.dt.float32
BF16 = mybir.dt.bfloat16
AF = mybir.ActivationFunctionType
ALU = mybir.AluOpType

ENGS = ("pe", "act", "dve", "pool", "sp")
NS = 8
EP = 4096
SAME_ENG_DIST = 1 << 30


class Sched:
    def __init__(self, nc):
        self.nc = nc
        self.ops = []
        self.last_w = {}
        self.readers = {}
        self.emitted = 0
        self.eng_rank = {e: 0 for e in ENGS}
        self.dma_cnt = {e: 0 for e in ENGS}
        self.dma_ops = {e: [] for e in ENGS}
        self.seen = {e: {} for e in ENGS}
        self.sems = {e: [] for e in ENGS}
        self.dsems = {}
        self.stack = contextlib.ExitStack()
        self.phase_dmas = []
        self.last_op = {e: None for e in ENGS}

    def add(self, eng, fn, reads=(), writes=(), dma=False):
        oid = len(self.ops)
        deps = set()
        for b in reads:
            w = self.last_w.get(b)
            if w is not None:
                deps.add(w)
        for b in writes:
            w = self.last_w.get(b)
            if w is not None:
                deps.add(w)
            rd = self.readers.get(b)
            if rd:
                deps.update(rd["e"].values())
                deps.update(rd["d"])
        for b in reads:
            rd = self.readers.setdefault(b, {"e": {}, "d": []})
            if dma:
                rd["d"].append(oid)
            else:
                rd["e"][eng] = oid
        for b in writes:
            self.last_w[b] = oid
            self.readers[b] = {"e": {}, "d": []}
        op = dict(eng=eng, fn=fn, dma=dma, deps=deps, id=oid)
        if dma:
            n = self.dma_cnt[eng]
            self.dma_cnt[eng] += 1
            op["tok"] = ("d", eng, n % NS, 16 * (n // NS + 1))
            if n >= NS:
                deps.add(self.dma_ops[eng][n - NS])
            self.dma_ops[eng].append(oid)
            op["erank"] = self.eng_rank[eng] + 1
            self.phase_dmas.append(oid)
        elif fn is not None:
            self.eng_rank[eng] += 1
            op["tok"] = ("e", eng, self.eng_rank[eng])
            op["erank"] = self.eng_rank[eng]
        else:
            op["tok"] = None
            op["erank"] = self.eng_rank[eng] + 1
        deps.discard(oid)
        self.ops.append(op)
        if fn is not None:
            self.last_op[eng] = oid
        return oid

    def dma(self, fn, reads=(), writes=()):
        return self.add("sp", fn, reads, writes, dma=True)

    def barrier(self):
        deps = set(v for v in self.last_op.values() if v is not None)
        deps.update(self.phase_dmas)
        self.phase_dmas = []
        for e in ENGS:
            oid = self.add(e, None)
            self.ops[oid]["deps"] = set(deps)

    def _sem(self, eng, idx):
        lst = self.sems[eng]
        while len(lst) <= idx:
            lst.append(self.stack.enter_context(self.nc.semaphore(f"s_{eng}_{len(lst)}")))
        return lst[idx]

    def _dsem(self, q, slot):
        k = (q, slot)
        if k not in self.dsems:
            self.dsems[k] = self.stack.enter_context(self.nc.semaphore(f"d_{q}_{slot}"))
        return self.dsems[k]

    def emit(self):
        ops = self.ops[self.emitted:]
        self.emitted = len(self.ops)
        streams = {e: [o for o in ops if o["eng"] == e] for e in ENGS}
        nc = self.nc
        for e in ENGS:
            if self.eng_rank[e] > 0:
                self._sem(e, (self.eng_rank[e] - 1) // EP)
            if self.dma_cnt[e] > 0:
                for sl in range(NS):
                    self._dsem(e, sl)
        with nc.Block() as block:
            def body(e, h):
                seen = self.seen[e]
                for op in streams[e]:
                    waits = {}
                    for d in op["deps"]:
                        tok = self.ops[d]["tok"]
                        if tok is None:
                            continue
                        if tok[0] == "e":
                            F, r = tok[1], tok[2]
                            if F == e:
                                if e == "pe" or op["erank"] - r > SAME_ENG_DIST:
                                    continue
                            if seen.get(F, 0) >= r:
                                continue
                            waits[F] = max(waits.get(F, 0), r)
                        else:
                            k = (tok[1], tok[2])
                            if seen.get(k, 0) >= tok[3]:
                                continue
                            waits[k] = max(waits.get(k, 0), tok[3])
                    for k, v in waits.items():
                        if isinstance(k, tuple):
                            h.wait_ge(self._dsem(*k), v)
                        else:
                            h.wait_ge(self._sem(k, (v - 1) // EP), (v - 1) % EP + 1)
                        seen[k] = v
                    if op["fn"] is None:
                        continue
                    ins = op["fn"](h)
                    tok = op["tok"]
                    if tok[0] == "e":
                        r = tok[2]
                        ins.then_inc(self._sem(e, (r - 1) // EP), 1)
                    else:
                        ins.then_inc(self._dsem(tok[1], tok[2]), 16)

            if streams["pe"]:
                @block.tensor
                def _(h):
                    body("pe", h)
            if streams["act"]:
                @block.scalar
                def _(h):
                    body("act", h)
            if streams["dve"]:
                @block.vector
                def _(h):
                    body("dve", h)
            if streams["pool"]:
                @block.gpsimd
                def _(h):
                    body("pool", h)
            if streams["sp"]:
                @block.sync
                def _(h):
                    body("sp", h)


D = 1024
DFF = 2816
NCF = 22
WR = 640
EPS = 1e-6
C_SBQ, C_SBK, C_SBV, C_GQ, C_GK, C_GV, C_GR, C_LR, C_MG = 0, 512, 1024, 1536, 2048, 2560, 3584, 4608, 4624
GK_ = 1.5957691216057308


def build(nslot):
    nc = bass.Bass("TRN2", target_bir_lowering=False)
    NBLK = 16 * nslot + 1
    NR = NBLK * 128
    NW = nslot
    NQ = NW * WR

    def din(name, shape, dt=F32):
        return nc.dram_tensor(name, shape, dt, kind="ExternalInput").ap()

    def dscr(name, shape, dt):
        return nc.dram_tensor(name, shape, dt).ap()

    xp = din("xp", [NR, D])
    xo = din("xo", [NQ, D])
    qpos_d = din("qpos", [128, NQ])
    kpos_d = din("kpos", [128, NBLK])
    oh_d = din("oh", [128, 4])
    cst_d = din("cst", [128, 5, 128])
    w_in = din("w_in", [D, 6672])
    w_lr = din("w_lr", [D, 32])
    wgk_d = din("wgk", [32, 512])
    wsb_d = din("w_sb_out", [512, D])
    wgla_d = din("w_gla_out", [D, D])
    wo_d = din("w_o", [D, D])
    wup_d = din("w_up", [D, DFF])
    wgt_d = din("w_gate", [D, DFF])
    wdn_d = din("w_down", [DFF, D])
    gb_d = din("gains", [128, 4, D])
    gh_d = din("ghead", [128, 256])
    cw_d = din("cw", [128, NCF, 4])
    out_d = nc.dram_tensor("out", [NW * 512, D], F32, kind="ExternalOutput").ap()

    kT_d = dscr("kT_d", [4, 128, NR], BF16)
    v_d = dscr("v_d", [8, 128, NBLK * 64], BF16)
    qT_d = dscr("qT_d", [4, 128, NQ], BF16)
    sbT_d = dscr("sbT_d", [8, 64, NQ], BF16)
    hnT_d = dscr("hnT_d", [NW, 128, 8 * WR], BF16)
    ogT_d = dscr("ogT_d", [NW, 128, 8 * WR], BF16)
    hn2T_d = dscr("hn2T_d", [NW, 128, 8 * WR], BF16)
    prodT_d = dscr("prodT_d", [NW, 128, NCF * WR], BF16)
    h1_d = dscr("h1_d", [NQ, D], F32)

    S = Sched(nc)
    uid = [0]

    def T(st, shape, dt, name=None):
        uid[0] += 1
        return st.enter_context(nc.sbuf_tensor("sb_" + (name or f"t{uid[0]}"), shape, dt))

    with S.stack:
        gst = S.stack
        PS = [gst.enter_context(nc.psum_tensor(f"ps{i}", [128, 1024], F32)) for i in range(4)]
        bk = [0]

        def bank():
            i = bk[0] % 8
            bk[0] += 1
            return PS[i // 2][:, (i % 2) * 512:(i % 2) * 512 + 512], f"ps{i}"

        def dbank():
            if bk[0] % 2:
                bk[0] += 1
            i = bk[0] % 8
            bk[0] += 2
            return PS[i // 2], [f"ps{i}", f"ps{i + 1}"]

        alt = [0]

        def evac(out_ap, in_ap, reads, writes, scale=None):
            alt[0] += 1
            if scale is not None:
                S.add("act", lambda h: h.activation(out=out_ap, in_=in_ap, func=AF.Copy, scale=scale), reads, writes)
            elif alt[0] % 2:
                S.add("act", lambda h: h.copy(out=out_ap, in_=in_ap), reads, writes)
            else:
                S.add("dve", lambda h: h.tensor_copy(out=out_ap, in_=in_ap), reads, writes)

        cst = T(gst, [128, 5, 128], F32, "cst")
        cstb = T(gst, [128, 2, 128], BF16, "cstb")
        m1rep = T(gst, [128, 4, 128], F32, "m1rep")
        ones2 = T(gst, [128, 2], F32, "ones2")
        gh = T(gst, [128, 256], F32, "gh")
        oh = T(gst, [128, 4], F32, "oh")
        kpos = T(gst, [128, NBLK], F32, "kpos")
        stage = [T(gst, [128, 512], F32, f"stage{i}") for i in range(6)]
        stg = [0]
        abst = gst.enter_context(contextlib.ExitStack())
        Sown = T(abst, [128, NW, D], F32, "Sown")
        S.dma(lambda h: h.dma_start(out=cst[:, :, :], in_=cst_d), writes=["cst"])
        S.dma(lambda h: h.dma_start(out=gh[:, :], in_=gh_d), writes=["gh"])
        S.dma(lambda h: h.dma_start(out=oh[:, :], in_=oh_d), writes=["oh"])
        S.dma(lambda h: h.dma_start(out=kpos[:, :], in_=kpos_d), writes=["kpos"])
        S.add("pool", lambda h: h.tensor_copy(out=cstb[:, :, :], in_=cst[:, 1:3, :]), ["cst"], ["cstb"])
        for i in range(4):
            S.add("pool", lambda h, i=i: h.tensor_copy(out=m1rep[:, i, :], in_=cst[:, 3, :]), ["cst"], ["m1rep"])
        S.add("pool", lambda h: h.memset(ones2[:, :], 1.0), [], ["ones2"])
        S.add("pool", lambda h: h.memset(Sown[:, :, :], 0.0), [], ["Sown"])
        ident = cst[:, 0, :]
        negtri = cstb[:, 0, :]
        negones = cstb[:, 1, :]
        M1 = cst[:, 3, :]
        M2 = cst[:, 4, :]

        def load_w(dst, dkey, col0, src, r0, nrows_chunks, c0, ncols, kdim=128):
            for kc in range(nrows_chunks):
                for cc in range(0, ncols, 512):
                    n = min(512, ncols - cc)
                    si = stg[0] % 6
                    stg[0] += 1
                    sk = f"stage{si}"
                    S.dma(lambda h, si=si, kc=kc, cc=cc, n=n: h.dma_start(
                        out=stage[si][0:kdim, 0:n],
                        in_=src[r0 + kc * kdim:r0 + (kc + 1) * kdim, c0 + cc:c0 + cc + n]), writes=[sk])
                    ceng = ("act", "dve", "pool", "act", "dve")[stg[0] % 5]
                    if ceng == "act":
                        S.add("act", lambda h, si=si, kc=kc, cc=cc, n=n: h.copy(
                            out=dst[0:kdim, kc, col0 + cc:col0 + cc + n], in_=stage[si][0:kdim, 0:n]), [sk], [dkey])
                    else:
                        S.add(ceng, lambda h, si=si, kc=kc, cc=cc, n=n: h.tensor_copy(
                            out=dst[0:kdim, kc, col0 + cc:col0 + cc + n], in_=stage[si][0:kdim, 0:n]), [sk], [dkey])

        def rstd_from_ss(ss_ap, rs_ap, n, key_in, key_out):
            S.add("act", lambda h: h.activation(out=rs_ap, in_=ss_ap, func=AF.Ln, scale=1.0 / n, bias=EPS),
                  [key_in], [key_out])
            S.add("act", lambda h: h.activation(out=rs_ap, in_=rs_ap, func=AF.Exp, scale=-0.5),
                  [key_out], [key_out])

        def mixer_phase(own):
            with contextlib.ExitStack() as st:
                if own:
                    WCOLS = 3616
                    O_SBQ, O_GQ, O_GK, O_GV, O_GR, O_LR = 0, 512, 1024, 1536, 2560, 3584
                else:
                    WCOLS = 2592
                    O_SBK, O_SBV, O_GK, O_GV, O_LR = 0, 512, 1024, 1536, 2560
                wA = T(st, [128, 8, WCOLS], BF16)
                wk = "wA"
                if own:
                    def wkey(col):
                        return "wA0" if col < 512 else ("wA1" if col < 1536 else ("wA2" if col < 3584 else "wA3"))
                    load_w(wA, "wA3", O_LR, w_lr, 0, 8, 0, 32)
                    load_w(wA, "wA0", O_SBQ, w_in, 0, 8, C_SBQ, 512)
                    load_w(wA, "wA1", O_GQ, w_in, 0, 8, C_GQ, 1024)
                    load_w(wA, "wA2", O_GV, w_in, 0, 8, C_GV, 2048)
                else:
                    def wkey(col):
                        return "wA0" if col < 1024 else ("wA1" if col < 2560 else "wA2")
                    load_w(wA, "wA2", O_LR, w_lr, 0, 8, 0, 32)
                    load_w(wA, "wA0", O_SBK, w_in, 0, 8, C_SBK, 1024)
                    load_w(wA, "wA1", O_GK, w_in, 0, 8, C_GK, 1536)
                wgk = T(st, [32, 1, 512], BF16)
                load_w(wgk, "wgk", 0, wgk_d, 0, 1, 0, 512, kdim=32)
                xt = [T(st, [128, D], F32) for _ in range(2)]
                g0 = T(st, [128, D], F32)
                S.dma(lambda h: h.dma_start(out=g0[:, :], in_=gb_d[:, 0, :]), writes=["gb"])
                junk = T(st, [128, D], F32)
                hn = [T(st, [128, D], F32) for _ in range(2)]
                hnT = [T(st, [128, 8, 128], BF16) for _ in range(2)]
                stat = [T(st, [128, 16], F32) for _ in range(2)]
                lrT = T(st, [32, 128], BF16)
                S.add("pool", lambda h: h.memset(lrT[:, :], 1.0), [], ["lrT"])
                e_t = T(st, [128, 512], F32)
                sp_t = T(st, [128, 512], F32)
                erev = T(st, [128, 512], F32)
                khat = T(st, [128, 512], BF16)
                gv = [T(st, [128, D], BF16) for _ in range(2)]
                dec = T(st, [128, 8], F32)
                Sst = T(st, [128, D], F32)
                if own:
                    hnTw = T(st, [128, 8, WR], BF16)
                    ogTw = T(st, [128, 8, WR], BF16)
                    qTw = T(st, [128, 4, 128], BF16)
                    eq = T(st, [128, 512], F32)
                    ek = T(st, [128, 512], F32)
                    qtl = T(st, [128, 512], BF16)
                    ktl = T(st, [128, 512], BF16)
                    attm = T(st, [128, 512], BF16)
                    Sbf = T(st, [128, D], BF16)
                    eg = T(st, [128, D], F32)
                    grs = T(st, [128, D], F32)
                    og2 = [T(st, [128, D], F32) for _ in range(2)]
                else:
                    kTb = T(st, [128, 4, 128], BF16)
                    vtb = T(st, [128, 512], BF16)
                    S.add("pool", lambda h: h.memset(Sst[:, :], 0.0), [], ["Sst"])

                nblocks = NW * 5 if own else NBLK
                src = xo if own else xp
                deferred = []
                for bi in range(nblocks):
                    w_i, wb_i = divmod(bi, 5)
                    xi = bi % 2
                    hi = bi % 2
                    if own:
                        og = og2[bi % 2]
                        ogk = f"og{bi % 2}"
                    xk, hk, hTk, sk_, gvk = f"xt{xi}", f"hn{hi}", f"hnT{hi}", f"stat{hi}", f"gv{hi}"
                    xt_, hn_, hnT_, st_, gv_ = xt[xi], hn[hi], hnT[hi], stat[hi], gv[hi]
                    S.dma(lambda h, xt_=xt_, bi=bi: h.dma_start(out=xt_[:, :], in_=src[bi * 128:(bi + 1) * 128, :]),
                          writes=[xk])
                    S.add("act", lambda h, xt_=xt_, st_=st_: h.activation(out=junk[:, :], in_=xt_[:, :], func=AF.Square,
                                                                          accum_out=st_[:, 0:1]), [xk], ["junk", sk_ + "a"])
                    rstd_from_ss(st_[:, 0:1], st_[:, 1:2], D, sk_ + "a", sk_ + "b")
                    S.add("dve", lambda h, xt_=xt_, st_=st_, hn_=hn_: h.scalar_tensor_tensor(
                        out=hn_[:, :], in0=xt_[:, :], scalar=st_[:, 1:2], in1=g0[:, :], op0=ALU.mult, op1=ALU.mult),
                        [xk, sk_ + "b", "gb"], [hk])
                    for half in range(2):
                        pb, pk = bank()
                        for q in range(4):
                            kc = half * 4 + q
                            S.add("pe", lambda h, pb=pb, q=q, kc=kc, hn_=hn_: h.transpose(
                                out=pb[:, q * 128:(q + 1) * 128], in_=hn_[:, kc * 128:(kc + 1) * 128], identity=ident),
                                [hk, "cst"], [pk])
                        evac(hnT_[:, half * 4:half * 4 + 4, :], pb.rearrange("p (a b) -> p a b", a=4), [pk], [hTk])
                    if own:
                        S.add("pool", lambda h, hnT_=hnT_, wb_i=wb_i: h.tensor_copy(
                            out=hnTw[:, :, wb_i * 128:(wb_i + 1) * 128], in_=hnT_[:, :, :]), [hTk], ["hnTw"])
                    while deferred:
                        deferred.pop(0)()
                    if own and wb_i == 0:
                        S.add("dve", lambda h, w_i=w_i: h.tensor_copy(out=Sst[:, :], in_=Sown[:, w_i, :]),
                              ["Sown"], ["Sst"])
                    if (not own) and bi % 4 == 0 and bi < 16 * nslot:
                        Tt = bi // 4
                        m_, j_ = divmod(Tt, 4)
                        S.add("dve", lambda h, m_=m_, j_=j_: h.scalar_tensor_tensor(
                            out=Sown[:, m_, :], in0=Sst[:, :], scalar=oh[:, j_:j_ + 1], in1=Sown[:, m_, :],
                            op0=ALU.mult, op1=ALU.add), ["Sst", "oh", "Sown"], ["Sown"])

                    def proj_tok(pb, pk, col0, n, hnT_=hnT_, hTk=hTk):
                        for kc in range(8):
                            S.add("pe", lambda h, kc=kc: h.matmul(pb[:, 0:n], lhsT=hnT_[:, kc, :],
                                                                   rhs=wA[:, kc, col0:col0 + n],
                                                                   start=(kc == 0), stop=(kc == 7)), [hTk, wkey(col0)], [pk])

                    def proj_feat(out_ap, pk, col0, m, hnT_=hnT_, hTk=hTk):
                        for kc in range(8):
                            S.add("pe", lambda h, kc=kc: h.matmul(out_ap, lhsT=wA[:, kc, col0:col0 + m],
                                                                   rhs=hnT_[:, kc, :],
                                                                   start=(kc == 0), stop=(kc == 7)), [hTk, wkey(col0)], [pk])

                    pb, pk = bank()
                    proj_feat(pb[0:32, 0:128], pk, O_LR, 32)
                    evac(lrT[0:16, :], pb[0:16, 0:128], [pk, "lrT"], ["lrT"])
                    hd_dst = qTw if own else kTb
                    hd_key = "qTw" if own else "kTb"
                    hd_col = O_SBQ if own else O_SBK
                    pb, pk = bank()
                    for pp in range(4):
                        proj_feat(pb[:, pp * 128:(pp + 1) * 128], pk, hd_col + pp * 128, 128)
                    evac(hd_dst[:, :, :], pb.rearrange("p (a b) -> p a b", a=4),
                         [pk], [hd_key], scale=(0.125 if own else None))
                    if own:
                        S.dma(lambda h, bi=bi: h.dma_start(
                            out=qT_d[:, :, bi * 128:(bi + 1) * 128].rearrange("h d n -> d h n"), in_=qTw[:, :, :]),
                            reads=["qTw"], writes=["qT_d"])
                    else:
                        S.dma(lambda h, bi=bi: h.dma_start(
                            out=kT_d[:, :, bi * 128:(bi + 1) * 128].rearrange("h d n -> d h n"), in_=kTb[:, :, :]),
                            reads=["kTb"], writes=["kT_d"])
                        pb, pk = bank()
                        proj_tok(pb, pk, O_SBV, 512)
                        evac(vtb[:, :], pb, [pk], ["vtb"])
                        S.dma(lambda h, bi=bi: h.dma_start(
                            out=v_d[:, :, bi * 64:(bi + 1) * 64].rearrange("h p d -> p h d"),
                            in_=vtb[:, :].rearrange("p (h d) -> p h d", h=8)), reads=["vtb"], writes=["v_d"])
                    if (not own) and bi == NBLK - 1:
                        continue
                    pg, pgk = bank()
                    S.add("pe", lambda h, pg=pg: h.matmul(pg, lhsT=lrT[:, :], rhs=wgk[:, 0, :], start=True, stop=True),
                          ["lrT", "wgk"], [pgk])
                    S.add("act", lambda h, pg=pg: h.activation(out=e_t[:, :], in_=pg, func=AF.Exp, scale=-1.0),
                          [pgk], ["e_t"])
                    S.add("act", lambda h: h.activation(out=sp_t[:, :], in_=e_t[:, :], func=AF.Ln, bias=1.0),
                          ["e_t"], ["sp_t"])
                    pkk, pkkk = bank()
                    proj_tok(pkk, pkkk, O_GK, 512)
                    pv, pvk = dbank()
                    for half in range(2):
                        for kc in range(8):
                            S.add("pe", lambda h, kc=kc, half=half, pv=pv, hnT_=hnT_: h.matmul(
                                pv[:, half * 512:(half + 1) * 512], lhsT=hnT_[:, kc, :],
                                rhs=wA[:, kc, O_GV + half * 512:O_GV + (half + 1) * 512],
                                start=(kc == 0), stop=(kc == 7)), [hTk, wkey(O_GV)], [pvk[half]])
                    evac(gv_[:, :], pv[:, :], pvk, [gvk])

                    pr, prk = bank()
                    S.add("pe", lambda h, pr=pr: h.matmul(pr, lhsT=M2, rhs=sp_t[:, :], start=True, stop=True),
                          ["sp_t", "cst"], [prk])
                    S.add("act", lambda h, pr=pr: h.activation(out=erev[:, :], in_=pr, func=AF.Exp, scale=-1.0 / 16),
                          [prk], ["erev"])
                    pd, pdk = bank()
                    for hh in range(4):
                        S.add("pe", lambda h, pd=pd, hh=hh: h.matmul(pd[:, 2 * hh:2 * hh + 2],
                                                                      lhsT=sp_t[:, hh * 128:(hh + 1) * 128],
                                                                      rhs=ones2[:, :], start=True, stop=True),
                              ["sp_t", "ones2"], [pdk])
                    S.add("act", lambda h, pd=pd: h.activation(out=dec[:, :], in_=pd[:, 0:8], func=AF.Exp,
                                                               scale=-1.0 / 16), [pdk], ["dec"])
                    S.add("dve", lambda h, pkk=pkk: h.tensor_tensor(out=khat[:, :], in0=pkk, in1=erev[:, :], op=ALU.mult),
                          [pkkk, "erev"], ["khat"])
                    if own:
                        pc, pck = bank()
                        for hh in range(4):
                            S.add("pe", lambda h, pc=pc, hh=hh: h.matmul(pc[:, hh * 128:(hh + 1) * 128],
                                                                          lhsT=sp_t[:, hh * 128:(hh + 1) * 128],
                                                                          rhs=M1, start=True, stop=True),
                                  ["sp_t", "cst"], [pck])
                        S.add("act", lambda h, pc=pc: h.activation(out=eq[:, :], in_=pc, func=AF.Exp, scale=-1.0 / 16),
                              [pck], ["eq"])
                        S.add("act", lambda h, pc=pc: h.activation(out=ek[:, :], in_=pc, func=AF.Exp, scale=1.0 / 16),
                              [pck], ["ek"])
                        pq, pqk = bank()
                        for hh in range(4):
                            proj_feat(pq[:, hh * 128:(hh + 1) * 128], pqk, O_GQ + hh * 128, 128)
                        S.add("dve", lambda h, pq=pq: h.scalar_tensor_tensor(
                            out=qtl[:, :], in0=pq, scalar=128.0 ** -0.5, in1=eq[:, :], op0=ALU.mult, op1=ALU.mult),
                            [pqk, "eq"], ["qtl"])
                        pk2, pk2k = bank()
                        for hh in range(4):
                            proj_feat(pk2[:, hh * 128:(hh + 1) * 128], pk2k, O_GK + hh * 128, 128)
                        S.add("dve", lambda h, pk2=pk2: h.tensor_tensor(out=ktl[:, :], in0=pk2, in1=ek[:, :], op=ALU.mult),
                              [pk2k, "ek"], ["ktl"])
                        pa, pak = bank()
                        for hh in range(4):
                            S.add("pe", lambda h, pa=pa, hh=hh: h.matmul(pa[:, hh * 128:(hh + 1) * 128],
                                                                          lhsT=ktl[:, hh * 128:(hh + 1) * 128],
                                                                          rhs=qtl[:, hh * 128:(hh + 1) * 128],
                                                                          start=True, stop=True), ["ktl", "qtl"], [pak])
                        S.add("dve", lambda h, pa=pa: h.tensor_tensor(
                            out=attm[:, :], in0=pa, in1=m1rep[:, :, :].rearrange("p a b -> p (a b)"), op=ALU.mult),
                            [pak, "m1rep"], ["attm"])
                        S.add("pool", lambda h: h.tensor_copy(out=Sbf[:, :], in_=Sst[:, :]), ["Sst"], ["Sbf"])
                        po, pok = dbank()
                        for hh in range(4):
                            oc = po[:, hh * 256:(hh + 1) * 256]
                            S.add("pe", lambda h, oc=oc, hh=hh, gv_=gv_: h.matmul(
                                oc, lhsT=attm[:, hh * 128:(hh + 1) * 128], rhs=gv_[:, hh * 256:(hh + 1) * 256],
                                start=True, stop=False), ["attm", gvk], [pok[hh // 2]])
                            S.add("pe", lambda h, oc=oc, hh=hh: h.matmul(
                                oc, lhsT=qtl[:, hh * 128:(hh + 1) * 128], rhs=Sbf[:, hh * 256:(hh + 1) * 256],
                                start=False, stop=True), ["qtl", "Sbf"], [pok[hh // 2]])
                        for hh in range(4):
                            S.add("act", lambda h, hh=hh, po=po, st_=st_: h.activation(
                                out=junk[:, 0:256], in_=po[:, hh * 256:(hh + 1) * 256], func=AF.Square,
                                accum_out=st_[:, 4 + hh:5 + hh]), [pok[hh // 2]], ["junk", sk_ + "c"])
                        rstd_from_ss(st_[:, 4:8], st_[:, 8:12], 256, sk_ + "c", sk_ + "d")
                        for hh in range(4):
                            S.add("dve", lambda h, hh=hh, po=po, st_=st_, og=og: h.scalar_tensor_tensor(
                                out=og[:, hh * 256:(hh + 1) * 256], in0=po[:, hh * 256:(hh + 1) * 256],
                                scalar=st_[:, 8 + hh:9 + hh], in1=gh[:, :], op0=ALU.mult, op1=ALU.mult),
                                [pok[hh // 2], sk_ + "d", "gh", ogk], [ogk])
                        pgr, pgrk = dbank()
                        for half in range(2):
                            for kc in range(8):
                                S.add("pe", lambda h, kc=kc, half=half, pgr=pgr, hnT_=hnT_: h.matmul(
                                    pgr[:, half * 512:(half + 1) * 512], lhsT=hnT_[:, kc, :],
                                    rhs=wA[:, kc, O_GR + half * 512:O_GR + (half + 1) * 512],
                                    start=(kc == 0), stop=(kc == 7)), [hTk, wkey(O_GR)], [pgrk[half]])
                        S.add("act", lambda h, pgr=pgr: h.copy(out=grs[:, :], in_=pgr[:, :]), pgrk, ["grs"])
                        S.add("act", lambda h: h.activation(out=eg[:, :], in_=grs[:, :], func=AF.Exp, scale=-1.0),
                              ["grs"], ["eg"])
                        S.add("dve", lambda h: h.tensor_scalar(out=eg[:, :], in0=eg[:, :], scalar1=1.0, scalar2=None,
                                                               op0=ALU.add), ["eg"], ["eg"])
                        S.add("dve", lambda h: h.reciprocal(out=eg[:, :], in_=eg[:, :]), ["eg"], ["eg"])
                        S.add("dve", lambda h: h.tensor_tensor(out=eg[:, :], in0=grs[:, :], in1=eg[:, :],
                                                               op=ALU.mult), ["grs", "eg"], ["eg"])
                        S.add("dve", lambda h, og=og: h.tensor_tensor(out=og[:, :], in0=og[:, :], in1=eg[:, :], op=ALU.mult),
                              [ogk, "eg"], [ogk])
                        def og_transposes(og=og, ogk=ogk, wb_i=wb_i, w_i=w_i):
                            for half in range(2):
                                pb, pk = bank()
                                for q in range(4):
                                    kc = half * 4 + q
                                    S.add("pe", lambda h, pb=pb, q=q, kc=kc: h.transpose(
                                        out=pb[:, q * 128:(q + 1) * 128], in_=og[:, kc * 128:(kc + 1) * 128], identity=ident),
                                        [ogk, "cst"], [pk])
                                evac(ogTw[:, half * 4:half * 4 + 4, wb_i * 128:(wb_i + 1) * 128],
                                     pb.rearrange("p (a b) -> p a b", a=4), [pk], ["ogTw"])
                            if wb_i == 4:
                                S.dma(lambda h: h.dma_start(out=ogT_d[w_i], in_=ogTw[:, :, :].rearrange("p a b -> p (a b)")),
                                      reads=["ogTw"], writes=["ogT_d"])
                        deferred.append(og_transposes)
                    def state_update(gv_=gv_, gvk=gvk):
                        ps_, psk = dbank()
                        for hh in range(4):
                            S.add("pe", lambda h, hh=hh, ps_=ps_, gv_=gv_: h.matmul(
                                ps_[:, hh * 256:(hh + 1) * 256], lhsT=khat[:, hh * 128:(hh + 1) * 128],
                                rhs=gv_[:, hh * 256:(hh + 1) * 256], start=True, stop=True), ["khat", gvk], [psk[hh // 2]])
                        for hh in range(4):
                            S.add("dve", lambda h, hh=hh, ps_=ps_: h.scalar_tensor_tensor(
                                out=Sst[:, hh * 256:(hh + 1) * 256], in0=Sst[:, hh * 256:(hh + 1) * 256],
                                scalar=dec[:, 2 * hh:2 * hh + 1], in1=ps_[:, hh * 256:(hh + 1) * 256],
                                op0=ALU.mult, op1=ALU.add), ["Sst", "dec", psk[hh // 2]], ["Sst"])
                    deferred.append(state_update)
                    if own and wb_i == 4:
                        S.dma(lambda h, w_i=w_i: h.dma_start(out=hnT_d[w_i], in_=hnTw[:, :, :].rearrange("p a b -> p (a b)")),
                              reads=["hnTw"], writes=["hnT_d"])
                while deferred:
                    deferred.pop(0)()
                S.barrier()
                S.emit()

        mixer_phase(False)
        mixer_phase(True)
        abst.close()

        with contextlib.ExitStack() as st:
            NM = 512
            NHL = 2 * NW
            qposb = T(st, [128, NQ], F32)
            S.dma(lambda h: h.dma_start(out=qposb[:, :], in_=qpos_d), writes=["qposb"])
            qposh = T(st, [128, NHL], F32)
            for w_i in range(NW):
                S.add("pool", lambda h, w_i=w_i: h.tensor_copy(out=qposh[:, 2 * w_i:2 * w_i + 2],
                                                               in_=qposb[:, w_i * WR + 126:w_i * WR + 128]),
                      ["qposb"], ["qposh"])
            mk = T(st, [128, 16, NM], BF16)
            for r in range(1, 17):
                S.add("dve", lambda h, r=r: h.tensor_scalar(out=mk[:, r - 1, :], in0=qposb[:, 128:128 + NM],
                                                            scalar1=kpos[:, r:r + 1], scalar2=None, op0=ALU.is_gt),
                      ["qposb", "kpos"], ["mk"])
            kTh = [T(st, [128, NR], BF16) for _ in range(2)]
            qTh = [T(st, [128, NQ], BF16) for _ in range(2)]
            qhl = [T(st, [128, NHL], BF16) for _ in range(2)]
            vh = [T(st, [128, NBLK * 64], BF16) for _ in range(4)]
            sbo = [T(st, [64, NQ], BF16) for _ in range(4)]
            for a in range(4):
                S.add("pool", lambda h, a=a: h.memset(sbo[a][:, :], 0.0), [], [f"sbo{a}"])
            NSL = 4
            e_s = [T(st, [128, NM], F32) for _ in range(NSL)]
            sp_s = [T(st, [128, NM], BF16) for _ in range(NSL)]
            spm_s = [T(st, [128, NM], BF16) for _ in range(NSL)]
            w_s = [T(st, [128, NM], BF16) for _ in range(NSL)]
            wm_s = [T(st, [128, NM], BF16) for _ in range(NSL)]
            Rf_s = [T(st, [128, NM], F32) for _ in range(NSL)]
            Rb_s = [T(st, [128, NM], BF16) for _ in range(NSL)]

            class Job:
                def __init__(self, a, w_i, halo, ab):
                    self.a, self.w_i, self.halo, self.ab = a, w_i, halo, ab
                    self.par = ab // 2
                    if halo:
                        self.N = NHL
                        self.top = 16 * (NW - 1) + 12
                        self.mfrom = 0
                    else:
                        self.N = NM
                        self.top = 16 * w_i + 16
                        self.mfrom = 16 * w_i + 1
                    self.kb = self.top
                    self.len = self.top + 1

                def bind(self, si):
                    self.si = si
                    N = self.N
                    self.Z = PS[si // 2][:, (si % 2) * 512:(si % 2) * 512 + N]
                    self.Zk = f"ps{si}"
                    self.O = PS[2 + si // 2][0:64, (si % 2) * 512:(si % 2) * 512 + N]
                    self.Ok = f"ps{4 + si}"

                def q_ap(self):
                    if self.halo:
                        return qhl[self.par][self.a * 64:self.a * 64 + 64, 0:self.N], f"qhl{self.par}"
                    c0 = self.w_i * WR + 128
                    return qTh[self.par][self.a * 64:self.a * 64 + 64, c0:c0 + self.N], f"qTh{self.par}"

                def pos_ap(self):
                    if self.halo:
                        return qposh[:, 0:self.N], "qposh"
                    c0 = self.w_i * WR + 128
                    return qposb[:, c0:c0 + self.N], "qposb"

                def stage(self, sg):
                    a, si, kb, N = self.ab, self.si, self.kb, self.N
                    first, last, masked = kb == self.top, kb == 0, kb >= self.mfrom
                    Z, Zk, O, Ok = self.Z, self.Zk, self.O, self.Ok
                    e_, sp_, spm_, w_, wm_, Rf_, Rb_ = (t[si][:, 0:N] for t in (e_s, sp_s, spm_s, w_s, wm_s, Rf_s, Rb_s))
                    if sg == 0:
                        q, qk = self.q_ap()
                        par, hb = self.par, self.a * 64
                        S.add("pe", lambda h: h.matmul(Z, lhsT=kTh[par][hb:hb + 64, kb * 128:(kb + 1) * 128], rhs=q,
                                                       start=True, stop=True), [f"kTh{par}", qk], [Zk])
                    elif sg == 1:
                        S.add("act", lambda h: h.activation(out=e_, in_=Z, func=AF.Exp), [Zk], [f"e{si}"])
                    elif sg == 2:
                        if masked:
                            S.add("act", lambda h: h.activation(out=sp_, in_=e_, func=AF.Ln, bias=1.0), [f"e{si}"], [f"sp{si}"])
                        else:
                            S.add("act", lambda h: h.activation(out=spm_, in_=e_, func=AF.Ln, bias=1.0), [f"e{si}"], [f"spm{si}"])
                    elif sg == 3:
                        if masked and not self.halo:
                            r = kb - 16 * self.w_i
                            S.add("dve", lambda h: h.tensor_tensor(out=spm_, in0=sp_, in1=mk[:, r - 1, 0:N], op=ALU.mult),
                                  ["mk", f"sp{si}"], [f"spm{si}"])
                        elif masked:
                            p, pk_ = self.pos_ap()
                            S.add("dve", lambda h: h.scalar_tensor_tensor(out=spm_, in0=p, scalar=kpos[:, kb:kb + 1], in1=sp_,
                                                                          op0=ALU.is_gt, op1=ALU.mult),
                                  [pk_, "kpos", f"sp{si}"], [f"spm{si}"])
                    elif sg == 4:
                        S.add("pe", lambda h: h.matmul(Z, lhsT=negtri, rhs=spm_, start=False, stop=True,
                                                       skip_group_check=True), ["cstb", f"spm{si}"], [Zk])
                        if not first:
                            S.add("pe", lambda h: h.matmul(Z, lhsT=negones, rhs=Rb_, start=False, stop=True,
                                                           skip_group_check=True), ["cstb", f"Rb{si}"], [Zk])
                    elif sg == 5:
                        if masked:
                            S.add("act", lambda h: h.activation(out=w_, in_=Z, func=AF.Exp), [Zk], [f"w{si}"])
                        else:
                            S.add("act", lambda h: h.activation(out=wm_, in_=Z, func=AF.Exp), [Zk], [f"wm{si}"])
                    elif sg == 6:
                        if masked and not self.halo:
                            r = kb - 16 * self.w_i
                            S.add("dve", lambda h: h.tensor_tensor(out=wm_, in0=w_, in1=mk[:, r - 1, 0:N], op=ALU.mult),
                                  ["mk", f"w{si}"], [f"wm{si}"])
                        elif masked:
                            p, pk_ = self.pos_ap()
                            S.add("dve", lambda h: h.scalar_tensor_tensor(out=wm_, in0=p, scalar=kpos[:, kb:kb + 1], in1=w_,
                                                                          op0=ALU.is_gt, op1=ALU.mult),
                                  [pk_, "kpos", f"w{si}"], [f"wm{si}"])
                    elif sg == 7:
                        S.add("pe", lambda h: h.matmul(O, lhsT=vh[a][:, kb * 64:(kb + 1) * 64], rhs=wm_,
                                                       start=first, stop=last), [f"vh{a}", f"wm{si}"], [Ok])
                    elif sg == 8:
                        if not last:
                            if first:
                                S.add("dve", lambda h: h.tensor_copy(out=Rf_, in_=spm_), [f"spm{si}"], [f"Rf{si}"])
                            else:
                                S.add("dve", lambda h: h.tensor_tensor(out=Rf_, in0=Rf_, in1=spm_, op=ALU.add),
                                      [f"Rf{si}", f"spm{si}"], [f"Rf{si}"])
                    elif sg == 9:
                        if not last:
                            S.add("dve", lambda h: h.tensor_copy(out=Rb_, in_=Rf_), [f"Rf{si}"], [f"Rb{si}"])

                def finish(self):
                    a, O, Ok = self.ab, self.O, self.Ok
                    if self.halo:
                        for w_i in range(NW):
                            S.add("act", lambda h, w_i=w_i: h.copy(out=sbo[a][:, w_i * WR + 126:w_i * WR + 128],
                                                                   in_=O[:, 2 * w_i:2 * w_i + 2]), [Ok], [f"sbo{a}"])
                    else:
                        c0 = self.w_i * WR + 128
                        S.add("act", lambda h: h.copy(out=sbo[a][:, c0:c0 + self.N], in_=O), [Ok], [f"sbo{a}"])

            def load_hp(hp):
                par = hp % 2
                S.dma(lambda h, par=par, hp=hp: h.dma_start(out=kTh[par][:, :], in_=kT_d[hp]), reads=["kT_d"], writes=[f"kTh{par}"])
                S.dma(lambda h, par=par, hp=hp: h.dma_start(out=qTh[par][:, :], in_=qT_d[hp]), reads=["qT_d"], writes=[f"qTh{par}"])
                for w_i in range(NW):
                    S.add("pool", lambda h, par=par, w_i=w_i: h.tensor_copy(
                        out=qhl[par][:, 2 * w_i:2 * w_i + 2], in_=qTh[par][:, w_i * WR + 126:w_i * WR + 128]),
                        [f"qTh{par}"], [f"qhl{par}"])
                for a in range(2):
                    hd = 2 * hp + a
                    ab = 2 * par + a
                    S.dma(lambda h, ab=ab, hd=hd: h.dma_start(out=vh[ab][:, :], in_=v_d[hd]), reads=["v_d"], writes=[f"vh{ab}"])

            load_hp(0)
            load_hp(1)
            jobs = []
            left = {}
            for hp in range(4):
                jl = ([Job(a, w_i, False, 2 * (hp % 2) + a) for w_i in range(NW) for a in range(2)]
                      + [Job(a, 0, True, 2 * (hp % 2) + a) for a in range(2)])
                jl.sort(key=lambda j_: -j_.len)
                for j_ in jl:
                    j_.hp = hp
                jobs += jl
                left[hp] = len(jl)
            slots = [None] * NSL
            sptr = [0] * NSL
            offs = [0, 2, 5, 7]
            tick = 0
            while jobs or any(sl is not None for sl in slots):
                for si in range(NSL):
                    if slots[si] is None and jobs and tick >= offs[si]:
                        slots[si] = jobs.pop(0)
                        slots[si].bind(si)
                        sptr[si] = 0
                    sl = slots[si]
                    if sl is None:
                        continue
                    sl.stage(sptr[si])
                    sptr[si] += 1
                    if sptr[si] == 10:
                        sptr[si] = 0
                        if sl.kb == 0:
                            sl.finish()
                            slots[si] = None
                            hp = sl.hp
                            left[hp] -= 1
                            if left[hp] == 0:
                                for a in range(2):
                                    hd = 2 * hp + a
                                    ab = 2 * (hp % 2) + a
                                    S.dma(lambda h, ab=ab, hd=hd: h.dma_start(out=sbT_d[hd], in_=sbo[ab][:, :]),
                                          reads=[f"sbo{ab}"], writes=["sbT_d"])
                                if hp + 2 < 4:
                                    load_hp(hp + 2)
                        else:
                            sl.kb -= 1
                tick += 1
            S.barrier()
            S.emit()
        bk[0] = 0

        with contextlib.ExitStack() as st:
            NH = 257
            H0 = 126
            wm = T(st, [128, 8, 2048], BF16)
            load_w(wm, "wm", 0, w_in, 0, 8, C_MG, 2048)
            wsb = T(st, [64, 8, D], BF16)
            load_w(wsb, "wsb", 0, wsb_d, 0, 8, 0, D, kdim=64)
            wgl = T(st, [128, 8, D], BF16)
            load_w(wgl, "wgl", 0, wgla_d, 0, 8, 0, D)
            wo = T(st, [128, 8, D], BF16)
            load_w(wo, "wo", 0, wo_d, 0, 8, 0, D)
            hnTw = T(st, [128, 8, WR], BF16)
            ogTw = T(st, [128, 8, WR], BF16)
            sbTw = T(st, [64, 8, WR], BF16)
            gT = T(st, [128, 16, WR], BF16)
            mpT = T(st, [128, 8, WR], BF16)
            S.add("pool", lambda h: h.memset(mpT[:, :, :], 0.0), [], ["mpT"])
            hn2Tw = hnTw
            g12 = T(st, [128, 2, D], F32)
            S.dma(lambda h: h.dma_start(out=g12[:, :, :], in_=gb_d[:, 1:3, :]), writes=["gb"])
            t1 = [T(st, [128, NH], F32) for _ in range(2)]
            t2 = [T(st, [128, NH], F32) for _ in range(2)]
            xw = [T(st, [128, D], F32)] * 2
            tt = T(st, [128, D], F32)
            h1 = [T(st, [128, D], F32)] * 2
            hn2 = T(st, [128, D], F32)
            stat = [T(st, [128, 8], F32) for _ in range(2)]
            for w_i in range(NW):
                S.dma(lambda h, w_i=w_i: h.dma_start(out=hnTw[:, :, :].rearrange("p a b -> p (a b)"), in_=hnT_d[w_i]),
                      reads=["hnT_d"], writes=["hnTw"])
                S.dma(lambda h, w_i=w_i: h.dma_start(out=ogTw[:, :, :].rearrange("p a b -> p (a b)"), in_=ogT_d[w_i]),
                      reads=["ogT_d"], writes=["ogTw"])
                S.dma(lambda h, w_i=w_i: h.dma_start(
                    out=sbTw[:, :, :], in_=sbT_d[:, :, w_i * WR:(w_i + 1) * WR].rearrange("h d n -> d h n")),
                    reads=["sbT_d"], writes=["sbTw"])
                for i in range(16):
                    for hf in range(2):
                        pb, pk = bank()
                        for kc in range(8):
                            S.add("pe", lambda h, pb=pb, kc=kc, i=i, hf=hf: h.matmul(
                                pb[:, 0:NH], lhsT=wm[:, kc, i * 128:(i + 1) * 128], rhs=hnTw[:, kc, H0 + hf * NH:H0 + (hf + 1) * NH],
                                start=(kc == 0), stop=(kc == 7)), ["wm", "hnTw"], [pk])
                        S.add("act", lambda h, pb=pb, i=i, hf=hf: h.activation(
                            out=gT[:, i, H0 + hf * NH:H0 + (hf + 1) * NH], in_=pb[:, 0:NH], func=AF.Sigmoid), [pk], ["gT"])
                for i in range(8):
                    for hf in range(2):
                        pb, pk = bank()
                        for hd in range(8):
                            S.add("pe", lambda h, pb=pb, hd=hd, i=i, hf=hf: h.matmul(
                                pb[:, 0:NH], lhsT=wsb[:, hd, i * 128:(i + 1) * 128], rhs=sbTw[:, hd, H0 + hf * NH:H0 + (hf + 1) * NH],
                                start=(hd == 0), stop=(hd == 7)), ["wsb", "sbTw"], [pk])
                        S.add("dve", lambda h, pb=pb, i=i, hf=hf: h.tensor_tensor(
                            out=t1[hf][:, :], in0=pb[:, 0:NH], in1=gT[:, i, H0 + hf * NH:H0 + (hf + 1) * NH], op=ALU.mult),
                            [pk, "gT"], [f"t1{hf}"])
                        pb2, pk2 = bank()
                        for kc in range(8):
                            S.add("pe", lambda h, pb2=pb2, kc=kc, i=i, hf=hf: h.matmul(
                                pb2[:, 0:NH], lhsT=wgl[:, kc, i * 128:(i + 1) * 128], rhs=ogTw[:, kc, H0 + hf * NH:H0 + (hf + 1) * NH],
                                start=(kc == 0), stop=(kc == 7)), ["wgl", "ogTw"], [pk2])
                        S.add("dve", lambda h, pb2=pb2, i=i, hf=hf: h.tensor_tensor(
                            out=t2[hf][:, :], in0=pb2[:, 0:NH], in1=gT[:, 8 + i, H0 + hf * NH:H0 + (hf + 1) * NH], op=ALU.mult),
                            [pk2, "gT"], [f"t2{hf}"])
                        S.add("pool", lambda h, i=i, hf=hf: h.tensor_tensor(
                            out=mpT[:, i, H0 + hf * NH:H0 + (hf + 1) * NH], in0=t1[hf][:, :], in1=t2[hf][:, :], op=ALU.add),
                            [f"t1{hf}", f"t2{hf}"], ["mpT"])
                d1def = []
                for b in range(5):
                    r0 = w_i * WR + b * 128
                    xi = 0
                    S.dma(lambda h, xi=xi, r0=r0: h.dma_start(out=xw[xi][:, :], in_=xo[r0:r0 + 128, :]), writes=[f"xw{xi}"])
                    pm, pmk = dbank()
                    for half in range(2):
                        for kc in range(8):
                            S.add("pe", lambda h, pm=pm, kc=kc, half=half, b=b: h.matmul(
                                pm[:, half * 512:(half + 1) * 512], lhsT=mpT[:, kc, b * 128:(b + 1) * 128],
                                rhs=wo[:, kc, half * 512:(half + 1) * 512], start=(kc == 0), stop=(kc == 7)),
                                ["mpT", "wo"], [pmk[half]])
                    while d1def:
                        d1def.pop(0)()
                    sk_ = f"dstat{xi}"
                    st_ = stat[xi]
                    S.add("act", lambda h, pm=pm, st_=st_: h.activation(out=tt[:, :], in_=pm[:, :], func=AF.Square,
                                                                        accum_out=st_[:, 0:1]), pmk, ["tt", sk_ + "a"])
                    rstd_from_ss(st_[:, 0:1], st_[:, 1:2], D, sk_ + "a", sk_ + "b")
                    S.add("dve", lambda h, pm=pm, st_=st_: h.scalar_tensor_tensor(
                        out=tt[:, :], in0=pm[:, :], scalar=st_[:, 1:2], in1=g12[:, 0, :], op0=ALU.mult, op1=ALU.mult),
                        pmk + [sk_ + "b", "gb"], ["tt"])
                    S.add("dve", lambda h, xi=xi: h.tensor_tensor(out=h1[xi][:, :], in0=tt[:, :], in1=xw[xi][:, :], op=ALU.add),
                          ["tt", f"xw{xi}"], [f"h1{xi}"])
                    S.dma(lambda h, xi=xi, r0=r0: h.dma_start(out=h1_d[r0:r0 + 128, :], in_=h1[xi][:, :]),
                          reads=[f"h1{xi}"], writes=["h1_d"])
                    S.add("act", lambda h, xi=xi, st_=st_: h.activation(out=tt[:, :], in_=h1[xi][:, :], func=AF.Square,
                                                                        accum_out=st_[:, 2:3]), [f"h1{xi}"], ["tt", sk_ + "c"])
                    rstd_from_ss(st_[:, 2:3], st_[:, 3:4], D, sk_ + "c", sk_ + "d")
                    S.add("dve", lambda h, xi=xi, st_=st_: h.scalar_tensor_tensor(
                        out=hn2[:, :], in0=h1[xi][:, :], scalar=st_[:, 3:4], in1=g12[:, 1, :], op0=ALU.mult, op1=ALU.mult),
                        [f"h1{xi}", sk_ + "d", "gb"], ["hn2"])
                    def hn2_transposes(b=b):
                        for half in range(2):
                            pb, pk = bank()
                            for q in range(4):
                                kc = half * 4 + q
                                S.add("pe", lambda h, pb=pb, q=q, kc=kc: h.transpose(
                                    out=pb[:, q * 128:(q + 1) * 128], in_=hn2[:, kc * 128:(kc + 1) * 128], identity=ident),
                                    ["hn2", "cst"], [pk])
                            evac(hn2Tw[:, half * 4:half * 4 + 4, b * 128:(b + 1) * 128],
                                 pb.rearrange("p (a b) -> p a b", a=4), [pk], ["hnTw"])
                    d1def.append(hn2_transposes)
                while d1def:
                    d1def.pop(0)()
                S.dma(lambda h, w_i=w_i: h.dma_start(out=hn2T_d[w_i], in_=hn2Tw[:, :, :].rearrange("p a b -> p (a b)")),
                      reads=["hnTw"], writes=["hn2T_d"])
            S.barrier()
            S.emit()

        with contextlib.ExitStack() as st:
            NH = 257
            H0 = 126
            wup = T(st, [128, 8, DFF], BF16)
            wgt = T(st, [128, 8, DFF], BF16)
            for g in range(0, DFF, 512):
                n_ = min(512, DFF - g)
                load_w(wup, f"wup{g // 512}", g, wup_d, 0, 8, g, n_)
                load_w(wgt, f"wgt{g // 512}", g, wgt_d, 0, 8, g, n_)
            cw = T(st, [128, NCF, 4], F32)
            S.dma(lambda h: h.dma_start(out=cw[:, :, :], in_=cw_d), writes=["cw"])
            hn2Tw = [T(st, [128, 8, WR], BF16)] * 2
            prodT = [T(st, [128, NCF, WR], BF16)] * 2
            S.add("pool", lambda h: h.memset(prodT[0][:, :, :], 0.0), [], ["prodT0"])
            upS = [T(st, [128, WR + 2], F32) for _ in range(2)]
            acc = [T(st, [128, WR], F32) for _ in range(2)]
            a2 = [T(st, [128, WR], F32) for _ in range(2)]
            sg = [T(st, [128, WR], F32) for _ in range(2)]
            gtS = [T(st, [128, 512], BF16) for _ in range(2)]
            for i in range(2):
                S.add("pool", lambda h, i=i: h.memset(upS[i][:, :], 0.0), [], [f"upS{i}"])
            for w_i in range(NW):
                wi = 0
                S.dma(lambda h, w_i=w_i, wi=wi: h.dma_start(out=hn2Tw[wi][:, :, :].rearrange("p a b -> p (a b)"),
                                                           in_=hn2T_d[w_i]), reads=["hn2T_d"], writes=[f"hn2Tw{wi}"])
                for c in range(NCF):
                    ci = c % 2
                    pgs = []
                    for hf in range(2):
                        pu, puk = bank()
                        for kc in range(8):
                            S.add("pe", lambda h, pu=pu, kc=kc, c=c, hf=hf, wi=wi: h.matmul(
                                pu[:, 0:NH], lhsT=wup[:, kc, c * 128:(c + 1) * 128], rhs=hn2Tw[wi][:, kc, H0 + hf * NH:H0 + (hf + 1) * NH],
                                start=(kc == 0), stop=(kc == 7)), [f"wup{c // 4}", f"hn2Tw{wi}"], [puk])
                        S.add("act", lambda h, pu=pu, hf=hf, ci=ci: h.copy(out=upS[ci][:, 2 + H0 + hf * NH:2 + H0 + (hf + 1) * NH],
                                                                           in_=pu[:, 0:NH]), [puk], [f"upS{ci}"])
                        pg, pgk = bank()
                        for kc in range(8):
                            S.add("pe", lambda h, pg=pg, kc=kc, c=c, hf=hf, wi=wi: h.matmul(
                                pg[:, 0:256], lhsT=wgt[:, kc, c * 128:(c + 1) * 128], rhs=hn2Tw[wi][:, kc, 128 + hf * 256:128 + (hf + 1) * 256],
                                start=(kc == 0), stop=(kc == 7)), [f"wgt{c // 4}", f"hn2Tw{wi}"], [pgk])
                        evac(gtS[ci][:, hf * 256:(hf + 1) * 256], pg[:, 0:256], [pgk], [f"gtS{ci}"])
                    ak, a2k, sgk = f"acc{ci}", f"a2{ci}", f"sg{ci}"
                    S.add("dve", lambda h, c=c, ci=ci: h.tensor_scalar(
                        out=acc[ci][:, 128:WR], in0=upS[ci][:, 130:WR + 2], scalar1=cw[:, c, 2:3], scalar2=cw[:, c, 3:4],
                        op0=ALU.mult, op1=ALU.add), [f"upS{ci}", "cw"], [ak])
                    S.add("dve", lambda h, c=c, ci=ci: h.scalar_tensor_tensor(
                        out=acc[ci][:, 128:WR], in0=upS[ci][:, 129:WR + 1], scalar=cw[:, c, 1:2], in1=acc[ci][:, 128:WR],
                        op0=ALU.mult, op1=ALU.add), [f"upS{ci}", "cw", ak], [ak])
                    S.add("dve", lambda h, c=c, ci=ci: h.scalar_tensor_tensor(
                        out=acc[ci][:, 128:WR], in0=upS[ci][:, 128:WR], scalar=cw[:, c, 0:1], in1=acc[ci][:, 128:WR],
                        op0=ALU.mult, op1=ALU.add), [f"upS{ci}", "cw", ak], [ak])
                    S.add("act", lambda h, ci=ci: h.activation(out=a2[ci][:, 128:WR], in_=acc[ci][:, 128:WR], func=AF.Square,
                                                               scale=0.044715 ** 0.5), [ak], [a2k])
                    S.add("dve", lambda h, ci=ci: h.scalar_tensor_tensor(
                        out=a2[ci][:, 128:WR], in0=a2[ci][:, 128:WR], scalar=1.0, in1=acc[ci][:, 128:WR],
                        op0=ALU.add, op1=ALU.mult), [a2k, ak], [a2k])
                    S.add("act", lambda h, ci=ci: h.activation(out=sg[ci][:, 128:WR], in_=a2[ci][:, 128:WR], func=AF.Sigmoid,
                                                               scale=GK_), [a2k], [sgk])
                    S.add("dve", lambda h, ci=ci: h.tensor_tensor(out=sg[ci][:, 128:WR], in0=sg[ci][:, 128:WR], in1=acc[ci][:, 128:WR],
                                                                  op=ALU.mult), [sgk, ak], [sgk])
                    S.add("dve", lambda h, c=c, ci=ci, wi=wi: h.tensor_tensor(
                        out=prodT[wi][:, c, 128:WR], in0=gtS[ci][:, :], in1=sg[ci][:, 128:WR],
                        op=ALU.mult), [f"gtS{ci}", sgk], [f"prodT{wi}"])
                S.dma(lambda h, w_i=w_i, wi=wi: h.dma_start(out=prodT_d[w_i], in_=prodT[wi][:, :, :].rearrange("p a b -> p (a b)")),
                      reads=[f"prodT{wi}"], writes=["prodT_d"])
            S.barrier()
            S.emit()

        with contextlib.ExitStack() as st:
            wdn = T(st, [128, NCF, D], BF16)
            load_w(wdn, "wdn0", 0, wdn_d, 0, NCF, 0, 512)
            load_w(wdn, "wdn1", 512, wdn_d, 0, NCF, 512, 512)
            prodT = [T(st, [128, NCF, WR], BF16) for _ in range(2)]
            h1 = [T(st, [128, D], F32) for _ in range(2)]
            ot = [T(st, [128, D], F32) for _ in range(2)]
            tt = T(st, [128, D], F32)
            junk = T(st, [128, D], F32)
            g3 = T(st, [128, D], F32)
            S.dma(lambda h: h.dma_start(out=g3[:, :], in_=gb_d[:, 3, :]), writes=["gb"])
            stat = [T(st, [128, 4], F32) for _ in range(2)]
            for w_i in range(NW):
                wi = w_i % 2
                S.dma(lambda h, w_i=w_i, wi=wi: h.dma_start(out=prodT[wi][:, :, :].rearrange("p a b -> p (a b)"),
                                                           in_=prodT_d[w_i]), reads=["prodT_d"], writes=[f"prodT{wi}"])
                for b in range(1, 5):
                    r0 = w_i * WR + b * 128
                    xi = b % 2
                    S.dma(lambda h, xi=xi, r0=r0: h.dma_start(out=h1[xi][:, :], in_=h1_d[r0:r0 + 128, :]),
                          reads=["h1_d"], writes=[f"h1{xi}"])
                    pm, pmk = dbank()
                    for half in range(2):
                        for c in range(NCF):
                            S.add("pe", lambda h, pm=pm, c=c, half=half, b=b, wi=wi: h.matmul(
                                pm[:, half * 512:(half + 1) * 512], lhsT=prodT[wi][:, c, b * 128:(b + 1) * 128],
                                rhs=wdn[:, c, half * 512:(half + 1) * 512], start=(c == 0), stop=(c == NCF - 1)),
                                [f"prodT{wi}", f"wdn{half}"], [pmk[half]])
                    sk_ = f"estat{xi}"
                    st_ = stat[xi]
                    S.add("act", lambda h, pm=pm, st_=st_: h.activation(out=junk[:, :], in_=pm[:, :], func=AF.Square,
                                                                        accum_out=st_[:, 0:1]), pmk, ["junk", sk_ + "a"])
                    rstd_from_ss(st_[:, 0:1], st_[:, 1:2], D, sk_ + "a", sk_ + "b")
                    S.add("dve", lambda h, pm=pm, st_=st_: h.scalar_tensor_tensor(
                        out=tt[:, :], in0=pm[:, :], scalar=st_[:, 1:2], in1=g3[:, :], op0=ALU.mult, op1=ALU.mult),
                        pmk + [sk_ + "b", "gb"], ["tt"])
                    S.add("dve", lambda h, xi=xi: h.tensor_tensor(out=ot[xi][:, :], in0=tt[:, :], in1=h1[xi][:, :], op=ALU.add),
                          ["tt", f"h1{xi}"], [f"ot{xi}"])
                    o0 = w_i * 512 + (b - 1) * 128
                    S.dma(lambda h, xi=xi, o0=o0: h.dma_start(out=out_d[o0:o0 + 128, :], in_=ot[xi][:, :]),
                          reads=[f"ot{xi}"], writes=[f"out{o0}"])
            S.barrier()
            S.emit()
    return nc


_CACHE = {}


def _consts():
    j = np.arange(128)
    c = np.zeros((128, 5, 128), np.float32)
    c[:, 0, :] = np.eye(128, dtype=np.float32)
    c[:, 1, :] = -(j[:, None] >= j[None, :]).astype(np.float32)
    c[:, 2, :] = -1.0
    c[:, 3, :] = (j[:, None] <= j[None, :]).astype(np.float32)
    c[:, 4, :] = (j[:, None] > j[None, :]).astype(np.float32)
    return c


def kernel(x, meta_tokens, norm_mix_pre, w_in, w_gk_up, b_gk, gla_head_norm, w_sb_out, w_gla_out, w_o,
           norm_mix_post, norm_ffn_pre, w_ffn_up, w_ffn_gate, conv_w, conv_b, w_ffn_down, norm_ffn_post):
    f = np.float32
    x = np.asarray(x, f)
    B, Sq, _ = x.shape
    nslot = Sq // 2048
    NBLK = 16 * nslot + 1
    if nslot not in _CACHE:
        _CACHE[nslot] = build(nslot)
    nc = _CACHE[nslot]
    w_in0 = np.ascontiguousarray(np.asarray(w_in, f)[0])
    w_lr = np.zeros((D, 32), f)
    w_lr[:, :16] = w_in0[:, C_LR:C_LR + 16]
    wgk = np.zeros((32, 512), f)
    wgk[:16] = np.asarray(w_gk_up, f)[0]
    wgk[16] = np.asarray(b_gk, f)[0]
    gains = np.stack([np.asarray(g, f)[0] for g in (norm_mix_pre, norm_mix_post, norm_ffn_pre, norm_ffn_post)], 0)
    gains = np.ascontiguousarray(np.broadcast_to(gains[None], (128, 4, D)))
    ghead = np.ascontiguousarray(np.broadcast_to(np.asarray(gla_head_norm, f)[0][None], (128, 256)))
    cwb = np.concatenate([np.asarray(conv_w, f)[0], np.asarray(conv_b, f)], 0)
    cw = np.ascontiguousarray(cwb.reshape(4, NCF, 128).transpose(2, 1, 0))
    kpos = (np.arange(NBLK)[None, :] * 128 + np.arange(128)[:, None]).astype(f)
    shared = dict(cst=_consts(), w_in=w_in0, w_lr=w_lr, wgk=wgk,
                  w_sb_out=np.ascontiguousarray(np.asarray(w_sb_out, f)[0]),
                  w_gla_out=np.ascontiguousarray(np.asarray(w_gla_out, f)[0]),
                  w_o=np.ascontiguousarray(np.asarray(w_o, f)[0]),
                  w_up=np.ascontiguousarray(np.asarray(w_ffn_up, f)[0]),
                  w_gate=np.ascontiguousarray(np.asarray(w_ffn_gate, f)[0]),
                  w_down=np.ascontiguousarray(np.asarray(w_ffn_down, f)[0]),
                  gains=gains, ghead=ghead, cw=cw, kpos=kpos)
    in_maps = []
    for c in range(8):
        b, j = divmod(c, 4)
        xpad = np.zeros((NBLK * 128, D), f)
        xpad[112:128] = np.asarray(meta_tokens, f)
        xpad[128:] = x[b]
        rows = np.concatenate([np.arange(512 * (4 * m + j), 512 * (4 * m + j) + WR) for m in range(nslot)])
        xo = np.ascontiguousarray(xpad[rows])
        qpos = np.ascontiguousarray(np.broadcast_to(rows.astype(f)[None], (128, rows.size)))
        oh = np.zeros((128, 4), f)
        oh[:, j] = 1.0
        in_maps.append(dict(shared, xp=xpad, xo=xo, qpos=qpos, oh=oh))
    res = run_bass_kernel_spmd(nc, in_maps, core_ids=list(range(8)))
    out = np.zeros((B, Sq, D), f)
    for c in range(8):
        b, j = divmod(c, 4)
        o = res.results[c]["out"]
        for m in range(nslot):
            t = 4 * m + j
            out[b, 512 * t:512 * t + 512] = o[m * 512:(m + 1) * 512]
    return out
```
